# Optimizing a Trainium2 kernel written in Bass

```python
import math
import jax
import jax.numpy as jnp
from jax import lax
import numpy as np

D_MODEL = 1024
BATCH = 8
SEQ = 4096
DEPTH = 4

NORM_EPS = 1e-6
D_FF = 2816
FFN_RES_SCALE = 0.5

A_HEADS = 8
A_HEAD_DIM = 64
A_PATTERNS = ((128, 1), (512, 4), (2048, 16))
A_BLOCK = 128

B_HEADS = 4
B_HEAD_DIM = 128
B_CONV = 4
B_CHUNK = 64

C_D_INNER = 2 * D_MODEL
C_HEAD_DIM = 64
C_HEADS = C_D_INNER // C_HEAD_DIM
C_GROUPS = 4
C_STATE = 128
C_CONV = 4
C_CHUNK = 64

A_WIDTH = A_HEADS * A_HEAD_DIM
B_WIDTH = B_HEADS * B_HEAD_DIM
EVEN_IN = 3 * A_WIDTH + 4 * B_WIDTH + 2 * B_HEADS
C_XBC = C_D_INNER + 2 * C_GROUPS * C_STATE
ODD_IN = C_D_INNER + C_XBC + C_HEADS
N_EVEN = (DEPTH + 1) // 2
N_ODD = DEPTH // 2

kernel_name = 'hybrid_dilated_attn_gdn_mamba2_trunk'


def rmsnorm(x, w):
    xf = x.astype(jnp.float32)
    y = xf * lax.rsqrt(jnp.mean(xf * xf, axis=-1, keepdims=True) + NORM_EPS)
    return (y * w.astype(jnp.float32)).astype(x.dtype)


def l2norm(x):
    return x * lax.rsqrt(jnp.sum(x * x, axis=-1, keepdims=True) + NORM_EPS)


def swiglu(x, w_gate, w_up, w_down):
    return (jax.nn.silu(x @ w_gate) * (x @ w_up)) @ w_down


def causal_dwconv(x, w):
    k_len, ch = w.shape
    return lax.conv_general_dilated(x, w[:, None, :].astype(x.dtype), window_strides=(1,),
                                    padding=((k_len - 1, 0),), dimension_numbers=('NWC', 'WIO', 'NWC'),
                                    feature_group_count=ch)


def dilated_window_attention(q, k, v):
    bsz, seq, heads, dh = q.shape
    scale = dh ** -0.5
    nums, maxes, dens = [], [], []
    for window, dil in A_PATTERNS:
        span = window // dil
        length = seq // dil
        blk = math.gcd(length, A_BLOCK)
        nb = length // blk

        def by_residue(t):
            return t.reshape(bsz, length, dil, heads, dh).transpose(0, 2, 1, 3, 4)

        pad = ((0, 0), (0, 0), (span, 0), (0, 0), (0, 0))
        kp = jnp.pad(by_residue(k), pad)
        vp = jnp.pad(by_residue(v), pad)
        idx = jnp.arange(nb)[:, None] * blk + jnp.arange(blk + span)[None, :]
        kb = kp[:, :, idx]
        vb = vp[:, :, idx]
        qb = by_residue(q).reshape(bsz, dil, nb, blk, heads, dh)
        s = jnp.einsum('brnqhd,brnkhd->brnhqk', qb, kb) * scale
        dist = jnp.arange(blk)[:, None] + span - jnp.arange(blk + span)[None, :]
        valid = ((dist >= 0) & (dist <= span))[None] & (idx >= span)[:, None, :]
        s = jnp.where(valid[:, None], s, -jnp.inf)
        m = jnp.max(s, axis=-1, keepdims=True)
        p = jnp.exp(s - m)
        den = jnp.sum(p, axis=-1)
        num = jnp.einsum('brnhqk,brnkhd->brnqhd', p, vb)

        def back(t):
            t = t.reshape(bsz, dil, length, *t.shape[4:])
            return jnp.moveaxis(t, 1, 2).reshape(bsz, seq, *t.shape[3:])

        nums.append(back(num))
        maxes.append(back(jnp.swapaxes(m[..., 0], -1, -2)))
        dens.append(back(jnp.swapaxes(den, -1, -2)))
    mx = jnp.max(jnp.stack(maxes), axis=0)
    wts = [jnp.exp(mi - mx) for mi in maxes]
    num = sum(n * w[..., None] for n, w in zip(nums, wts))
    den = sum(d * w for d, w in zip(dens, wts))
    return num / den[..., None]


def gated_delta_rule(q, k, v, g, beta):
    bsz, seq, heads, dk = q.shape
    dv = v.shape[-1]
    nc = seq // B_CHUNK

    def chunks(t):
        t = t.reshape(bsz, nc, B_CHUNK, heads, *t.shape[3:])
        return jnp.swapaxes(t, 2, 3)

    qc, kc, vc = chunks(q * dk ** -0.5), chunks(k), chunks(v)
    gc = jnp.cumsum(chunks(g), axis=-1)
    bc = chunks(beta)[..., None]
    tri = jnp.tril(jnp.ones((B_CHUNK, B_CHUNK), bool))
    strict = jnp.tril(jnp.ones((B_CHUNK, B_CHUNK), bool), -1)
    decay = jnp.exp(jnp.where(tri, gc[..., :, None] - gc[..., None, :], -jnp.inf))
    k_beta = kc * bc
    a_mat = jnp.where(strict, jnp.einsum('bnhid,bnhjd->bnhij', k_beta, kc) * decay, 0.0)
    t_mat = a_mat + jnp.eye(B_CHUNK, dtype=a_mat.dtype)
    u = lax.linalg.triangular_solve(t_mat, vc * bc, left_side=True, lower=True, unit_diagonal=True)
    w = lax.linalg.triangular_solve(t_mat, k_beta * jnp.exp(gc)[..., None], left_side=True, lower=True,
                                    unit_diagonal=True)
    qk = jnp.where(tri, jnp.einsum('bnhid,bnhjd->bnhij', qc, kc) * decay, 0.0)
    q_dec = qc * jnp.exp(gc)[..., None]
    g_last = gc[..., -1]
    k_dec = kc * jnp.exp(g_last[..., None] - gc)[..., None]

    def step(state, inp):
        u_i, w_i, q_i, qk_i, k_i, gl_i = inp
        v_new = u_i - jnp.einsum('bhck,bhkv->bhcv', w_i, state)
        o_i = jnp.einsum('bhck,bhkv->bhcv', q_i, state) + jnp.einsum('bhij,bhjv->bhiv', qk_i, v_new)
        state = state * jnp.exp(gl_i)[..., None, None] + jnp.einsum('bhck,bhcv->bhkv', k_i, v_new)
        return state, o_i

    xs = tuple(jnp.moveaxis(t, 1, 0) for t in (u, w, q_dec, qk, k_dec, g_last))
    state0 = jnp.zeros((bsz, heads, dk, dv), q.dtype)
    _, o = lax.scan(step, state0, xs)
    return jnp.transpose(o, (1, 0, 3, 2, 4)).reshape(bsz, seq, heads, dv)


def ssd_chunked_scan(x, a, b_in, c_in):
    bsz, seq, heads, p = x.shape
    groups, n = b_in.shape[2:]
    rep = heads // groups
    nc = seq // C_CHUNK
    xc = x.reshape(bsz, nc, C_CHUNK, groups, rep, p)
    acum = jnp.cumsum(a.reshape(bsz, nc, C_CHUNK, groups, rep), axis=2)
    bc = b_in.reshape(bsz, nc, C_CHUNK, groups, n)
    cc = c_in.reshape(bsz, nc, C_CHUNK, groups, n)
    tri = jnp.tril(jnp.ones((C_CHUNK, C_CHUNK), bool))[:, :, None, None]
    seg = jnp.exp(jnp.where(tri, acum[:, :, :, None] - acum[:, :, None, :], -jnp.inf))
    cb = jnp.einsum('bnlgd,bnsgd->bnlsg', cc, bc)
    y_diag = jnp.einsum('bnlsgr,bnsgrp->bnlgrp', cb[..., None] * seg, xc)

    def step(state, inp):
        x_i, a_i, b_i, c_i = inp
        y_off = jnp.einsum('blgd,bgrpd,blgr->blgrp', c_i, state, jnp.exp(a_i))
        a_last = a_i[:, -1]
        state = state * jnp.exp(a_last)[..., None, None] + jnp.einsum(
            'bsgd,bsgr,bsgrp->bgrpd', b_i, jnp.exp(a_last[:, None] - a_i), x_i)
        return state, y_off

    xs = tuple(jnp.moveaxis(t, 1, 0) for t in (xc, acum, bc, cc))
    state0 = jnp.zeros((bsz, groups, rep, p, n), x.dtype)
    _, y_off = lax.scan(step, state0, xs)
    return (y_diag + jnp.moveaxis(y_off, 0, 1)).reshape(bsz, seq, heads, p)


def even_mixer(u, w_in, conv_w, a_log, dt_bias, head_norm_w, w_out):
    bsz, seq, _ = u.shape
    proj = (u @ w_in).astype(jnp.float32)
    cuts = [int(c) for c in np.cumsum([A_WIDTH, A_WIDTH, A_WIDTH, 3 * B_WIDTH, B_WIDTH, B_HEADS])]
    qa, ka, va, qkv_b, z, beta_raw, a_raw = jnp.split(proj, cuts, axis=-1)
    qa, ka, va = (t.reshape(bsz, seq, A_HEADS, A_HEAD_DIM) for t in (qa, ka, va))
    o_a = dilated_window_attention(qa, ka, va)
    qkv_b = jax.nn.silu(causal_dwconv(qkv_b, conv_w))
    qb, kb, vb = (t.reshape(bsz, seq, B_HEADS, B_HEAD_DIM) for t in jnp.split(qkv_b, 3, axis=-1))
    beta = jax.nn.sigmoid(beta_raw)
    g = -jnp.exp(a_log) * jax.nn.softplus(a_raw + dt_bias)
    o_b = gated_delta_rule(l2norm(qb), l2norm(kb), vb, g, beta)
    o_b = rmsnorm(o_b, head_norm_w) * jax.nn.silu(z.reshape(bsz, seq, B_HEADS, B_HEAD_DIM))
    o = jnp.concatenate([o_a.reshape(bsz, seq, A_WIDTH), o_b.reshape(bsz, seq, B_WIDTH)], axis=-1)
    return (o @ w_out).astype(u.dtype)


def odd_mixer(u, w_in, conv_w, conv_b, dt_bias, a_log, d_skip, out_norm_w, w_out):
    bsz, seq, _ = u.shape
    proj = (u @ w_in).astype(jnp.float32)
    z, xbc, dt = jnp.split(proj, [C_D_INNER, C_D_INNER + C_XBC], axis=-1)
    xbc = jax.nn.silu(causal_dwconv(xbc, conv_w) + conv_b)
    xs, b_in, c_in = jnp.split(xbc, [C_D_INNER, C_D_INNER + C_GROUPS * C_STATE], axis=-1)
    xs = xs.reshape(bsz, seq, C_HEADS, C_HEAD_DIM)
    b_in = b_in.reshape(bsz, seq, C_GROUPS, C_STATE)
    c_in = c_in.reshape(bsz, seq, C_GROUPS, C_STATE)
    dt = jax.nn.softplus(dt + dt_bias)
    a = -jnp.exp(a_log) * dt
    y = ssd_chunked_scan(xs * dt[..., None], a, b_in, c_in) + d_skip[:, None] * xs
    y = y.reshape(bsz, seq, C_D_INNER) * jax.nn.silu(z)
    y = rmsnorm(y.reshape(bsz, seq, C_GROUPS, C_D_INNER // C_GROUPS), out_norm_w.reshape(C_GROUPS, -1))
    return (y.reshape(bsz, seq, C_D_INNER) @ w_out).astype(u.dtype)


def _dt_bias(key, shape):
    dt = jnp.exp(jax.random.uniform(key, shape, jnp.float32, math.log(1e-3), math.log(1e-1)))
    return dt + jnp.log(-jnp.expm1(-dt))


def setup_inputs(seed: int = 0) -> dict:
    key = jax.random.key(seed)
    ks = iter(jax.random.split(key, 24))

    def nrm(shape, scale):
        return scale * jax.random.normal(next(ks), shape, jnp.float32)

    x = nrm((BATCH, SEQ, D_MODEL), 1.0)
    norm_w = 1.0 + nrm((DEPTH, 6, D_MODEL), 0.05)
    ffn_w_gate = nrm((DEPTH, 2, D_MODEL, D_FF), D_MODEL ** -0.5)
    ffn_w_up = nrm((DEPTH, 2, D_MODEL, D_FF), D_MODEL ** -0.5)
    ffn_w_down = nrm((DEPTH, 2, D_FF, D_MODEL), D_FF ** -0.5)
    even_w_in = nrm((N_EVEN, D_MODEL, EVEN_IN), D_MODEL ** -0.5)
    even_conv_w = nrm((N_EVEN, B_CONV, 3 * B_WIDTH), B_CONV ** -0.5)
    even_a_log = jnp.log(jax.random.uniform(next(ks), (N_EVEN, B_HEADS), jnp.float32, 1.0, 16.0))
    even_dt_bias = _dt_bias(next(ks), (N_EVEN, B_HEADS))
    even_head_norm_w = 1.0 + nrm((N_EVEN, B_HEAD_DIM), 0.05)
    even_w_out = nrm((N_EVEN, A_WIDTH + B_WIDTH, D_MODEL), (A_WIDTH + B_WIDTH) ** -0.5)
    odd_w_in = nrm((N_ODD, D_MODEL, ODD_IN), D_MODEL ** -0.5)
    odd_conv_w = nrm((N_ODD, C_CONV, C_XBC), C_CONV ** -0.5)
    odd_conv_b = nrm((N_ODD, C_XBC), 0.02)
    odd_dt_bias = _dt_bias(next(ks), (N_ODD, C_HEADS))
    odd_a_log = jnp.log(jax.random.uniform(next(ks), (N_ODD, C_HEADS), jnp.float32, 1.0, 16.0))
    odd_d_skip = 1.0 + nrm((N_ODD, C_HEADS), 0.05)
    odd_out_norm_w = 1.0 + nrm((N_ODD, C_D_INNER), 0.05)
    odd_w_out = nrm((N_ODD, C_D_INNER, D_MODEL), C_D_INNER ** -0.5)
    return {'x': x, 'norm_w': norm_w, 'ffn_w_gate': ffn_w_gate, 'ffn_w_up': ffn_w_up,
            'ffn_w_down': ffn_w_down, 'even_w_in': even_w_in, 'even_conv_w': even_conv_w,
            'even_a_log': even_a_log, 'even_dt_bias': even_dt_bias, 'even_head_norm_w': even_head_norm_w,
            'even_w_out': even_w_out, 'odd_w_in': odd_w_in, 'odd_conv_w': odd_conv_w,
            'odd_conv_b': odd_conv_b, 'odd_dt_bias': odd_dt_bias, 'odd_a_log': odd_a_log,
            'odd_d_skip': odd_d_skip, 'odd_out_norm_w': odd_out_norm_w, 'odd_w_out': odd_w_out}


def reference(x, norm_w, ffn_w_gate, ffn_w_up, ffn_w_down, even_w_in, even_conv_w, even_a_log,
              even_dt_bias, even_head_norm_w, even_w_out, odd_w_in, odd_conv_w, odd_conv_b, odd_dt_bias,
              odd_a_log, odd_d_skip, odd_out_norm_w, odd_w_out):
    h = x
    for layer in range(DEPTH):
        nw = norm_w[layer]
        i = layer // 2
        f = swiglu(rmsnorm(h, nw[0]), ffn_w_gate[layer, 0], ffn_w_up[layer, 0], ffn_w_down[layer, 0])
        h = h + FFN_RES_SCALE * rmsnorm(f, nw[1])
        u = rmsnorm(h, nw[2])
        if layer % 2 == 0:
            m = even_mixer(u, even_w_in[i], even_conv_w[i], even_a_log[i], even_dt_bias[i],
                           even_head_norm_w[i], even_w_out[i])
        else:
            m = odd_mixer(u, odd_w_in[i], odd_conv_w[i], odd_conv_b[i], odd_dt_bias[i], odd_a_log[i],
                          odd_d_skip[i], odd_out_norm_w[i], odd_w_out[i])
        h = h + rmsnorm(m, nw[3])
        f = swiglu(rmsnorm(h, nw[4]), ffn_w_gate[layer, 1], ffn_w_up[layer, 1], ffn_w_down[layer, 1])
        h = h + FFN_RES_SCALE * rmsnorm(f, nw[5])
    return h
```

```python
from contextlib import ExitStack
import numpy as np
import ml_dtypes
import concourse.bass as bass
import concourse.mybir as mybir
from concourse.bass_utils import run_bass_kernel_spmd

F32 = mybir.dt.float32
BF16 = mybir.dt.bfloat16
ALU = mybir.AluOpType
AF = mybir.ActivationFunctionType

SEQ = 4096
D = 1024
DFF = 2816
NT = SEQ // 128
EPS = 1e-6
import os
EVEN_PARTS = os.environ.get('EVEN_PARTS', 'ab')
NT_LIM = int(os.environ.get('NT_LIM', '32'))
STRICT = os.environ.get('STRICT', '0') == '1'
CUT = int(os.environ.get('CUT', '99'))
SEM_LIMIT = int(os.environ.get('SEM_LIMIT', '24000'))
NQ_SEMS = 20


class Buf:
    __slots__ = ("w", "r")

    def __init__(self):
        self.w = {}
        self.r = {}


class V:
    __slots__ = ("ap", "buf")

    def __init__(self, ap, buf):
        self.ap = ap
        self.buf = buf

    def rearrange(self, pat, **kw):
        return V(self.ap.rearrange(pat, **kw), self.buf)


class T:
    def __init__(self, h, buf=None):
        self.h = h
        self.buf = buf if buf is not None else Buf()
        self.chunks = {}

    def __getitem__(self, idx):
        return V(self.h[idx], self.buf)

    def c(self, key):
        t = self.chunks.get(key)
        if t is None:
            t = T(self.h, Buf())
            self.chunks[key] = t
        return t


class Eng:
    def __init__(self, ctx, name, e, compute=True, dma=False):
        self.ctx = ctx
        self.name = name
        self.e = e
        self.waited = {}
        self.last_tok = None
        self.own = set()
        self.cnt = 0
        self.sem = None
        if compute:
            self._new_sem()
        self.qsems = []
        self.qvals = []
        self.qi = 0
        if dma:
            for _ in range(NQ_SEMS):
                self.qsems.append(ctx.new_sem())
                self.qvals.append(0)

    def _new_sem(self):
        self.sem = self.ctx.new_sem()
        self.own.add(self.sem)
        self.cnt = 0

    def wait(self, toks):
        for s, v in toks.items():
            if self.waited.get(s, 0) < v:
                self.e.wait_ge(self.ctx.sems[s], v)
                self.waited[s] = v

    def deps(self, reads, writes):
        need = {}
        for b in reads:
            for s, v in b.w.items():
                if need.get(s, 0) < v:
                    need[s] = v
        skip_own = (not STRICT) or self.name == "pe"
        for b in writes:
            for s, v in b.w.items():
                if s in self.own and skip_own:
                    continue
                if need.get(s, 0) < v:
                    need[s] = v
            for s, v in b.r.items():
                if s in self.own and skip_own:
                    continue
                if need.get(s, 0) < v:
                    need[s] = v
        self.wait(need)

    def token(self, inc):
        if inc and self.cnt >= SEM_LIMIT:
            pass
        return (self.sem, self.cnt + 1)

    def mark(self, reads, writes, tok):
        s, v = tok
        for b in reads:
            if b.r.get(s, 0) < v:
                b.r[s] = v
        for b in writes:
            if b.w.get(s, 0) < v:
                b.w[s] = v

    def issue(self, fn, reads, writes, inc=True):
        self.deps(reads, writes)
        tok = (self.sem, self.cnt + 1)
        ins = fn()
        self.mark(reads, writes, tok)
        if inc:
            ins.then_inc(self.ctx.sems[self.sem], 1)
            self.cnt += 1
            self.last_tok = (self.sem, self.cnt)
            if self.cnt >= SEM_LIMIT:
                self._new_sem()
        return ins

    def dma(self, out, in_, share=False, **kw):
        need = {}
        if not share:
            self.qi = (self.qi + 1) % NQ_SEMS
        qi = self.qi
        s = self.qsems[qi]
        for ss, v in in_.buf.w.items():
            if need.get(ss, 0) < v:
                need[ss] = v
        for dct in (out.buf.w, out.buf.r):
            for ss, v in dct.items():
                if ss == s and share:
                    continue
                if need.get(ss, 0) < v:
                    need[ss] = v
        if not share and self.qvals[qi] > 0:
            if need.get(s, 0) < self.qvals[qi]:
                need[s] = self.qvals[qi]
        self.wait(need)
        self.qvals[qi] += 16
        v = self.qvals[qi]
        self.e.dma_start(out=out.ap, in_=in_.ap, **kw).then_inc(self.ctx.sems[s], 16)
        if in_.buf.r.get(s, 0) < v:
            in_.buf.r[s] = v
        if out.buf.w.get(s, 0) < v:
            out.buf.w[s] = v
        return (s, v)


class Ctx:
    def __init__(self, nc, es):
        self.nc = nc
        self.es = es
        self.sems = []
        self.pe = Eng(self, "pe", nc.tensor)
        self.act = Eng(self, "act", nc.scalar, dma=True)
        self.dve = Eng(self, "dve", nc.vector)
        self.pool = Eng(self, "pool", nc.gpsimd, dma=True)
        self.sp = Eng(self, "sp", nc.sync, compute=False, dma=True)
        self.engs = [self.pe, self.act, self.dve, self.pool, self.sp]
        self.nalloc = 0

    def new_sem(self):
        h = self.es.enter_context(self.nc.semaphore("s%d" % len(self.sems)))
        self.sems.append(h)
        return len(self.sems) - 1

    def sb(self, es, shape, dt, name=None):
        self.nalloc += 1
        return T(es.enter_context(self.nc.sbuf_tensor("%s_%d" % (name or "t", self.nalloc), shape, dt)))

    def ps(self, es, shape, dt, name=None):
        self.nalloc += 1
        return T(es.enter_context(self.nc.psum_tensor("%s_%d" % (name or "p", self.nalloc), shape, dt)))

    def barrier(self):
        toks = {}
        for e in self.engs:
            if e.last_tok is not None:
                toks[e.last_tok[0]] = e.last_tok[1]
            for s, v in zip(e.qsems, e.qvals):
                if v > 0:
                    toks[s] = v
        for e in self.engs:
            e.wait(toks)

    def mm(self, out, lhsT, rhs, start, stop, last=None, **kw):
        if last is None:
            last = stop
        return self.pe.issue(
            lambda: self.nc.tensor.matmul(out=out.ap, lhsT=lhsT.ap, rhs=rhs.ap, start=start, stop=stop, **kw),
            [lhsT.buf, rhs.buf], [out.buf], inc=last)

    def tr(self, out, in_, ident, last=True):
        return self.pe.issue(
            lambda: self.nc.tensor.transpose(out=out.ap, in_=in_.ap, identity=ident.ap),
            [in_.buf, ident.buf], [out.buf], inc=last)

    def actf(self, out, in_, func, bias=None, scale=None, accum=None):
        reads = [in_.buf]
        writes = [out.buf]
        kw = {}
        if bias is not None:
            if isinstance(bias, V):
                reads.append(bias.buf)
                kw["bias"] = bias.ap
            else:
                kw["bias"] = bias
        if scale is not None:
            if isinstance(scale, V):
                reads.append(scale.buf)
                kw["scale"] = scale.ap
            else:
                kw["scale"] = scale
        if accum is not None:
            writes.append(accum.buf)
            kw["accum_out"] = accum.ap
        return self.act.issue(
            lambda: self.nc.scalar.activation(out=out.ap, in_=in_.ap, func=func, **kw), reads, writes)

    def _veng(self, eng):
        return (self.dve, self.nc.vector) if eng == "dve" else (self.pool, self.nc.gpsimd)

    def ts(self, eng, out, in0, s1, s2, op0, op1=None, accum=None):
        E, e = self._veng(eng)
        reads = [in0.buf]
        writes = [out.buf]
        a1 = s1
        a2 = s2
        if isinstance(s1, V):
            reads.append(s1.buf)
            a1 = s1.ap
        if isinstance(s2, V):
            reads.append(s2.buf)
            a2 = s2.ap
        kw = {}
        if op1 is not None:
            kw["op1"] = op1
        if accum is not None:
            writes.append(accum.buf)
            kw["accum_out"] = accum.ap
        return E.issue(lambda: e.tensor_scalar(out=out.ap, in0=in0.ap, scalar1=a1, scalar2=a2, op0=op0, **kw),
                       reads, writes)

    def tt(self, eng, out, in0, in1, op):
        E, e = self._veng(eng)
        return E.issue(lambda: e.tensor_tensor(out=out.ap, in0=in0.ap, in1=in1.ap, op=op),
                       [in0.buf, in1.buf], [out.buf])

    def stt(self, out, in0, scalar, in1, op0, op1):
        reads = [in0.buf, in1.buf]
        a = scalar
        if isinstance(scalar, V):
            reads.append(scalar.buf)
            a = scalar.ap
        return self.dve.issue(
            lambda: self.nc.vector.scalar_tensor_tensor(out=out.ap, in0=in0.ap, scalar=a, in1=in1.ap,
                                                        op0=op0, op1=op1), reads, [out.buf])

    def copy(self, eng, out, in_):
        if eng == "act":
            return self.actf(out, in_, AF.Copy)
        E, e = self._veng(eng)
        return E.issue(lambda: e.tensor_copy(out=out.ap, in_=in_.ap), [in_.buf], [out.buf])

    def recip(self, out, in_):
        return self.dve.issue(lambda: self.nc.vector.reciprocal(out=out.ap, in_=in_.ap), [in_.buf], [out.buf])

    def memset(self, eng, out, val):
        E, e = self._veng(eng)
        return E.issue(lambda: e.memset(out.ap, val), [], [out.buf])


class Stats:
    def __init__(self, c, es, n=64):
        self.t = c.sb(es, [128, n], F32)
        self.n = n
        self.i = 0

    def get(self):
        k = self.i % self.n
        self.i += 1
        return self.t.c(k)[:, k:k + 1]


def rstd_chain(c, st, ss, n):
    a = st.get()
    c.ts("dve", a, ss, 1.0 / n, EPS, ALU.mult, ALU.add)
    b = st.get()
    c.actf(b, a, AF.Sqrt)
    r = st.get()
    c.recip(r, b)
    return r


def ffn_stage(c, P, L, j, hsrc, hdst):
    nc = c.nc
    pre_i, post_i = (0, 1) if j == 0 else (4, 5)
    with ExitStack() as es:
        Wg = c.sb(es, [128, 8, DFF], BF16, "Wg")
        Wu = c.sb(es, [128, 8, DFF], BF16, "Wu")
        Wd = c.sb(es, [128, 22, D], BF16, "Wd")
        gpre = c.sb(es, [128, D], F32, "gpre")
        gpost = c.sb(es, [128, D], F32, "gpost")
        hl = [c.sb(es, [128, D], F32, "hl%d" % i) for i in range(2)]
        hr = [c.sb(es, [128, D], F32, "hr%d" % i) for i in range(2)]
        xn = [c.sb(es, [128, D], BF16, "xn%d" % i) for i in range(4)]
        xT = [c.sb(es, [128, 8, 512], BF16, "xT%d" % i) for i in range(1)]
        actb = c.sb(es, [128, 22, 512], BF16, "actb")
        sg = [c.sb(es, [128, 512], F32, "sg%d" % i) for i in range(1)]
        tmp = [c.sb(es, [128, D], F32, "tmp%d" % i) for i in range(1)]
        junk = c.sb(es, [128, D], BF16, "junk")
        st = Stats(c, es, 64)
        pT = c.ps(es, [128, 8, 128], BF16, "pT")
        pG = [c.ps(es, [128, 512], F32, "pG%d" % i) for i in range(2)]
        pU = [c.ps(es, [128, 512], F32, "pU%d" % i) for i in range(2)]
        pD = [c.ps(es, [128, 512], F32, "pD%d" % i) for i in range(3)]

        gsrc = P["ffn_w_gate"].h[L, j].rearrange("(k p) f -> p k f", p=128)
        usrc = P["ffn_w_up"].h[L, j].rearrange("(k p) f -> p k f", p=128)
        dsrc = P["ffn_w_down"].h[L, j].rearrange("(k p) f -> p k f", p=128)
        for k in range(8):
            c.pool.dma(Wg[:, k, :], V(gsrc[:, k, :], P["ffn_w_gate"].buf), share=(k > 0))
        for k in range(8):
            c.pool.dma(Wu[:, k, :], V(usrc[:, k, :], P["ffn_w_up"].buf), share=(k > 0))
        for k in range(22):
            c.pool.dma(Wd[:, k, :], V(dsrc[:, k, :], P["ffn_w_down"].buf), share=(k > 0))
        c.sp.dma(gpre[:], V(P["norm_w"].h[L, pre_i, :].partition_broadcast(128), P["norm_w"].buf))
        c.sp.dma(gpost[:], V(P["norm_w"].h[L, post_i, :].partition_broadcast(128), P["norm_w"].buf))
        c.ts("dve", gpost[:], gpost[:], 0.5, None, ALU.mult)

        ident = P["ident"]
        hts = {}

        def norm(g):
            for s in range(4):
                ti = g * 4 + s
                ht = hl[ti % 2]
                c.sp.dma(ht[:], hsrc[ti][:, :])
                ss = st.get()
                c.actf(junk[:], ht[:], AF.Square, accum=ss)
                r = rstd_chain(c, st, ss, D)
                x = xn[ti % 4]
                c.stt(x[:], ht[:], r, gpre[:], ALU.mult, ALU.mult)

        def transp(g):
            for s in range(4):
                ti = g * 4 + s
                x = xn[ti % 4]
                for k in range(8):
                    c.tr(pT[:, k, :], x[:, k * 128:(k + 1) * 128], ident[:], last=(k == 7))
                c.copy("act", xT[0].c(s)[:, :, s * 128:(s + 1) * 128], pT[:])

        def xTv(g, k):
            return [xT[0].c(s).buf for s in range(4)]

        def gateup(g, hook):
            xt = xT[0]
            for f in range(22):
                pg = pG[f % 2]
                pu = pU[f % 2]
                for (W, pp) in ((Wg, pg), (Wu, pu)):
                    for k in range(8):
                        rhs = V(xt.h[:, k, :], xt.c(0).buf)
                        ins = c.pe.issue(
                            lambda W=W, pp=pp, k=k, rhs=rhs: nc.tensor.matmul(
                                out=pp.h[:], lhsT=W.h[:, k, f * 128:(f + 1) * 128], rhs=rhs.ap,
                                start=(k == 0), stop=(k == 7)),
                            [W.buf] + xTv(g, k), [pp.buf], inc=(k == 7))
                c.actf(sg[0][:], pg[:], AF.Silu)
                c.tt("dve", actb.c(f)[:, f, :], sg[0][:], pu[:], ALU.mult)
                if f == 8 and hook is not None:
                    hook()

        dcount = [0]

        def down(g):
            for s in range(4):
                ti = g * 4 + s
                ht = hr[ti % 2]
                c.sp.dma(ht[:], hsrc[ti][:, :])
                banks = []
                sss = []
                for half in range(2):
                    pd = pD[dcount[0] % 3]
                    dcount[0] += 1
                    banks.append(pd)
                    for f in range(22):
                        c.mm(pd[:], actb.c(f)[:, f, s * 128:(s + 1) * 128], Wd[:, f, half * 512:(half + 1) * 512],
                             start=(f == 0), stop=(f == 21))
                    ssh = st.get()
                    c.actf(junk[:, 0:512], pd[:], AF.Square, accum=ssh)
                    sss.append(ssh)
                ss = st.get()
                c.tt("dve", ss, sss[0], sss[1], ALU.add)
                r = rstd_chain(c, st, ss, D)
                tm = tmp[0]
                for half in range(2):
                    c.stt(tm[:, half * 512:(half + 1) * 512], banks[half][:], r,
                          gpost[:, half * 512:(half + 1) * 512], ALU.mult, ALU.mult)
                c.tt("pool", ht[:], ht[:], tm[:], ALU.add)
                c.sp.dma(hdst[ti][:, :], ht[:])

        NG = NT // 4
        norm(0)
        transp(0)
        for g in range(NG):
            nxt = (lambda g=g: norm(g + 1)) if g + 1 < NG else None
            gateup(g, nxt)
            if g + 1 < NG:
                transp(g + 1)
            down(g)
        c.barrier()


def load_bcast(c, es, src_ap, srcbuf, n, name):
    t = c.sb(es, [128, n], F32, name)
    c.sp.dma(t[:], V(src_ap.partition_broadcast(128), srcbuf))
    return t


def front_norm_T(c, st, ht, gain, xn, junk, pT, uT, ident):
    ss = st.get()
    c.actf(junk[:], ht[:], AF.Square, accum=ss)
    r = rstd_chain(c, st, ss, D)
    c.stt(xn[:], ht[:], r, gain[:], ALU.mult, ALU.mult)
    for k in range(8):
        c.tr(pT[:, k, :], xn[:, k * 128:(k + 1) * 128], ident[:], last=(k == 7))
    c.copy("act", uT[:], pT[:])


def outproj_stage(c, P, L, ysrc, wname, i, K, hsrc, hdst, fm=None):
    KC = K // 128
    with ExitStack() as es:
        W = c.sb(es, [128, KC, D], BF16, "Wo")
        wsrc = P[wname].h[i].rearrange("(k p) f -> p k f", p=128)
        for k in range(KC):
            c.pool.dma(W[:, k, :], V(wsrc[:, k, :], P[wname].buf), share=(k > 0))
        g3 = load_bcast(c, es, P["norm_w"].h[L, 3, :], P["norm_w"].buf, D, "g3")
        yt = [c.sb(es, [128, K], BF16, "yt") for _ in range(2)]
        yT = [c.sb(es, [128, KC, 128], BF16, "yT") for _ in range(2)]
        hr = [c.sb(es, [128, D], F32, "hr") for _ in range(2)]
        tmp = c.sb(es, [128, D], F32, "tmp")
        junk = c.sb(es, [128, 512], BF16, "junk")
        st = Stats(c, es, 32)
        pT = [c.ps(es, [128, 8, 128], BF16, "pT") for _ in range(2)]
        pD = [c.ps(es, [128, 512], F32, "pD") for _ in range(4)]
        ident = P["ident"]
        dc = 0
        for ti in range(NT):
            y = yt[ti % 2]
            h = hr[ti % 2]
            c.sp.dma(h[:], hsrc[ti][:, :])
            yt_T = yT[ti % 2]
            if fm is not None:
                c.sp.dma(yt_T[:], V(fm.h[:, ti * 128:(ti + 1) * 128].rearrange("(k p) t -> p k t", p=128), fm.buf))
            else:
                c.sp.dma(y[:], ysrc[ti][:, :])
            for kb in range(KC // 8 if fm is None else 0):
                p = pT[kb % 2]
                for k in range(8):
                    kk = kb * 8 + k
                    c.tr(p[:, k, :], y[:, kk * 128:(kk + 1) * 128], ident[:], last=(k == 7))
                c.copy("act", yt_T[:, kb * 8:(kb + 1) * 8, :], p[:])
            banks = []
            sss = []
            for half in range(2):
                pd = pD[dc % 4]
                dc += 1
                banks.append(pd)
                for k in range(KC):
                    c.mm(pd[:], yt_T[:, k, :], W[:, k, half * 512:(half + 1) * 512], start=(k == 0), stop=(k == KC - 1))
                ssh = st.get()
                c.actf(junk[:], pd[:], AF.Square, accum=ssh)
                sss.append(ssh)
            ss = st.get()
            c.tt("dve", ss, sss[0], sss[1], ALU.add)
            r = rstd_chain(c, st, ss, D)
            for half in range(2):
                c.stt(tmp[:, half * 512:(half + 1) * 512], banks[half][:], r, g3[:, half * 512:(half + 1) * 512],
                      ALU.mult, ALU.mult)
            c.tt("pool", h[:], h[:], tmp[:], ALU.add)
            c.sp.dma(hdst[ti][:, :], h[:])
        c.barrier()


def small_T(c, es, rows_src, nrow, ncol, P, name):
    nch = ncol // 128
    rowt = c.sb(es, [nrow, ncol], F32, name + "r")
    for j, (ap, buf) in enumerate(rows_src):
        c.sp.dma(rowt[j:j + 1, :], V(ap.partition_broadcast(1), buf))
    pt = c.ps(es, [128, nch, nrow], F32, name + "p")
    for ch in range(nch):
        c.tr(pt[:, ch, :], rowt[0:nrow, ch * 128:(ch + 1) * 128], P["identf"][0:nrow, 0:nrow], last=(ch == nch - 1))
    out = c.sb(es, [128, nch, nrow], F32, name)
    c.copy("dve", out[:], pt[:])
    return out


def ssd_stage(c, P, L, hsrc, ydst):
    nc = c.nc
    i = L // 2
    NX = 5152
    with ExitStack() as es0:
        cw = c.sb(es0, [128, 24, 5], F32, "cwk")
        with ExitStack() as es1:
            rows = [(P["odd_conv_w"].h[i, k, :], P["odd_conv_w"].buf) for k in range(4)]
            rows.append((P["odd_conv_b"].h[i, :], P["odd_conv_b"].buf))
            cw_tmp = small_T(c, es1, rows, 5, 3072, P, "cw")
            c.copy("dve", cw[:], cw_tmp[:])
            c.barrier()
        es = es0
        W = c.sb(es, [128, 8, NX], BF16, "Win")
        wsrc = P["odd_w_in"].h[i].rearrange("(k p) f -> p k f", p=128)
        for k in range(8):
            c.pool.dma(W[:, k, :], V(wsrc[:, k, :], P["odd_w_in"].buf), share=(k > 0))
        g2 = load_bcast(c, es, P["norm_w"].h[L, 2, :], P["norm_w"].buf, D, "g2")
        dtb = load_bcast(c, es, P["odd_dt_bias"].h[i, :], P["odd_dt_bias"].buf, 32, "dtb")
        negA = load_bcast(c, es, P["odd_a_log"].h[i, :], P["odd_a_log"].buf, 32, "negA")
        dsk = load_bcast(c, es, P["odd_d_skip"].h[i, :], P["odd_d_skip"].buf, 32, "dsk")
        onw = load_bcast(c, es, P["odd_out_norm_w"].h[i, :], P["odd_out_norm_w"].buf, 2048, "onw")
        c.actf(negA[:], negA[:], AF.Exp)
        c.ts("dve", negA[:], negA[:], -1.0, None, ALU.mult)

        ident, identf, triU, ones, maskb = P["ident"], P["identf"], P["triU"], P["ones"], P["maskb"]
        hl = [c.sb(es, [128, D], F32, "hl") for _ in range(2)]
        xn = c.sb(es, [128, D], BF16, "xn")
        junk = c.sb(es, [128, D], BF16, "junk")
        uT = [c.sb(es, [128, 8, 128], BF16, "uT") for _ in range(2)]
        st = Stats(c, es, 64)
        pcb = c.sb(es, [128, 24, 131], F32, "pcb")
        acc = [c.sb(es, [128, 128], F32, "acc") for _ in range(2)]
        xc = c.sb(es, [128, 24, 128], BF16, "xc")
        xtm = c.sb(es, [128, 32, 64], BF16, "xtm")
        Btm = c.sb(es, [128, 4, 128], BF16, "Btm")
        xdt = c.sb(es, [128, 32, 64], BF16, "xdt")
        xdec = c.sb(es, [128, 32, 64], BF16, "xdec")
        xD = c.sb(es, [128, 32, 64], BF16, "xD")
        sm = c.sb(es, [128, 12, 32], F32, "sm")
        R1 = [c.sb(es, [128, 8, 128], F32, "R1") for _ in range(2)]
        segT = [c.sb(es, [128, 8, 128], BF16, "segT") for _ in range(2)]
        cbs = [c.sb(es, [128, 128], BF16, "cbs") for _ in range(2)]
        MT = [c.sb(es, [128, 8, 128], BF16, "MT") for _ in range(2)]
        tb = c.sb(es, [128, 8, 64], F32, "tb")
        yb = c.sb(es, [128, 512], F32, "yb")
        zs = c.sb(es, [128, 512], F32, "zs")
        yz = c.sb(es, [128, 512], F32, "yz")
        yn = [c.sb(es, [128, 2048], BF16, "yn") for _ in range(2)]
        S = c.sb(es, [128, 32, 64], F32, "S")
        Sb = c.sb(es, [128, 32, 64], BF16, "Sb")
        for g in range(4):
            c.memset("dve", S.c(g)[:, g * 8:(g + 1) * 8, :], 0.0)
            c.memset("dve", Sb.c(g)[:, g * 8:(g + 1) * 8, :], 0.0)
        for ch in range(24):
            c.memset("pool", pcb.c(ch)[:, ch, :], 0.0)

        pT = c.ps(es, [128, 8, 128], BF16, "pT")
        pA = [c.ps(es, [128, 512], F32, "pA") for _ in range(7)]
        pai = [0]

        def bank():
            b = pA[pai[0] % 7]
            pai[0] += 1
            return b

        def smv(j):
            return sm.c(j)[:, j, :]

        for ti in range(NT):
            ht = hl[ti % 2]
            c.sp.dma(ht[:], hsrc[ti][:, :])
            u = uT[ti % 2]
            front_norm_T(c, st, ht, g2, xn, junk, pT, u, ident)

            pdt = bank()
            for k in range(8):
                c.mm(pdt[:, 0:32], u[:, k, :], W[:, k, 5120:5152], start=(k == 0), stop=(k == 7))
            dtr, ex, dt, a, nacum, eacum, tot, edarg, edec, etot = [smv(j) for j in range(10)]
            c.tt("dve", dtr, pdt[:, 0:32], dtb[:], ALU.add)
            c.actf(ex, dtr, AF.Exp)
            c.actf(dt, ex, AF.Ln, bias=1.0)
            c.tt("dve", a, dt, negA[:], ALU.mult)
            pac = bank()
            c.mm(pac[:, 0:32], triU[:], a, start=True, stop=True)
            c.mm(pac[:, 32:64], ones[:], a, start=True, stop=True)
            c.actf(nacum, pac[:, 0:32], AF.Copy, scale=-1.0)
            c.actf(eacum, pac[:, 0:32], AF.Exp)
            c.actf(tot, pac[:, 32:64], AF.Copy)
            c.tt("dve", edarg, tot, nacum, ALU.add)
            c.actf(edec, edarg, AF.Exp)
            c.actf(etot, tot, AF.Exp)

            for ch in range(24):
                pp = bank()
                col = 2048 + ch * 128
                for k in range(8):
                    c.mm(pp[:, 0:128], W[:, k, col:col + 128], u[:, k, :], start=(k == 0), stop=(k == 7))
                pc = pcb.c(ch)
                c.copy("act", pc[:, ch, 3:131], pp[:, 0:128])
                ac = acc[ch % 2]
                c.ts("dve", ac[:], pc[:, ch, 3:131], cw[:, ch, 3:4], cw[:, ch, 4:5], ALU.mult, ALU.add)
                for kk in (2, 1, 0):
                    c.stt(ac[:], pc[:, ch, kk:kk + 128], cw[:, ch, kk:kk + 1], ac[:], ALU.mult, ALU.add)
                c.copy("pool", pc[:, ch, 0:3], pc[:, ch, 128:131])
                c.actf(xc.c(ch)[:, ch, :], ac[:], AF.Silu)

            for kb in range(2):
                for k in range(8):
                    ch = kb * 8 + k
                    c.tr(pT[:, k, :], xc.c(ch)[:, ch, :], ident[:], last=(k == 7))
                c.copy("act", xtm[:, kb * 16:(kb + 1) * 16, :], pT[:].rearrange("p k (a b) -> p (k a) b", a=2))
            for g in range(4):
                c.tr(pT[:, g, :], xc.c(16 + g)[:, 16 + g, :], ident[:], last=(g == 3))
            c.copy("act", Btm[:], pT[:, 0:4, :])
            bc = lambda v: V(v.ap.unsqueeze(2).broadcast_to([128, 32, 64]), v.buf)
            c.tt("dve", xdt[:], xtm[:], bc(dt), ALU.mult)
            c.tt("pool", xdec[:], xdt[:], bc(edec), ALU.mult)
            c.tt("pool", xD[:], xtm[:], V(dsk.h[:, :].unsqueeze(2).broadcast_to([128, 32, 64]), dsk.buf), ALU.mult)

            ynt = yn[ti % 2]
            for g in range(4):
                hs = slice(g * 8, (g + 1) * 8)
                r1 = R1[g % 2]
                c.tt("pool", r1[:], V(a.ap[:, hs].unsqueeze(2).broadcast_to([128, 8, 128]), a.buf),
                     V(triU.h[:, :].unsqueeze(1).broadcast_to([128, 8, 128]), triU.buf), ALU.mult)
                sg_ = segT[g % 2]
                for half in range(2):
                    ps_ = bank()
                    c.mm(ps_[:], ones[:], r1[:, half * 4:(half + 1) * 4, :], start=True, stop=False, last=False)
                    c.mm(ps_[:], ident[:], maskb[:], start=False, stop=True)
                    for rr in range(4):
                        r = half * 4 + rr
                        hcol = g * 8 + r
                        c.actf(sg_[:, r, :], ps_[:, rr * 128:(rr + 1) * 128], AF.Exp,
                               bias=V(nacum.ap[:, hcol:hcol + 1], nacum.buf))
                pcbk = bank()
                c.mm(pcbk[:, 0:128], xc.c(16 + g)[:, 16 + g, :], xc.c(20 + g)[:, 20 + g, :], start=True, stop=True)
                cb_ = cbs[g % 2]
                c.copy("act", cb_[:], pcbk[:, 0:128])
                mt = MT[g % 2]
                c.tt("dve", mt[:], sg_[:], V(cb_.h[:, :].unsqueeze(1).broadcast_to([128, 8, 128]), cb_.buf), ALU.mult)
                pY1 = bank()
                c.mm(pY1[:], ident[:], xD[:, hs, :], start=True, stop=False, last=False)
                for r in range(8):
                    c.mm(pY1[:, r * 64:(r + 1) * 64], mt[:, r, :], xdt[:, g * 8 + r, :], start=False, stop=(r == 7),
                         last=(r == 7))
                pY2 = bank()
                c.mm(pY2[:], xc.c(20 + g)[:, 20 + g, :], Sb.c(g)[:, hs, :], start=True, stop=True)
                c.tt("dve", tb[:], pY2[:].rearrange("p (r d) -> p r d", r=8),
                     V(eacum.ap[:, hs].unsqueeze(2).broadcast_to([128, 8, 64]), eacum.buf), ALU.mult)
                c.tt("dve", yb[:], tb[:].rearrange("p r d -> p (r d)"), pY1[:], ALU.add)
                pz = bank()
                for k in range(8):
                    c.mm(pz[:], u[:, k, :], W[:, k, g * 512:(g + 1) * 512], start=(k == 0), stop=(k == 7))
                c.actf(zs[:], pz[:], AF.Silu)
                c.tt("pool", yz[:], yb[:], zs[:], ALU.mult)
                ss = st.get()
                c.actf(junk[:, 0:512], yz[:], AF.Square, accum=ss)
                r = rstd_chain(c, st, ss, 512)
                c.stt(ynt[:, g * 512:(g + 1) * 512], yz[:], r, onw[:, g * 512:(g + 1) * 512], ALU.mult, ALU.mult)
                pS = bank()
                c.mm(pS[:], Btm[:, g, :], xdec[:, hs, :], start=True, stop=True)
                Sg = S.c(g)
                c.tt("pool", Sg[:, hs, :], Sg[:, hs, :],
                     V(etot.ap[:, hs].unsqueeze(2).broadcast_to([128, 8, 64]), etot.buf), ALU.mult)
                c.tt("dve", Sg[:, hs, :], Sg[:, hs, :], pS[:].rearrange("p (r d) -> p r d", r=8), ALU.add)
                c.copy("act", Sb.c(g)[:, hs, :], Sg[:, hs, :])
            c.sp.dma(ydst[ti][:, :], ynt[:])
        c.barrier()


def even_stage(c, P, L, hsrc, omT, vatt):
    nc = c.nc
    i = L // 2
    NX = 3592
    ident, identf, triU, ones, maskb = P["ident"], P["identf"], P["triU"], P["ones"], P["maskb"]
    negones, maskus, onesb, maskp, sel = P["negones"], P["maskus"], P["onesb"], P["maskp"], P["sel"]
    SCALE_B = 128.0 ** -0.5
    with ExitStack() as esq:
        qkd = P["qkd"]
        with ExitStack() as es:
            cw = c.sb(es, [128, 12, 4], F32, "cwk")
            with ExitStack() as es1:
                rows = [(P["even_conv_w"].h[i, k, :], P["even_conv_w"].buf) for k in range(4)]
                cw_tmp = small_T(c, es1, rows, 4, 1536, P, "cw")
                c.copy("dve", cw[:], cw_tmp[:])
                c.barrier()
            W = c.sb(es, [128, 8, NX], BF16, "Win")
            wsrc = P["even_w_in"].h[i].rearrange("(k p) f -> p k f", p=128)
            for k in range(8):
                c.pool.dma(W[:, k, :], V(wsrc[:, k, :], P["even_w_in"].buf), share=(k > 0))
            g2 = load_bcast(c, es, P["norm_w"].h[L, 2, :], P["norm_w"].buf, D, "g2")
            dtb = load_bcast(c, es, P["even_dt_bias"].h[i, :], P["even_dt_bias"].buf, 4, "dtb")
            negA = load_bcast(c, es, P["even_a_log"].h[i, :], P["even_a_log"].buf, 4, "negA")
            hnw = load_bcast(c, es, P["even_head_norm_w"].h[i, :], P["even_head_norm_w"].buf, 128, "hnw")
            c.actf(negA[:], negA[:], AF.Exp)
            c.ts("dve", negA[:], negA[:], -1.0, None, ALU.mult)

            hl = [c.sb(es, [128, D], F32, "hl") for _ in range(2)]
            xn = c.sb(es, [128, D], BF16, "xn")
            junk = c.sb(es, [128, D], BF16, "junk")
            uT = [c.sb(es, [128, 8, 128], BF16, "uT") for _ in range(2)]
            st = Stats(c, es, 64)
            vt = [c.sb(es, [128, 512], BF16, "vt") for _ in range(2)]
            qkt = [c.sb(es, [128, 8, 128], BF16, "qkt") for _ in range(2)]
            zs = c.sb(es, [128, 512], F32, "zs")
            sm = c.sb(es, [128, 48, 4], F32, "sm")
            smi = [0]

            def smv():
                j = smi[0] % 32
                smi[0] += 1
                return sm.c(j)[:, j, :]

            pcb = c.sb(es, [128, 12, 131], F32, "pcb")
            acc = [c.sb(es, [128, 128], F32, "acc") for _ in range(2)]
            xg = c.sb(es, [128, 12, 128], F32, "xg")
            xtm = c.sb(es, [128, 12, 128], F32, "xtm")
            dg = [c.sb(es, [128, 4, 128], F32, "dg") for _ in range(2)]
            knT = c.sb(es, [128, 4, 128], F32, "knT")
            kbT = c.sb(es, [128, 4, 128], F32, "kbT")
            qnT = c.sb(es, [128, 4, 128], F32, "qnT")
            qdT = c.sb(es, [128, 4, 128], F32, "qdT")
            kbg = c.sb(es, [128, 4, 128], F32, "kbg")
            kdec = c.sb(es, [128, 4, 128], F32, "kdec")
            vb = c.sb(es, [128, 4, 128], F32, "vb")
            R1 = c.sb(es, [128, 4, 128], F32, "R1")
            segT = c.sb(es, [128, 4, 128], F32, "segT")
            segU = c.sb(es, [128, 4, 128], F32, "segU")
            qkT = c.sb(es, [128, 4, 128], F32, "qkT")
            Am = [c.sb(es, [128, 4, 128], F32, "Am") for _ in range(2)]
            Bm = [c.sb(es, [128, 4, 128], F32, "Bm") for _ in range(2)]
            Xm = [c.sb(es, [128, 4, 128], F32, "Xm") for _ in range(2)]
            nwT = c.sb(es, [128, 4, 128], F32, "nwT")
            vn = c.sb(es, [128, 4, 128], F32, "vn")
            S = c.sb(es, [128, 4, 128], F32, "S")
            on = c.sb(es, [128, 4, 128], F32, "on")
            ob = c.sb(es, [128, 4, 128], BF16, "ob")
            obT = [c.sb(es, [128, 4, 128], BF16, "obT") for _ in range(2)]
            c.memset("dve", S[:], 0.0)
            for ch in range(12):
                c.memset("pool", pcb.c(ch)[:, ch, :], 0.0)

            pT = c.ps(es, [128, 8, 128], BF16, "pT")
            pA = [c.ps(es, [128, 512], F32, "pA") for _ in range(7)]
            pai = [0]

            def bank():
                b = pA[pai[0] % 7]
                pai[0] += 1
                return b

            def trf(dst3, src_fn, n):
                pb_ = bank()
                for j in range(n):
                    c.tr(pb_[:, j * 128:(j + 1) * 128], src_fn(j), identf[:], last=(j == n - 1))
                return pb_

            def b4(v):
                return v.rearrange("p (h d) -> p h d", h=4)

            def bcl(v, n=128):
                return V(v.ap.unsqueeze(2).broadcast_to([128, 4, n]), v.buf)

            def bcm(t):
                return V(t.h[:, :].unsqueeze(1).broadcast_to([128, 4, 128]), t.buf)

            for ti in range(min(NT, NT_LIM)):
                tsl = slice(ti * 128, (ti + 1) * 128)
                ht = hl[ti % 2]
                c.sp.dma(ht[:], hsrc[ti][:, :])
                u = uT[ti % 2]
                front_norm_T(c, st, ht, g2, xn, junk, pT, u, ident)

                for cch in range(8):
                    pp = bank()
                    for k in range(8):
                        c.mm(pp[:, 0:128], W[:, k, cch * 128:(cch + 1) * 128], u[:, k, :], start=(k == 0), stop=(k == 7))
                    c.copy("act" if cch % 2 == 0 else "dve", qkt[ti % 2][:, cch, :], pp[:, 0:128])
                c.sp.dma(V(qkd.h[:, :, tsl], qkd.buf), qkt[ti % 2][:])
                pv = bank()
                for k in range(8):
                    c.mm(pv[:], u[:, k, :], W[:, k, 1024:1536], start=(k == 0), stop=(k == 7))
                v_ = vt[ti % 2]
                c.copy("dve", v_[:], pv[:])
                c.sp.dma(vatt[ti][:, :], v_[:])
                if CUT <= 1:
                    continue
                pz = bank()
                for k in range(8):
                    c.mm(pz[:], u[:, k, :], W[:, k, 3072:3584], start=(k == 0), stop=(k == 7))
                c.actf(zs[:], pz[:], AF.Silu)
                if CUT <= 2:
                    continue
                pba = bank()
                for k in range(8):
                    c.mm(pba[:, 0:8], u[:, k, :], W[:, k, 3584:3592], start=(k == 0), stop=(k == 7))
                beta, spi, ex, spv, g, nacum, acum, egc, tot, edarg, edec, etot = [smv() for _ in range(12)]
                c.actf(beta, pba[:, 0:4], AF.Sigmoid)
                c.tt("dve", spi, pba[:, 4:8], dtb[:], ALU.add)
                c.actf(ex, spi, AF.Exp)
                c.actf(spv, ex, AF.Ln, bias=1.0)
                c.tt("dve", g, spv, negA[:], ALU.mult)
                pac = bank()
                c.mm(pac[:, 0:4], triU[:], g, start=True, stop=True)
                c.mm(pac[:, 4:8], ones[:], g, start=True, stop=True)
                c.actf(nacum, pac[:, 0:4], AF.Copy, scale=-1.0)
                c.actf(acum, pac[:, 0:4], AF.Copy)
                c.actf(egc, pac[:, 0:4], AF.Exp)
                c.actf(tot, pac[:, 4:8], AF.Copy)
                c.tt("dve", edarg, tot, nacum, ALU.add)
                c.actf(edec, edarg, AF.Exp)
                c.actf(etot, tot, AF.Exp)

                if CUT <= 3:
                    continue
                for ch in range(12):
                    pp = bank()
                    col = 1536 + ch * 128
                    for k in range(8):
                        c.mm(pp[:, 0:128], W[:, k, col:col + 128], u[:, k, :], start=(k == 0), stop=(k == 7))
                    pc = pcb.c(ch)
                    c.copy("act", pc[:, ch, 3:131], pp[:, 0:128])
                    ac = acc[ch % 2]
                    c.ts("dve", ac[:], pc[:, ch, 3:131], cw[:, ch, 3:4], None, ALU.mult)
                    for kk in (2, 1, 0):
                        c.stt(ac[:], pc[:, ch, kk:kk + 128], cw[:, ch, kk:kk + 1], ac[:], ALU.mult, ALU.add)
                    c.copy("pool", pc[:, ch, 0:3], pc[:, ch, 128:131])
                    c.actf(xg.c(ch)[:, ch, :], ac[:], AF.Silu)
                if CUT <= 4:
                    continue
                for q3 in range(3):
                    pq = trf(None, lambda j: xg.c(q3 * 4 + j)[:, q3 * 4 + j, :], 4)
                    c.copy("act" if q3 != 1 else "dve", xtm[:, q3 * 4:(q3 + 1) * 4, :], b4(pq[:]))
                if CUT <= 5:
                    continue
                ssq = sm.c("ssq")
                ssqk = [V(sm.h[:, 40 + (j // 4), (j % 4):(j % 4) + 1], ssq.buf) for j in range(8)]
                for j in range(8):
                    c.actf(junk[:, 0:128], xtm[:, j, :], AF.Square, accum=ssqk[j])
                ssv = V(sm.h[:, 40:42, :], ssq.buf)
                rn0 = V(sm.h[:, 42:44, :], sm.c("rn0").buf)
                rn1 = V(sm.h[:, 44:46, :], sm.c("rn1").buf)
                rn = V(sm.h[:, 46:48, :], sm.c("rn").buf)
                c.ts("dve", rn0, ssv, EPS, None, ALU.add)
                c.actf(rn1, rn0, AF.Sqrt)
                c.recip(rn, rn1)
                rq = V(sm.h[:, 46, :], rn.buf)
                rk = V(sm.h[:, 47, :], rn.buf)
                s_kb, s_qn, s_qd, s_kbg, s_kdec = [smv() for _ in range(5)]
                c.tt("dve", s_kb, rk, beta, ALU.mult)
                c.ts("dve", s_qn, rq, SCALE_B, None, ALU.mult)
                c.tt("dve", s_qd, s_qn, egc, ALU.mult)
                c.tt("dve", s_kbg, s_kb, egc, ALU.mult)
                c.tt("dve", s_kdec, rk, edec, ALU.mult)
                if CUT <= 6:
                    continue
                for qi, (sc, src0, dstT) in enumerate(((rk, 4, knT), (s_kb, 4, kbT), (s_qn, 0, qnT), (s_qd, 0, qdT))):
                    d_ = dg[qi % 2]
                    c.tt("pool", d_[:], bcm(identf), bcl(sc), ALU.mult)
                    pb_ = bank()
                    c.mm(pb_[:], ones[:], d_[:], start=True, stop=True)
                    srcv = V(xg.h[:, src0:src0 + 4, :], xg.c(src0).buf)
                    E = c.dve
                    E.issue(lambda: nc.vector.tensor_tensor(out=dstT.h[:], in0=srcv.ap, in1=b4(pb_[:]).ap, op=ALU.mult),
                            [xg.c(src0 + j).buf for j in range(4)] + [pb_.buf], [dstT.buf])
                if CUT <= 7:
                    continue
                c.tt("pool", kbg[:], xtm[:, 4:8, :], bcl(s_kbg), ALU.mult)
                c.tt("pool", kdec[:], xtm[:, 4:8, :], bcl(s_kdec), ALU.mult)
                c.tt("pool", vb[:], xtm[:, 8:12, :], bcl(beta), ALU.mult)
                if CUT <= 8:
                    continue
                c.tt("pool", R1[:], bcl(g), bcm(triU), ALU.mult)
                pL = bank()
                c.mm(pL[:], ones[:], R1[:], start=True, stop=False, last=False)
                c.mm(pL[:], ident[:], maskb[:], start=False, stop=True)
                pU = bank()
                c.mm(pU[:], negones[:], R1[:], start=True, stop=False, last=False)
                c.mm(pU[:], ident[:], maskus[:], start=False, stop=True)
                for h in range(4):
                    c.actf(segT[:, h, :], pL[:, h * 128:(h + 1) * 128], AF.Exp, bias=V(nacum.ap[:, h:h + 1], nacum.buf))
                    c.actf(segU[:, h, :], pU[:, h * 128:(h + 1) * 128], AF.Exp, bias=V(acum.ap[:, h:h + 1], acum.buf))
                if CUT <= 9:
                    continue
                pG = bank()
                for h in range(4):
                    c.mm(pG[:, h * 128:(h + 1) * 128], kbT[:, h, :], knT[:, h, :], start=True, stop=True, last=(h == 3))
                pQK = bank()
                for h in range(4):
                    c.mm(pQK[:, h * 128:(h + 1) * 128], knT[:, h, :], qnT[:, h, :], start=True, stop=True, last=(h == 3))
                A_, B_, X_ = Am[0], Bm[0], Xm[0]
                c.stt(A_[:], b4(pG[:]), negones[:, 0:1], segU[:], ALU.mult, ALU.mult)
                c.tt("dve", qkT[:], b4(pQK[:]), segT[:], ALU.mult)
                pq = trf(None, lambda j: A_[:, j, :], 4)
                c.copy("act", B_[:], b4(pq[:]))
                c.tt("dve", X_[:], B_[:], bcm(identf), ALU.add)
                if CUT <= 10:
                    continue
                for lvl in range(1, 7):
                    A2, B2, X2 = Am[lvl % 2], Bm[lvl % 2], Xm[lvl % 2]
                    pAq = bank()
                    for h in range(4):
                        c.mm(pAq[:, h * 128:(h + 1) * 128], B_[:, h, :], A_[:, h, :], start=True, stop=True, last=(h == 3))
                    c.copy("act", A2[:], b4(pAq[:]))
                    if lvl < 6:
                        pBq = bank()
                        for h in range(4):
                            c.mm(pBq[:, h * 128:(h + 1) * 128], A_[:, h, :], B_[:, h, :], start=True, stop=True,
                                 last=(h == 3))
                        c.copy("dve", B2[:], b4(pBq[:]))
                    pX = bank()
                    for h in range(4):
                        c.mm(pX[:, h * 128:(h + 1) * 128], identf[:], X_[:, h, :], start=True, stop=False, last=False)
                        c.mm(pX[:, h * 128:(h + 1) * 128], A2[:, h, :], X_[:, h, :], start=False, stop=True, last=(h == 3))
                    c.copy("dve" if lvl % 2 else "act", X2[:], b4(pX[:]))
                    A_, B_, X_ = A2, B2, X2
                PT_ = X_
                if CUT <= 11:
                    continue
                pW = bank()
                for h in range(4):
                    c.mm(pW[:, h * 128:(h + 1) * 128], kbg[:, h, :], PT_[:, h, :], start=True, stop=True, last=(h == 3))
                c.actf(nwT[:], b4(pW[:]), AF.Copy, scale=-1.0)
                pV = bank()
                for h in range(4):
                    c.mm(pV[:, h * 128:(h + 1) * 128], PT_[:, h, :], vb[:, h, :], start=True, stop=False, last=False)
                    c.mm(pV[:, h * 128:(h + 1) * 128], nwT[:, h, :], S[:, h, :], start=False, stop=True, last=(h == 3))
                c.copy("dve", vn[:], b4(pV[:]))
                pO = bank()
                for h in range(4):
                    c.mm(pO[:, h * 128:(h + 1) * 128], qdT[:, h, :], S[:, h, :], start=True, stop=False, last=False)
                    c.mm(pO[:, h * 128:(h + 1) * 128], qkT[:, h, :], vn[:, h, :], start=False, stop=True, last=(h == 3))
                pS = bank()
                for h in range(4):
                    c.mm(pS[:, h * 128:(h + 1) * 128], kdec[:, h, :], vn[:, h, :], start=True, stop=True, last=(h == 3))
                for h in range(4):
                    c.stt(S[:, h, :], S[:, h, :], V(etot.ap[:, h:h + 1], etot.buf), pS[:, h * 128:(h + 1) * 128],
                          ALU.mult, ALU.add)
                if CUT <= 12:
                    continue
                sso = smv()
                for h in range(4):
                    c.actf(junk[:, 0:128], pO[:, h * 128:(h + 1) * 128], AF.Square, accum=V(sso.ap[:, h:h + 1], sso.buf))
                r0, r1_, r2 = smv(), smv(), smv()
                c.ts("dve", r0, sso, 1.0 / 128, EPS, ALU.mult, ALU.add)
                c.actf(r1_, r0, AF.Sqrt)
                c.recip(r2, r1_)
                for h in range(4):
                    c.stt(on[:, h, :], pO[:, h * 128:(h + 1) * 128], V(r2.ap[:, h:h + 1], r2.buf), hnw[:], ALU.mult, ALU.mult)
                c.tt("pool", ob[:], on[:], zs[:].rearrange("p (h d) -> p h d", h=4), ALU.mult)
                for h in range(4):
                    c.tr(pT[:, h, :], ob[:, h, :], ident[:], last=(h == 3))
                o_ = obT[ti % 2]
                c.copy("act", o_[:], pT[:, 0:4, :])
                c.sp.dma(V(omT.h[512:1024, tsl].rearrange("(c p) t -> p c t", p=128), omT.c(ti).buf), o_[:])
            c.barrier()

        if "b" not in EVEN_PARTS:
            return
        with ExitStack() as es:
            QT = c.sb(es, [128, 4, SEQ], BF16, "QT")
            KT = c.sb(es, [128, 4, SEQ], BF16, "KT")
            for cq_ in range(4):
                c.sp.dma(QT[:, cq_, :], V(qkd.h[:, cq_, :], qkd.buf))
                c.sp.dma(KT[:, cq_, :], V(qkd.h[:, 4 + cq_, :], qkd.buf))
            accs = c.sb(es, [65, 4, 2048], F32, "accs")
            Vp = [c.sb(es, [128, 32, 4, 65], BF16, "Vp") for _ in range(2)]
            eb = [c.sb(es, [128, 2, 128], BF16, "eb") for _ in range(3)]
            rden = c.sb(es, [64, 512], F32, "rden")
            oT = [c.sb(es, [64, 2048], BF16, "oT") for _ in range(2)]
            pA = [c.ps(es, [128, 512], F32, "pA") for _ in range(8)]
            pai = [0]

            def bank():
                b = pA[pai[0] % 8]
                pai[0] += 1
                return b

            for v_ in Vp:
                c.memset("pool", v_[:, :, :, 64:65], 1.0)
            cnt = 0
            ecnt = 0
            ocnt = 0
            vall = T(vatt[0].h, Buf())
            for hg in range(2):
                for H in range(2):
                    c.memset("pool", accs[:], 0.0)
                    for d in (1, 4, 16):
                        nb = 16 // d
                        b0 = nb * H
                        vp = Vp[cnt % 2]
                        cnt += 1
                        for r in range(d):
                            for lb in range(nb + 1):
                                b = b0 - 1 + lb
                                if b < 0:
                                    continue
                                t0 = r + d * 128 * b
                                src = P["vatt_full"].h[t0:t0 + d * 127 + 1:d, hg * 256:(hg + 1) * 256]
                                q = c.sp
                                q.dma(vp[:, r * (nb + 1) + lb, :, 0:64],
                                      V(src.rearrange("p (h e) -> p h e", h=4), P["vatt_full"].buf))
                        for hh in range(4):
                            h = hg * 4 + hh
                            cq = h // 2
                            pb = 64 * (h % 2)
                            units = [(r, b) for r in range(d) for b in range(b0, b0 + nb)]
                            for u4 in range(4):
                                pnum = bank()
                                for ui in range(4):
                                    r, b = units[u4 * 4 + ui]
                                    q0 = r + d * 128 * b
                                    q_ap = QT.h[pb:pb + 64, cq, q0:q0 + d * 127 + 1:d]
                                    kbs = [b - 1, b] if b >= 1 else [b]
                                    ps_ = bank()
                                    for j, kb_ in enumerate(kbs):
                                        k0 = r + d * 128 * kb_
                                        k_ap = KT.h[pb:pb + 64, cq, k0:k0 + d * 127 + 1:d]
                                        c.pe.issue(lambda: nc.tensor.matmul(out=ps_.h[:, j * 128:(j + 1) * 128], lhsT=k_ap,
                                                                            rhs=q_ap, start=True, stop=False),
                                                   [QT.buf, KT.buf], [ps_.buf], inc=False)
                                        mk = maskp if kb_ == b - 1 else maskb
                                        c.mm(ps_[:, j * 128:(j + 1) * 128], ident[:], mk[:, 0:128], start=False, stop=True)
                                    e_ = eb[ecnt % 3]
                                    ecnt += 1
                                    nk = len(kbs)
                                    c.actf(e_[:, 0:nk, :], ps_[:, 0:nk * 128].rearrange("p (j q) -> p j q", j=nk), AF.Exp,
                                           scale=0.125)
                                    for j, kb_ in enumerate(kbs):
                                        slot = r * (nb + 1) + (kb_ - (b0 - 1))
                                        c.mm(pnum[0:65, ui * 128:(ui + 1) * 128], vp[:, slot, hh, :], e_[:, j, :],
                                             start=(j == 0), stop=(j == nk - 1))
                                if d == 1:
                                    av = accs[0:65, hh, u4 * 512:(u4 + 1) * 512]
                                    pvw = pnum[0:65, :]
                                elif d == 4:
                                    av = accs[0:65, hh, u4:2048:4]
                                    pvw = pnum[0:65, :]
                                else:
                                    av = V(accs.h[0:65, hh, :].rearrange("p (i r) -> p r i", r=16)[:, u4 * 4:(u4 + 1) * 4, :],
                                           accs.buf)
                                    pvw = pnum[0:65, :].rearrange("p (u q) -> p u q", u=4)
                                c.tt("dve", av, av, pvw, ALU.add)
                    for hh in range(4):
                        h = hg * 4 + hh
                        o_ = oT[ocnt % 2]
                        ocnt += 1
                        for q4 in range(4):
                            pden = bank()
                            c.mm(pden[0:64, :], sel[0:65, 0:64], accs[0:65, hh, q4 * 512:(q4 + 1) * 512], start=True, stop=True)
                            c.recip(rden[:], pden[0:64, :])
                            c.tt("pool", o_[:, q4 * 512:(q4 + 1) * 512], accs[0:64, hh, q4 * 512:(q4 + 1) * 512], rden[:],
                                 ALU.mult)
                        c.sp.dma(V(omT.h[h * 64:(h + 1) * 64, H * 2048:(H + 1) * 2048], omT.c("a%d_%d" % (h, H)).buf), o_[:])
            c.barrier()

INPUT_NAMES = ["norm_w", "ffn_w_gate", "ffn_w_up", "ffn_w_down", "even_w_in", "even_conv_w", "even_a_log",
               "even_dt_bias", "even_head_norm_w", "even_w_out", "odd_w_in", "odd_conv_w", "odd_conv_b",
               "odd_dt_bias", "odd_a_log", "odd_d_skip", "odd_out_norm_w", "odd_w_out"]


def consts_np():
    k = np.arange(128)
    triU = (k[:, None] <= k[None, :]).astype(np.float32)
    maskb = np.where(k[None, :] >= k[:, None], 0.0, -30000.0).astype(np.float32)
    return {"ident": np.eye(128, dtype=np.float32).astype(ml_dtypes.bfloat16),
            "identf": np.eye(128, dtype=np.float32),
            "triU": triU, "ones": np.ones((128, 128), np.float32),
            "maskb": np.tile(maskb, (1, 4)).astype(ml_dtypes.bfloat16),
            "negones": -np.ones((128, 128), np.float32),
            "maskus": np.tile(np.where(k[None, :] < k[:, None], 0.0, -30000.0), (1, 4)).astype(ml_dtypes.bfloat16),
            "onesb": np.ones((128, 128), np.float32).astype(ml_dtypes.bfloat16),
            "maskp": np.where(k[None, :] <= k[:, None], 0.0, -30000.0).astype(ml_dtypes.bfloat16),
            "sel": np.concatenate([np.zeros((64, 64), np.float32), np.ones((64, 64), np.float32)], 0)}


CONST_SPECS = [("ident", [128, 128], BF16), ("identf", [128, 128], F32), ("triU", [128, 128], F32),
               ("ones", [128, 128], F32), ("maskb", [128, 512], BF16), ("negones", [128, 128], F32),
               ("maskus", [128, 512], BF16), ("onesb", [128, 128], BF16), ("maskp", [128, 128], BF16),
               ("sel", [128, 64], F32)]


def build(shapes, stages=None, debug=False):
    if stages is None:
        stages = list(range(12))
    nc = bass.Bass("TRN2", target_bir_lowering=False)
    P = {}
    x = nc.dram_tensor("x", [SEQ, D], F32, kind="ExternalInput").ap()
    for n in INPUT_NAMES:
        P[n] = T(nc.dram_tensor(n, list(shapes[n]), F32, kind="ExternalInput").ap())
    out = nc.dram_tensor("out", [SEQ, D], F32, kind="ExternalOutput").ap()
    ymix = nc.dram_tensor("ymix", [SEQ, 2048], BF16, kind="Internal").ap()
    omix = nc.dram_tensor("omix", [D, SEQ], BF16, kind="ExternalOutput" if debug else "Internal").ap()
    vattd = nc.dram_tensor("vattd", [SEQ, 512], BF16, kind="Internal").ap()
    qkdd = nc.dram_tensor("qkdd", [128, 8, SEQ], BF16, kind="Internal").ap()
    with ExitStack() as es:
        c = Ctx(nc, es)
        for (n, shp, dt) in CONST_SPECS:
            d = nc.dram_tensor(n, shp, dt, kind="ExternalInput").ap()
            t = c.sb(es, shp, dt, n + "_sb")
            c.sp.dma(t[:], V(d[:, :], Buf()))
            P[n] = t
        xt = [T(x[i * 128:(i + 1) * 128, :]) for i in range(NT)]
        ht = [T(out[i * 128:(i + 1) * 128, :]) for i in range(NT)]
        yt2048 = [T(ymix[i * 128:(i + 1) * 128, :]) for i in range(NT)]
        omT = T(omix)
        P["vatt_full"] = T(vattd)
        P["qkd"] = T(qkdd)
        vatt = [T(vattd[i * 128:(i + 1) * 128, :], P["vatt_full"].buf) for i in range(NT)]
        src = xt
        for sid in stages:
            L, kind = sid // 3, sid % 3
            if kind == 0:
                ffn_stage(c, P, L, 0, src, ht)
            elif kind == 2:
                ffn_stage(c, P, L, 1, src, ht)
            else:
                if L % 2 == 1:
                    ssd_stage(c, P, L, src, yt2048)
                    outproj_stage(c, P, L, yt2048, "odd_w_out", L // 2, 2048, src, ht)
                else:
                    even_stage(c, P, L, src, omT, vatt)
                    if "o" in os.environ.get("EVEN_PARTS", "abo"):
                        outproj_stage(c, P, L, None, "even_w_out", L // 2, 1024, src, ht, fm=omT)
            src = ht
        c.barrier()
    return nc


def kernel(**inputs):
    x = np.ascontiguousarray(inputs["x"], dtype=np.float32)
    nb = x.shape[0]
    shapes = {n: inputs[n].shape for n in INPUT_NAMES}
    nc = build(shapes)
    base = {n: np.ascontiguousarray(inputs[n], dtype=np.float32) for n in INPUT_NAMES}
    base.update(consts_np())
    in_maps = []
    for b in range(nb):
        m = dict(base)
        m["x"] = x[b]
        in_maps.append(m)
    res = run_bass_kernel_spmd(nc, in_maps, core_ids=list(range(nb)))
    return np.stack([np.asarray(r["out"]) for r in res.results], axis=0).astype(np.float32)
```

```python
from contextlib import ExitStack
import numpy as np
import ml_dtypes
import concourse.bass as bass
import concourse.mybir as mybir
from concourse.bass_utils import run_bass_kernel_spmd

F32 = mybir.dt.float32
BF16 = mybir.dt.bfloat16
ALU = mybir.AluOpType
AF = mybir.ActivationFunctionType

SEQ = 4096
D = 1024
DFF = 2816
NT = SEQ // 128
EPS = 1e-6
import os
EVEN_PARTS = os.environ.get('EVEN_PARTS', 'ab')
NT_LIM = int(os.environ.get('NT_LIM', '32'))
STRICT = os.environ.get('STRICT', '0') == '1'
CUT = int(os.environ.get('CUT', '99'))
SEM_LIMIT = int(os.environ.get('SEM_LIMIT', '24000'))
NQ_SEMS = 20


class Buf:
    __slots__ = ("w", "r")

    def __init__(self):
        self.w = {}
        self.r = {}


class V:
    __slots__ = ("ap", "buf")

    def __init__(self, ap, buf):
        self.ap = ap
        self.buf = buf

    def rearrange(self, pat, **kw):
        return V(self.ap.rearrange(pat, **kw), self.buf)


class T:
    def __init__(self, h, buf=None):
        self.h = h
        self.buf = buf if buf is not None else Buf()
        self.chunks = {}

    def __getitem__(self, idx):
        return V(self.h[idx], self.buf)

    def c(self, key):
        t = self.chunks.get(key)
        if t is None:
            t = T(self.h, Buf())
            self.chunks[key] = t
        return t


class Eng:
    def __init__(self, ctx, name, e, compute=True, dma=False):
        self.ctx = ctx
        self.name = name
        self.e = e
        self.waited = {}
        self.last_tok = None
        self.own = set()
        self.cnt = 0
        self.sem = None
        if compute:
            self._new_sem()
        self.qsems = []
        self.qvals = []
        self.qi = 0
        if dma:
            for _ in range(NQ_SEMS):
                self.qsems.append(ctx.new_sem())
                self.qvals.append(0)

    def _new_sem(self):
        self.sem = self.ctx.new_sem()
        self.own.add(self.sem)
        self.cnt = 0

    def wait(self, toks):
        for s, v in toks.items():
            if self.waited.get(s, 0) < v:
                self.e.wait_ge(self.ctx.sems[s], v)
                self.waited[s] = v

    def deps(self, reads, writes):
        need = {}
        for b in reads:
            for s, v in b.w.items():
                if need.get(s, 0) < v:
                    need[s] = v
        skip_own = (not STRICT) or self.name == "pe"
        for b in writes:
            for s, v in b.w.items():
                if s in self.own and skip_own:
                    continue
                if need.get(s, 0) < v:
                    need[s] = v
            for s, v in b.r.items():
                if s in self.own and skip_own:
                    continue
                if need.get(s, 0) < v:
                    need[s] = v
        self.wait(need)

    def token(self, inc):
        if inc and self.cnt >= SEM_LIMIT:
            pass
        return (self.sem, self.cnt + 1)

    def mark(self, reads, writes, tok):
        s, v = tok
        for b in reads:
            if b.r.get(s, 0) < v:
                b.r[s] = v
        for b in writes:
            if b.w.get(s, 0) < v:
                b.w[s] = v

    def issue(self, fn, reads, writes, inc=True):
        self.deps(reads, writes)
        tok = (self.sem, self.cnt + 1)
        ins = fn()
        self.mark(reads, writes, tok)
        if inc:
            ins.then_inc(self.ctx.sems[self.sem], 1)
            self.cnt += 1
            self.last_tok = (self.sem, self.cnt)
            if self.cnt >= SEM_LIMIT:
                self._new_sem()
        return ins

    def dma(self, out, in_, share=False, **kw):
        need = {}
        if not share:
            self.qi = (self.qi + 1) % NQ_SEMS
        qi = self.qi
        s = self.qsems[qi]
        for ss, v in in_.buf.w.items():
            if need.get(ss, 0) < v:
                need[ss] = v
        for dct in (out.buf.w, out.buf.r):
            for ss, v in dct.items():
                if ss == s and share:
                    continue
                if need.get(ss, 0) < v:
                    need[ss] = v
        if not share and self.qvals[qi] > 0:
            if need.get(s, 0) < self.qvals[qi]:
                need[s] = self.qvals[qi]
        self.wait(need)
        self.qvals[qi] += 16
        v = self.qvals[qi]
        self.e.dma_start(out=out.ap, in_=in_.ap, **kw).then_inc(self.ctx.sems[s], 16)
        if in_.buf.r.get(s, 0) < v:
            in_.buf.r[s] = v
        if out.buf.w.get(s, 0) < v:
            out.buf.w[s] = v
        return (s, v)


class Ctx:
    def __init__(self, nc, es):
        self.nc = nc
        self.es = es
        self.sems = []
        self.pe = Eng(self, "pe", nc.tensor)
        self.act = Eng(self, "act", nc.scalar, dma=True)
        self.dve = Eng(self, "dve", nc.vector)
        self.pool = Eng(self, "pool", nc.gpsimd, dma=True)
        self.sp = Eng(self, "sp", nc.sync, compute=False, dma=True)
        self.engs = [self.pe, self.act, self.dve, self.pool, self.sp]
        self.nalloc = 0

    def new_sem(self):
        h = self.es.enter_context(self.nc.semaphore("s%d" % len(self.sems)))
        self.sems.append(h)
        return len(self.sems) - 1

    def sb(self, es, shape, dt, name=None):
        self.nalloc += 1
        return T(es.enter_context(self.nc.sbuf_tensor("%s_%d" % (name or "t", self.nalloc), shape, dt)))

    def ps(self, es, shape, dt, name=None):
        self.nalloc += 1
        return T(es.enter_context(self.nc.psum_tensor("%s_%d" % (name or "p", self.nalloc), shape, dt)))

    def barrier(self):
        toks = {}
        for e in self.engs:
            if e.last_tok is not None:
                toks[e.last_tok[0]] = e.last_tok[1]
            for s, v in zip(e.qsems, e.qvals):
                if v > 0:
                    toks[s] = v
        for e in self.engs:
            e.wait(toks)

    def mm(self, out, lhsT, rhs, start, stop, last=None, **kw):
        if last is None:
            last = stop
        return self.pe.issue(
            lambda: self.nc.tensor.matmul(out=out.ap, lhsT=lhsT.ap, rhs=rhs.ap, start=start, stop=stop, **kw),
            [lhsT.buf, rhs.buf], [out.buf], inc=last)

    def tr(self, out, in_, ident, last=True):
        return self.pe.issue(
            lambda: self.nc.tensor.transpose(out=out.ap, in_=in_.ap, identity=ident.ap),
            [in_.buf, ident.buf], [out.buf], inc=last)

    def actf(self, out, in_, func, bias=None, scale=None, accum=None):
        reads = [in_.buf]
        writes = [out.buf]
        kw = {}
        if bias is not None:
            if isinstance(bias, V):
                reads.append(bias.buf)
                kw["bias"] = bias.ap
            else:
                kw["bias"] = bias
        if scale is not None:
            if isinstance(scale, V):
                reads.append(scale.buf)
                kw["scale"] = scale.ap
            else:
                kw["scale"] = scale
        if accum is not None:
            writes.append(accum.buf)
            kw["accum_out"] = accum.ap
        return self.act.issue(
            lambda: self.nc.scalar.activation(out=out.ap, in_=in_.ap, func=func, **kw), reads, writes)

    def _veng(self, eng):
        return (self.dve, self.nc.vector) if eng == "dve" else (self.pool, self.nc.gpsimd)

    def ts(self, eng, out, in0, s1, s2, op0, op1=None, accum=None):
        E, e = self._veng(eng)
        reads = [in0.buf]
        writes = [out.buf]
        a1 = s1
        a2 = s2
        if isinstance(s1, V):
            reads.append(s1.buf)
            a1 = s1.ap
        if isinstance(s2, V):
            reads.append(s2.buf)
            a2 = s2.ap
        kw = {}
        if op1 is not None:
            kw["op1"] = op1
        if accum is not None:
            writes.append(accum.buf)
            kw["accum_out"] = accum.ap
        return E.issue(lambda: e.tensor_scalar(out=out.ap, in0=in0.ap, scalar1=a1, scalar2=a2, op0=op0, **kw),
                       reads, writes)

    def tt(self, eng, out, in0, in1, op):
        E, e = self._veng(eng)
        return E.issue(lambda: e.tensor_tensor(out=out.ap, in0=in0.ap, in1=in1.ap, op=op),
                       [in0.buf, in1.buf], [out.buf])

    def stt(self, out, in0, scalar, in1, op0, op1):
        reads = [in0.buf, in1.buf]
        a = scalar
        if isinstance(scalar, V):
            reads.append(scalar.buf)
            a = scalar.ap
        return self.dve.issue(
            lambda: self.nc.vector.scalar_tensor_tensor(out=out.ap, in0=in0.ap, scalar=a, in1=in1.ap,
                                                        op0=op0, op1=op1), reads, [out.buf])

    def copy(self, eng, out, in_):
        if eng == "act":
            return self.actf(out, in_, AF.Copy)
        E, e = self._veng(eng)
        return E.issue(lambda: e.tensor_copy(out=out.ap, in_=in_.ap), [in_.buf], [out.buf])

    def recip(self, out, in_):
        return self.dve.issue(lambda: self.nc.vector.reciprocal(out=out.ap, in_=in_.ap), [in_.buf], [out.buf])

    def memset(self, eng, out, val):
        E, e = self._veng(eng)
        return E.issue(lambda: e.memset(out.ap, val), [], [out.buf])


class Stats:
    def __init__(self, c, es, n=64):
        self.t = c.sb(es, [128, n], F32)
        self.n = n
        self.i = 0

    def get(self):
        k = self.i % self.n
        self.i += 1
        return self.t.c(k)[:, k:k + 1]


def rstd_chain(c, st, ss, n):
    a = st.get()
    c.ts("dve", a, ss, 1.0 / n, EPS, ALU.mult, ALU.add)
    b = st.get()
    c.actf(b, a, AF.Sqrt)
    r = st.get()
    c.recip(r, b)
    return r


def ffn_stage(c, P, L, j, hsrc, hdst):
    nc = c.nc
    pre_i, post_i = (0, 1) if j == 0 else (4, 5)
    with ExitStack() as es:
        Wg = c.sb(es, [128, 8, DFF], BF16, "Wg")
        Wu = c.sb(es, [128, 8, DFF], BF16, "Wu")
        Wd = c.sb(es, [128, 22, D], BF16, "Wd")
        gpre = c.sb(es, [128, D], F32, "gpre")
        gpost = c.sb(es, [128, D], F32, "gpost")
        hl = [c.sb(es, [128, D], F32, "hl%d" % i) for i in range(2)]
        hr = [c.sb(es, [128, D], F32, "hr%d" % i) for i in range(2)]
        xn = [c.sb(es, [128, D], BF16, "xn%d" % i) for i in range(4)]
        xT = [c.sb(es, [128, 8, 512], BF16, "xT%d" % i) for i in range(1)]
        actb = c.sb(es, [128, 22, 512], BF16, "actb")
        sg = [c.sb(es, [128, 512], F32, "sg%d" % i) for i in range(1)]
        tmp = [c.sb(es, [128, D], F32, "tmp%d" % i) for i in range(1)]
        junk = c.sb(es, [128, D], BF16, "junk")
        st = Stats(c, es, 64)
        pT = c.ps(es, [128, 8, 128], BF16, "pT")
        pG = [c.ps(es, [128, 512], F32, "pG%d" % i) for i in range(2)]
        pU = [c.ps(es, [128, 512], F32, "pU%d" % i) for i in range(2)]
        pD = [c.ps(es, [128, 512], F32, "pD%d" % i) for i in range(3)]

        gsrc = P["ffn_w_gate"].h[L, j].rearrange("(k p) f -> p k f", p=128)
        usrc = P["ffn_w_up"].h[L, j].rearrange("(k p) f -> p k f", p=128)
        dsrc = P["ffn_w_down"].h[L, j].rearrange("(k p) f -> p k f", p=128)
        for k in range(8):
            c.pool.dma(Wg[:, k, :], V(gsrc[:, k, :], P["ffn_w_gate"].buf), share=(k > 0))
        for k in range(8):
            c.pool.dma(Wu[:, k, :], V(usrc[:, k, :], P["ffn_w_up"].buf), share=(k > 0))
        for k in range(22):
            c.pool.dma(Wd[:, k, :], V(dsrc[:, k, :], P["ffn_w_down"].buf), share=(k > 0))
        c.sp.dma(gpre[:], V(P["norm_w"].h[L, pre_i, :].partition_broadcast(128), P["norm_w"].buf))
        c.sp.dma(gpost[:], V(P["norm_w"].h[L, post_i, :].partition_broadcast(128), P["norm_w"].buf))
        c.ts("dve", gpost[:], gpost[:], 0.5, None, ALU.mult)

        ident = P["ident"]
        hts = {}

        def norm(g):
            for s in range(4):
                ti = g * 4 + s
                ht = hl[ti % 2]
                c.sp.dma(ht[:], hsrc[ti][:, :])
                ss = st.get()
                c.actf(junk[:], ht[:], AF.Square, accum=ss)
                r = rstd_chain(c, st, ss, D)
                x = xn[ti % 4]
                c.stt(x[:], ht[:], r, gpre[:], ALU.mult, ALU.mult)

        def transp(g):
            for s in range(4):
                ti = g * 4 + s
                x = xn[ti % 4]
                for k in range(8):
                    c.tr(pT[:, k, :], x[:, k * 128:(k + 1) * 128], ident[:], last=(k == 7))
                c.copy("act", xT[0].c(s)[:, :, s * 128:(s + 1) * 128], pT[:])

        def xTv(g, k):
            return [xT[0].c(s).buf for s in range(4)]

        def gateup(g, hook):
            xt = xT[0]
            for f in range(22):
                pg = pG[f % 2]
                pu = pU[f % 2]
                for (W, pp) in ((Wg, pg), (Wu, pu)):
                    for k in range(8):
                        rhs = V(xt.h[:, k, :], xt.c(0).buf)
                        ins = c.pe.issue(
                            lambda W=W, pp=pp, k=k, rhs=rhs: nc.tensor.matmul(
                                out=pp.h[:], lhsT=W.h[:, k, f * 128:(f + 1) * 128], rhs=rhs.ap,
                                start=(k == 0), stop=(k == 7)),
                            [W.buf] + xTv(g, k), [pp.buf], inc=(k == 7))
                c.actf(sg[0][:], pg[:], AF.Silu)
                c.tt("dve", actb.c(f)[:, f, :], sg[0][:], pu[:], ALU.mult)
                if f == 8 and hook is not None:
                    hook()

        dcount = [0]

        def down(g):
            for s in range(4):
                ti = g * 4 + s
                ht = hr[ti % 2]
                c.sp.dma(ht[:], hsrc[ti][:, :])
                banks = []
                sss = []
                for half in range(2):
                    pd = pD[dcount[0] % 3]
                    dcount[0] += 1
                    banks.append(pd)
                    for f in range(22):
                        c.mm(pd[:], actb.c(f)[:, f, s * 128:(s + 1) * 128], Wd[:, f, half * 512:(half + 1) * 512],
                             start=(f == 0), stop=(f == 21))
                    ssh = st.get()
                    c.actf(junk[:, 0:512], pd[:], AF.Square, accum=ssh)
                    sss.append(ssh)
                ss = st.get()
                c.tt("dve", ss, sss[0], sss[1], ALU.add)
                r = rstd_chain(c, st, ss, D)
                tm = tmp[0]
                for half in range(2):
                    c.stt(tm[:, half * 512:(half + 1) * 512], banks[half][:], r,
                          gpost[:, half * 512:(half + 1) * 512], ALU.mult, ALU.mult)
                c.tt("pool", ht[:], ht[:], tm[:], ALU.add)
                c.sp.dma(hdst[ti][:, :], ht[:])

        NG = NT // 4
        norm(0)
        transp(0)
        for g in range(NG):
            nxt = (lambda g=g: norm(g + 1)) if g + 1 < NG else None
            gateup(g, nxt)
            if g + 1 < NG:
                transp(g + 1)
            down(g)
        c.barrier()


def load_bcast(c, es, src_ap, srcbuf, n, name):
    t = c.sb(es, [128, n], F32, name)
    c.sp.dma(t[:], V(src_ap.partition_broadcast(128), srcbuf))
    return t


def front_norm_T(c, st, ht, gain, xn, junk, pT, uT, ident):
    ss = st.get()
    c.actf(junk[:], ht[:], AF.Square, accum=ss)
    r = rstd_chain(c, st, ss, D)
    c.stt(xn[:], ht[:], r, gain[:], ALU.mult, ALU.mult)
    for k in range(8):
        c.tr(pT[:, k, :], xn[:, k * 128:(k + 1) * 128], ident[:], last=(k == 7))
    c.copy("act", uT[:], pT[:])


def outproj_stage(c, P, L, ysrc, wname, i, K, hsrc, hdst, fm=None):
    KC = K // 128
    with ExitStack() as es:
        W = c.sb(es, [128, KC, D], BF16, "Wo")
        wsrc = P[wname].h[i].rearrange("(k p) f -> p k f", p=128)
        for k in range(KC):
            c.pool.dma(W[:, k, :], V(wsrc[:, k, :], P[wname].buf), share=(k > 0))
        g3 = load_bcast(c, es, P["norm_w"].h[L, 3, :], P["norm_w"].buf, D, "g3")
        yt = [c.sb(es, [128, K], BF16, "yt") for _ in range(2)]
        yT = [c.sb(es, [128, KC, 128], BF16, "yT") for _ in range(2)]
        hr = [c.sb(es, [128, D], F32, "hr") for _ in range(2)]
        tmp = c.sb(es, [128, D], F32, "tmp")
        junk = c.sb(es, [128, 512], BF16, "junk")
        st = Stats(c, es, 32)
        pT = [c.ps(es, [128, 8, 128], BF16, "pT") for _ in range(2)]
        pD = [c.ps(es, [128, 512], F32, "pD") for _ in range(4)]
        ident = P["ident"]
        dc = 0
        for ti in range(NT):
            y = yt[ti % 2]
            h = hr[ti % 2]
            c.sp.dma(h[:], hsrc[ti][:, :])
            yt_T = yT[ti % 2]
            if fm is not None:
                c.sp.dma(yt_T[:], V(fm.h[:, ti * 128:(ti + 1) * 128].rearrange("(k p) t -> p k t", p=128), fm.buf))
            else:
                c.sp.dma(y[:], ysrc[ti][:, :])
            for kb in range(KC // 8 if fm is None else 0):
                p = pT[kb % 2]
                for k in range(8):
                    kk = kb * 8 + k
                    c.tr(p[:, k, :], y[:, kk * 128:(kk + 1) * 128], ident[:], last=(k == 7))
                c.copy("act", yt_T[:, kb * 8:(kb + 1) * 8, :], p[:])
            banks = []
            sss = []
            for half in range(2):
                pd = pD[dc % 4]
                dc += 1
                banks.append(pd)
                for k in range(KC):
                    c.mm(pd[:], yt_T[:, k, :], W[:, k, half * 512:(half + 1) * 512], start=(k == 0), stop=(k == KC - 1))
                ssh = st.get()
                c.actf(junk[:], pd[:], AF.Square, accum=ssh)
                sss.append(ssh)
            ss = st.get()
            c.tt("dve", ss, sss[0], sss[1], ALU.add)
            r = rstd_chain(c, st, ss, D)
            for half in range(2):
                c.stt(tmp[:, half * 512:(half + 1) * 512], banks[half][:], r, g3[:, half * 512:(half + 1) * 512],
                      ALU.mult, ALU.mult)
            c.tt("pool", h[:], h[:], tmp[:], ALU.add)
            c.sp.dma(hdst[ti][:, :], h[:])
        c.barrier()


def small_T(c, es, rows_src, nrow, ncol, P, name):
    nch = ncol // 128
    rowt = c.sb(es, [nrow, ncol], F32, name + "r")
    for j, (ap, buf) in enumerate(rows_src):
        c.sp.dma(rowt[j:j + 1, :], V(ap.partition_broadcast(1), buf))
    pt = c.ps(es, [128, nch, nrow], F32, name + "p")
    for ch in range(nch):
        c.tr(pt[:, ch, :], rowt[0:nrow, ch * 128:(ch + 1) * 128], P["identf"][0:nrow, 0:nrow], last=(ch == nch - 1))
    out = c.sb(es, [128, nch, nrow], F32, name)
    c.copy("dve", out[:], pt[:])
    return out


def ssd_stage(c, P, L, hsrc, ydst):
    nc = c.nc
    i = L // 2
    NX = 5152
    with ExitStack() as es0:
        cw = c.sb(es0, [128, 24, 5], F32, "cwk")
        with ExitStack() as es1:
            rows = [(P["odd_conv_w"].h[i, k, :], P["odd_conv_w"].buf) for k in range(4)]
            rows.append((P["odd_conv_b"].h[i, :], P["odd_conv_b"].buf))
            cw_tmp = small_T(c, es1, rows, 5, 3072, P, "cw")
            c.copy("dve", cw[:], cw_tmp[:])
            c.barrier()
        es = es0
        W = c.sb(es, [128, 8, NX], BF16, "Win")
        wsrc = P["odd_w_in"].h[i].rearrange("(k p) f -> p k f", p=128)
        for k in range(8):
            c.pool.dma(W[:, k, :], V(wsrc[:, k, :], P["odd_w_in"].buf), share=(k > 0))
        g2 = load_bcast(c, es, P["norm_w"].h[L, 2, :], P["norm_w"].buf, D, "g2")
        dtb = load_bcast(c, es, P["odd_dt_bias"].h[i, :], P["odd_dt_bias"].buf, 32, "dtb")
        negA = load_bcast(c, es, P["odd_a_log"].h[i, :], P["odd_a_log"].buf, 32, "negA")
        dsk = load_bcast(c, es, P["odd_d_skip"].h[i, :], P["odd_d_skip"].buf, 32, "dsk")
        onw = load_bcast(c, es, P["odd_out_norm_w"].h[i, :], P["odd_out_norm_w"].buf, 2048, "onw")
        c.actf(negA[:], negA[:], AF.Exp)
        c.ts("dve", negA[:], negA[:], -1.0, None, ALU.mult)

        ident, identf, triU, ones, maskb = P["ident"], P["identf"], P["triU"], P["ones"], P["maskb"]
        hl = [c.sb(es, [128, D], F32, "hl") for _ in range(1)]
        xn = c.sb(es, [128, D], BF16, "xn")
        junk = xn
        uT = [c.sb(es, [128, 8, 128], BF16, "uT") for _ in range(1)]
        st = Stats(c, es, 64)
        pcb = c.sb(es, [128, 24, 131], BF16, "pcb")
        dgw = c.sb(es, [128, 24, 4, 128], BF16, "dgw")
        for k in range(4):
            c.tt("pool", dgw[:, :, k, :], V(ident.h[:, :].unsqueeze(1).broadcast_to([128, 24, 128]), ident.buf),
                 V(cw.h[:, :, k:k + 1].broadcast_to([128, 24, 128]), cw.buf), ALU.mult)
        xc = c.sb(es, [128, 24, 128], BF16, "xc")
        xtm = c.sb(es, [128, 32, 64], BF16, "xtm")
        Btm = c.sb(es, [128, 4, 128], BF16, "Btm")
        xdt = c.sb(es, [128, 32, 64], BF16, "xdt")
        xdec = c.sb(es, [128, 32, 64], BF16, "xdec")
        xD = c.sb(es, [128, 32, 64], BF16, "xD")
        sm = c.sb(es, [128, 12, 32], F32, "sm")
        R1 = [c.sb(es, [128, 8, 128], F32, "R1") for _ in range(1)]
        segT = [c.sb(es, [128, 8, 128], BF16, "segT") for _ in range(4)]
        cbs4 = c.sb(es, [128, 4, 128], BF16, "cbs4")
        MT = [c.sb(es, [128, 8, 128], BF16, "MT") for _ in range(4)]
        tb = [c.sb(es, [128, 8, 64], F32, "tb") for _ in range(1)]
        yb = [c.sb(es, [128, 512], F32, "yb") for _ in range(4)]

        yn = [c.sb(es, [128, 512], BF16, "yn") for _ in range(2)]
        S = c.sb(es, [128, 32, 64], F32, "S")
        Sb = c.sb(es, [128, 32, 64], BF16, "Sb")
        for g in range(4):
            c.memset("dve", S.c(g)[:, g * 8:(g + 1) * 8, :], 0.0)
            c.memset("dve", Sb.c(g)[:, g * 8:(g + 1) * 8, :], 0.0)
        for ch in range(24):
            c.memset("pool", pcb.c(ch)[:, ch, :], 0.0)

        pT = c.ps(es, [128, 8, 128], BF16, "pT")
        pA = [c.ps(es, [128, 512], F32, "pA") for _ in range(7)]
        pai = [0]

        def bank():
            b = pA[pai[0] % 7]
            pai[0] += 1
            return b

        def smv(j):
            return sm.c(j)[:, j, :]

        for ti in range(NT):
            ht = hl[0]
            c.sp.dma(ht[:], hsrc[ti][:, :])
            u = uT[0]
            front_norm_T(c, st, ht, g2, xn, junk, pT, u, ident)

            pdt = bank()
            for k in range(8):
                c.mm(pdt[:, 0:32], u[:, k, :], W[:, k, 5120:5152], start=(k == 0), stop=(k == 7))
            dtr, ex, dt, a, nacum, eacum, tot, edarg, edec, etot = [smv(j) for j in range(10)]
            c.tt("dve", dtr, pdt[:, 0:32], dtb[:], ALU.add)
            c.actf(ex, dtr, AF.Exp)
            c.actf(dt, ex, AF.Ln, bias=1.0)
            c.tt("dve", a, dt, negA[:], ALU.mult)
            pac = bank()
            c.mm(pac[:, 0:32], triU[:], a, start=True, stop=True)
            c.mm(pac[:, 32:64], ones[:], a, start=True, stop=True)
            c.actf(nacum, pac[:, 0:32], AF.Copy, scale=-1.0)
            c.actf(eacum, pac[:, 0:32], AF.Exp)
            c.actf(tot, pac[:, 32:64], AF.Copy)
            c.tt("dve", edarg, tot, nacum, ALU.add)
            c.actf(edec, edarg, AF.Exp)
            c.actf(etot, tot, AF.Exp)

            pps = {}

            def cv_in(ch):
                pp = bank()
                pps[ch] = pp
                col = 2048 + ch * 128
                for k in range(8):
                    c.mm(pp[:, 0:128], W[:, k, col:col + 128], u[:, k, :], start=(k == 0), stop=(k == 7))
                c.copy("act" if ch % 2 == 0 else "dve", pcb.c(ch)[:, ch, 3:131], pp[:, 0:128])

            def cv_out(ch):
                pp = pps.pop(ch)
                pc = pcb.c(ch)
                for kk in range(4):
                    c.mm(pp[:, 128:256], dgw[:, ch, kk, :], pc[:, ch, kk:kk + 128], start=(kk == 0), stop=(kk == 3))
                c.copy("pool", pc[:, ch, 0:3], pc[:, ch, 128:131])
                c.actf(xc.c(ch)[:, ch, :], pp[:, 128:256], AF.Silu, bias=cw[:, ch, 4:5])

            LAG = 3
            for ch in range(24 + LAG):
                if ch < 24:
                    cv_in(ch)
                if ch >= LAG:
                    cv_out(ch - LAG)

            for kb in range(2):
                for k in range(8):
                    ch = kb * 8 + k
                    c.tr(pT[:, k, :], xc.c(ch)[:, ch, :], ident[:], last=(k == 7))
                c.copy("act", xtm[:, kb * 16:(kb + 1) * 16, :], pT[:].rearrange("p k (a b) -> p (k a) b", a=2))
            for g in range(4):
                c.tr(pT[:, g, :], xc.c(16 + g)[:, 16 + g, :], ident[:], last=(g == 3))
            c.copy("act", Btm[:], pT[:, 0:4, :])
            bc = lambda v: V(v.ap.unsqueeze(2).broadcast_to([128, 32, 64]), v.buf)
            c.tt("dve", xdt[:], xtm[:], bc(dt), ALU.mult)
            c.tt("pool", xdec[:], xdt[:], bc(edec), ALU.mult)
            c.tt("pool", xD[:], xtm[:], V(dsk.h[:, :].unsqueeze(2).broadcast_to([128, 32, 64]), dsk.buf), ALU.mult)

            zs4 = T(xtm.h, xtm.buf)
            zs4v = lambda g: V(xtm.h[:, g * 8:(g + 1) * 8, :].rearrange("p r d -> p (r d)"), xtm.buf)
            HS = [slice(g * 8, (g + 1) * 8) for g in range(4)]
            bcr = lambda v, hs: V(v.ap[:, hs].unsqueeze(2).broadcast_to([128, 8, 64]), v.buf)
            for g in range(4):
                pz = bank()
                for k in range(8):
                    c.mm(pz[:], u[:, k, :], W[:, k, g * 512:(g + 1) * 512], start=(k == 0), stop=(k == 7))
                c.actf(zs4v(g), pz[:], AF.Silu)
            pcbk = bank()
            for g in range(4):
                c.mm(pcbk[:, g * 128:(g + 1) * 128], xc.c(16 + g)[:, 16 + g, :], xc.c(20 + g)[:, 20 + g, :], start=True,
                     stop=True, last=(g == 3))
            c.copy("dve", cbs4[:], pcbk[:].rearrange("p (g l) -> p g l", g=4))
            if CUT <= 2:
                continue
            for g in range(4):
                r1 = R1[0]
                c.tt("pool", r1[:], V(a.ap[:, HS[g]].unsqueeze(2).broadcast_to([128, 8, 128]), a.buf),
                     V(triU.h[:, :].unsqueeze(1).broadcast_to([128, 8, 128]), triU.buf), ALU.mult)
                for half in range(2):
                    ps_ = bank()
                    c.mm(ps_[:], ones[:], r1[:, half * 4:(half + 1) * 4, :], start=True, stop=False, last=False)
                    c.mm(ps_[:], ident[:], maskb[:], start=False, stop=True)
                    for rr in range(4):
                        r = half * 4 + rr
                        hcol = g * 8 + r
                        c.actf(segT[g][:, r, :], ps_[:, rr * 128:(rr + 1) * 128], AF.Exp,
                               bias=V(nacum.ap[:, hcol:hcol + 1], nacum.buf))
            for g in range(4):
                c.tt("dve", MT[g][:], segT[g][:], V(cbs4.h[:, g:g + 1, :].broadcast_to([128, 8, 128]), cbs4.buf), ALU.mult)
            if CUT <= 3:
                continue
            for g in range(4):
                hs = HS[g]
                pY1 = bank()
                c.mm(pY1[:], ident[:], xD[:, hs, :], start=True, stop=False, last=False)
                for r in range(8):
                    c.mm(pY1[:, r * 64:(r + 1) * 64], MT[g][:, r, :], xdt[:, g * 8 + r, :], start=False, stop=(r == 7),
                         last=(r == 7))
                pY2 = bank()
                c.mm(pY2[:], xc.c(20 + g)[:, 20 + g, :], Sb.c(g)[:, hs, :], start=True, stop=True)
                t_ = tb[0]
                c.tt("dve", t_[:], pY2[:].rearrange("p (r d) -> p r d", r=8), bcr(eacum, hs), ALU.mult)
                c.tt("dve", yb[g][:], t_[:].rearrange("p r d -> p (r d)"), pY1[:], ALU.add)
            if CUT <= 4:
                continue
            pSs = []
            for g in range(4):
                pS = bank()
                pSs.append(pS)
                c.mm(pS[:], Btm[:, g, :], xdec[:, HS[g], :], start=True, stop=True)
                Sg = S.c(g)
                c.tt("pool", Sg[:, HS[g], :], Sg[:, HS[g], :], bcr(etot, HS[g]), ALU.mult)
            for g in range(4):
                Sg = S.c(g)
                c.tt("dve", Sg[:, HS[g], :], Sg[:, HS[g], :], pSs[g][:].rearrange("p (r d) -> p r d", r=8), ALU.add)
                c.copy("act", Sb.c(g)[:, HS[g], :], Sg[:, HS[g], :])
            if CUT <= 5:
                continue
            sss = []
            for g in range(4):
                c.tt("pool", yb[g][:], yb[g][:], zs4v(g), ALU.mult)
                ss = st.get()
                sss.append(ss)
                c.actf(junk[:, 0:512], yb[g][:], AF.Square, accum=ss)
            aa = []
            for g in range(4):
                a_ = st.get()
                c.ts("dve", a_, sss[g], 1.0 / 512, EPS, ALU.mult, ALU.add)
                aa.append(a_)
            bb = []
            for g in range(4):
                b_ = st.get()
                c.actf(b_, aa[g], AF.Sqrt)
                bb.append(b_)
            for g in range(4):
                r_ = st.get()
                c.recip(r_, bb[g])
                ynt = yn[g % 2]
                c.stt(ynt[:], yb[g][:], r_, onw[:, g * 512:(g + 1) * 512], ALU.mult, ALU.mult)
                c.sp.dma(V(ydst[ti].h[:, g * 512:(g + 1) * 512], ydst[ti].buf), ynt[:])
        c.barrier()


def even_stage(c, P, L, hsrc, omT, vatt):
    nc = c.nc
    i = L // 2
    NX = 3592
    ident, identf, triU, ones, maskb = P["ident"], P["identf"], P["triU"], P["ones"], P["maskb"]
    negones, maskus, onesb, maskp, sel = P["negones"], P["maskus"], P["onesb"], P["maskp"], P["sel"]
    SCALE_B = 128.0 ** -0.5
    with ExitStack() as esq:
        qkd = P["qkd"]
        with ExitStack() as es:
            cw = c.sb(es, [128, 12, 4], F32, "cwk")
            with ExitStack() as es1:
                rows = [(P["even_conv_w"].h[i, k, :], P["even_conv_w"].buf) for k in range(4)]
                cw_tmp = small_T(c, es1, rows, 4, 1536, P, "cw")
                c.copy("dve", cw[:], cw_tmp[:])
                c.barrier()
            W = c.sb(es, [128, 8, NX], BF16, "Win")
            wsrc = P["even_w_in"].h[i].rearrange("(k p) f -> p k f", p=128)
            for k in range(8):
                c.pool.dma(W[:, k, :], V(wsrc[:, k, :], P["even_w_in"].buf), share=(k > 0))
            g2 = load_bcast(c, es, P["norm_w"].h[L, 2, :], P["norm_w"].buf, D, "g2")
            dtb = load_bcast(c, es, P["even_dt_bias"].h[i, :], P["even_dt_bias"].buf, 4, "dtb")
            negA = load_bcast(c, es, P["even_a_log"].h[i, :], P["even_a_log"].buf, 4, "negA")
            hnw = load_bcast(c, es, P["even_head_norm_w"].h[i, :], P["even_head_norm_w"].buf, 128, "hnw")
            c.actf(negA[:], negA[:], AF.Exp)
            c.ts("dve", negA[:], negA[:], -1.0, None, ALU.mult)

            hl = [c.sb(es, [128, D], F32, "hl") for _ in range(2)]
            xn = c.sb(es, [128, D], BF16, "xn")
            junk = c.sb(es, [128, D], BF16, "junk")
            uT = [c.sb(es, [128, 8, 128], BF16, "uT") for _ in range(2)]
            st = Stats(c, es, 64)
            vt = [c.sb(es, [128, 512], BF16, "vt") for _ in range(2)]
            qkt = [c.sb(es, [128, 8, 128], BF16, "qkt") for _ in range(2)]
            zs = c.sb(es, [128, 512], F32, "zs")
            sm = c.sb(es, [128, 48, 4], F32, "sm")
            smi = [0]

            def smv():
                j = smi[0] % 32
                smi[0] += 1
                return sm.c(j)[:, j, :]

            pcb = c.sb(es, [128, 12, 131], BF16, "pcb")
            dgw = c.sb(es, [128, 12, 4, 128], BF16, "dgw")
            for k in range(4):
                c.tt("pool", dgw[:, :, k, :], V(ident.h[:, :].unsqueeze(1).broadcast_to([128, 12, 128]), ident.buf),
                     V(cw.h[:, :, k:k + 1].broadcast_to([128, 12, 128]), cw.buf), ALU.mult)
            xg = c.sb(es, [128, 12, 128], F32, "xg")
            xtm = c.sb(es, [128, 12, 128], F32, "xtm")
            dg = [c.sb(es, [128, 4, 128], F32, "dg") for _ in range(2)]
            knT = c.sb(es, [128, 4, 128], F32, "knT")
            kbT = c.sb(es, [128, 4, 128], F32, "kbT")
            qnT = c.sb(es, [128, 4, 128], F32, "qnT")
            qdT = c.sb(es, [128, 4, 128], F32, "qdT")
            kbg = c.sb(es, [128, 4, 128], F32, "kbg")
            kdec = c.sb(es, [128, 4, 128], F32, "kdec")
            vb = c.sb(es, [128, 4, 128], F32, "vb")
            R1 = c.sb(es, [128, 4, 128], F32, "R1")
            segT = c.sb(es, [128, 4, 128], F32, "segT")
            segU = c.sb(es, [128, 4, 128], F32, "segU")
            qkT = c.sb(es, [128, 4, 128], F32, "qkT")
            Am = [c.sb(es, [128, 4, 128], F32, "Am") for _ in range(2)]
            Bm = [c.sb(es, [128, 4, 128], F32, "Bm") for _ in range(2)]
            Xm = [c.sb(es, [128, 4, 128], F32, "Xm") for _ in range(2)]
            nwT = c.sb(es, [128, 4, 128], F32, "nwT")
            vn = c.sb(es, [128, 4, 128], F32, "vn")
            S = c.sb(es, [128, 4, 128], F32, "S")
            on = c.sb(es, [128, 4, 128], F32, "on")
            ob = c.sb(es, [128, 4, 128], BF16, "ob")
            obT = [c.sb(es, [128, 4, 128], BF16, "obT") for _ in range(2)]
            c.memset("dve", S[:], 0.0)
            for ch in range(12):
                c.memset("pool", pcb.c(ch)[:, ch, :], 0.0)

            pT = c.ps(es, [128, 8, 128], BF16, "pT")
            pA = [c.ps(es, [128, 512], F32, "pA") for _ in range(7)]
            pai = [0]

            def bank():
                b = pA[pai[0] % 7]
                pai[0] += 1
                return b

            def trf(dst3, src_fn, n):
                pb_ = bank()
                for j in range(n):
                    c.tr(pb_[:, j * 128:(j + 1) * 128], src_fn(j), identf[:], last=(j == n - 1))
                return pb_

            def b4(v):
                return v.rearrange("p (h d) -> p h d", h=4)

            def bcl(v, n=128):
                return V(v.ap.unsqueeze(2).broadcast_to([128, 4, n]), v.buf)

            def bcm(t):
                return V(t.h[:, :].unsqueeze(1).broadcast_to([128, 4, 128]), t.buf)

            for ti in range(min(NT, NT_LIM)):
                tsl = slice(ti * 128, (ti + 1) * 128)
                ht = hl[ti % 2]
                c.sp.dma(ht[:], hsrc[ti][:, :])
                u = uT[ti % 2]
                front_norm_T(c, st, ht, g2, xn, junk, pT, u, ident)

                for cch in range(8):
                    pp = bank()
                    for k in range(8):
                        c.mm(pp[:, 0:128], W[:, k, cch * 128:(cch + 1) * 128], u[:, k, :], start=(k == 0), stop=(k == 7))
                    c.copy("act" if cch % 2 == 0 else "dve", qkt[ti % 2][:, cch, :], pp[:, 0:128])
                c.sp.dma(V(qkd.h[:, :, tsl], qkd.buf), qkt[ti % 2][:])
                pv = bank()
                for k in range(8):
                    c.mm(pv[:], u[:, k, :], W[:, k, 1024:1536], start=(k == 0), stop=(k == 7))
                v_ = vt[ti % 2]
                c.copy("dve", v_[:], pv[:])
                c.sp.dma(vatt[ti][:, :], v_[:])
                if CUT <= 1:
                    continue
                pz = bank()
                for k in range(8):
                    c.mm(pz[:], u[:, k, :], W[:, k, 3072:3584], start=(k == 0), stop=(k == 7))
                c.actf(zs[:], pz[:], AF.Silu)
                if CUT <= 2:
                    continue
                pba = bank()
                for k in range(8):
                    c.mm(pba[:, 0:8], u[:, k, :], W[:, k, 3584:3592], start=(k == 0), stop=(k == 7))
                beta, spi, ex, spv, g, nacum, acum, egc, tot, edarg, edec, etot = [smv() for _ in range(12)]
                c.actf(beta, pba[:, 0:4], AF.Sigmoid)
                c.tt("dve", spi, pba[:, 4:8], dtb[:], ALU.add)
                c.actf(ex, spi, AF.Exp)
                c.actf(spv, ex, AF.Ln, bias=1.0)
                c.tt("dve", g, spv, negA[:], ALU.mult)
                pac = bank()
                c.mm(pac[:, 0:4], triU[:], g, start=True, stop=True)
                c.mm(pac[:, 4:8], ones[:], g, start=True, stop=True)
                c.actf(nacum, pac[:, 0:4], AF.Copy, scale=-1.0)
                c.actf(acum, pac[:, 0:4], AF.Copy)
                c.actf(egc, pac[:, 0:4], AF.Exp)
                c.actf(tot, pac[:, 4:8], AF.Copy)
                c.tt("dve", edarg, tot, nacum, ALU.add)
                c.actf(edec, edarg, AF.Exp)
                c.actf(etot, tot, AF.Exp)

                if CUT <= 3:
                    continue
                pps = {}

                def cv_in(ch):
                    pp = bank()
                    pps[ch] = pp
                    col = 1536 + ch * 128
                    for k in range(8):
                        c.mm(pp[:, 0:128], W[:, k, col:col + 128], u[:, k, :], start=(k == 0), stop=(k == 7))
                    c.copy("act" if ch % 2 == 0 else "dve", pcb.c(ch)[:, ch, 3:131], pp[:, 0:128])

                def cv_out(ch):
                    pp = pps.pop(ch)
                    pc = pcb.c(ch)
                    for kk in range(4):
                        c.mm(pp[:, 128:256], dgw[:, ch, kk, :], pc[:, ch, kk:kk + 128], start=(kk == 0), stop=(kk == 3))
                    c.copy("pool", pc[:, ch, 0:3], pc[:, ch, 128:131])
                    c.actf(xg.c(ch)[:, ch, :], pp[:, 128:256], AF.Silu)

                LAG = 3
                for ch in range(12 + LAG):
                    if ch < 12:
                        cv_in(ch)
                    if ch >= LAG:
                        cv_out(ch - LAG)
                if CUT <= 4:
                    continue
                for q3 in range(3):
                    pq = trf(None, lambda j: xg.c(q3 * 4 + j)[:, q3 * 4 + j, :], 4)
                    c.copy("act" if q3 != 1 else "dve", xtm[:, q3 * 4:(q3 + 1) * 4, :], b4(pq[:]))
                if CUT <= 5:
                    continue
                ssq = sm.c("ssq")
                ssqk = [V(sm.h[:, 40 + (j // 4), (j % 4):(j % 4) + 1], ssq.buf) for j in range(8)]
                for j in range(8):
                    c.actf(junk[:, 0:128], xtm[:, j, :], AF.Square, accum=ssqk[j])
                ssv = V(sm.h[:, 40:42, :], ssq.buf)
                rn0 = V(sm.h[:, 42:44, :], sm.c("rn0").buf)
                rn1 = V(sm.h[:, 44:46, :], sm.c("rn1").buf)
                rn = V(sm.h[:, 46:48, :], sm.c("rn").buf)
                c.ts("dve", rn0, ssv, EPS, None, ALU.add)
                c.actf(rn1, rn0, AF.Sqrt)
                c.recip(rn, rn1)
                rq = V(sm.h[:, 46, :], rn.buf)
                rk = V(sm.h[:, 47, :], rn.buf)
                s_kb, s_qn, s_qd, s_kbg, s_kdec = [smv() for _ in range(5)]
                c.tt("dve", s_kb, rk, beta, ALU.mult)
                c.ts("dve", s_qn, rq, SCALE_B, None, ALU.mult)
                c.tt("dve", s_qd, s_qn, egc, ALU.mult)
                c.tt("dve", s_kbg, s_kb, egc, ALU.mult)
                c.tt("dve", s_kdec, rk, edec, ALU.mult)
                if CUT <= 6:
                    continue
                for qi, (sc, src0, dstT) in enumerate(((rk, 4, knT), (s_kb, 4, kbT), (s_qn, 0, qnT), (s_qd, 0, qdT))):
                    d_ = dg[qi % 2]
                    c.tt("pool", d_[:], bcm(identf), bcl(sc), ALU.mult)
                    pb_ = bank()
                    c.mm(pb_[:], ones[:], d_[:], start=True, stop=True)
                    srcv = V(xg.h[:, src0:src0 + 4, :], xg.c(src0).buf)
                    E = c.dve
                    E.issue(lambda: nc.vector.tensor_tensor(out=dstT.h[:], in0=srcv.ap, in1=b4(pb_[:]).ap, op=ALU.mult),
                            [xg.c(src0 + j).buf for j in range(4)] + [pb_.buf], [dstT.buf])
                if CUT <= 7:
                    continue
                c.tt("pool", kbg[:], xtm[:, 4:8, :], bcl(s_kbg), ALU.mult)
                c.tt("pool", kdec[:], xtm[:, 4:8, :], bcl(s_kdec), ALU.mult)
                c.tt("pool", vb[:], xtm[:, 8:12, :], bcl(beta), ALU.mult)
                if CUT <= 8:
                    continue
                c.tt("pool", R1[:], bcl(g), bcm(triU), ALU.mult)
                pL = bank()
                c.mm(pL[:], ones[:], R1[:], start=True, stop=False, last=False)
                c.mm(pL[:], ident[:], maskb[:], start=False, stop=True)
                pU = bank()
                c.mm(pU[:], negones[:], R1[:], start=True, stop=False, last=False)
                c.mm(pU[:], ident[:], maskus[:], start=False, stop=True)
                for h in range(4):
                    c.actf(segT[:, h, :], pL[:, h * 128:(h + 1) * 128], AF.Exp, bias=V(nacum.ap[:, h:h + 1], nacum.buf))
                    c.actf(segU[:, h, :], pU[:, h * 128:(h + 1) * 128], AF.Exp, bias=V(acum.ap[:, h:h + 1], acum.buf))
                if CUT <= 9:
                    continue
                pG = bank()
                for h in range(4):
                    c.mm(pG[:, h * 128:(h + 1) * 128], kbT[:, h, :], knT[:, h, :], start=True, stop=True, last=(h == 3))
                pQK = bank()
                for h in range(4):
                    c.mm(pQK[:, h * 128:(h + 1) * 128], knT[:, h, :], qnT[:, h, :], start=True, stop=True, last=(h == 3))
                A_, B_, X_ = Am[0], Bm[0], Xm[0]
                c.stt(A_[:], b4(pG[:]), negones[:, 0:1], segU[:], ALU.mult, ALU.mult)
                c.tt("dve", qkT[:], b4(pQK[:]), segT[:], ALU.mult)
                pq = trf(None, lambda j: A_[:, j, :], 4)
                c.copy("act", B_[:], b4(pq[:]))
                c.tt("dve", X_[:], B_[:], bcm(identf), ALU.add)
                if CUT <= 10:
                    continue
                for lvl in range(1, 7):
                    A2, B2, X2 = Am[lvl % 2], Bm[lvl % 2], Xm[lvl % 2]
                    pAq = bank()
                    for h in range(4):
                        c.mm(pAq[:, h * 128:(h + 1) * 128], B_[:, h, :], A_[:, h, :], start=True, stop=True, last=(h == 3))
                    c.copy("act", A2[:], b4(pAq[:]))
                    if lvl < 6:
                        pBq = bank()
                        for h in range(4):
                            c.mm(pBq[:, h * 128:(h + 1) * 128], A_[:, h, :], B_[:, h, :], start=True, stop=True,
                                 last=(h == 3))
                        c.copy("dve", B2[:], b4(pBq[:]))
                    pX = bank()
                    for h in range(4):
                        c.mm(pX[:, h * 128:(h + 1) * 128], identf[:], X_[:, h, :], start=True, stop=False, last=False)
                        c.mm(pX[:, h * 128:(h + 1) * 128], A2[:, h, :], X_[:, h, :], start=False, stop=True, last=(h == 3))
                    c.copy("dve" if lvl % 2 else "act", X2[:], b4(pX[:]))
                    A_, B_, X_ = A2, B2, X2
                PT_ = X_
                if CUT <= 11:
                    continue
                pW = bank()
                for h in range(4):
                    c.mm(pW[:, h * 128:(h + 1) * 128], kbg[:, h, :], PT_[:, h, :], start=True, stop=True, last=(h == 3))
                c.actf(nwT[:], b4(pW[:]), AF.Copy, scale=-1.0)
                pV = bank()
                for h in range(4):
                    c.mm(pV[:, h * 128:(h + 1) * 128], PT_[:, h, :], vb[:, h, :], start=True, stop=False, last=False)
                    c.mm(pV[:, h * 128:(h + 1) * 128], nwT[:, h, :], S[:, h, :], start=False, stop=True, last=(h == 3))
                c.copy("dve", vn[:], b4(pV[:]))
                pO = bank()
                for h in range(4):
                    c.mm(pO[:, h * 128:(h + 1) * 128], qdT[:, h, :], S[:, h, :], start=True, stop=False, last=False)
                    c.mm(pO[:, h * 128:(h + 1) * 128], qkT[:, h, :], vn[:, h, :], start=False, stop=True, last=(h == 3))
                pS = bank()
                for h in range(4):
                    c.mm(pS[:, h * 128:(h + 1) * 128], kdec[:, h, :], vn[:, h, :], start=True, stop=True, last=(h == 3))
                for h in range(4):
                    c.stt(S[:, h, :], S[:, h, :], V(etot.ap[:, h:h + 1], etot.buf), pS[:, h * 128:(h + 1) * 128],
                          ALU.mult, ALU.add)
                if CUT <= 12:
                    continue
                sso = smv()
                for h in range(4):
                    c.actf(junk[:, 0:128], pO[:, h * 128:(h + 1) * 128], AF.Square, accum=V(sso.ap[:, h:h + 1], sso.buf))
                r0, r1_, r2 = smv(), smv(), smv()
                c.ts("dve", r0, sso, 1.0 / 128, EPS, ALU.mult, ALU.add)
                c.actf(r1_, r0, AF.Sqrt)
                c.recip(r2, r1_)
                for h in range(4):
                    c.stt(on[:, h, :], pO[:, h * 128:(h + 1) * 128], V(r2.ap[:, h:h + 1], r2.buf), hnw[:], ALU.mult, ALU.mult)
                c.tt("pool", ob[:], on[:], zs[:].rearrange("p (h d) -> p h d", h=4), ALU.mult)
                for h in range(4):
                    c.tr(pT[:, h, :], ob[:, h, :], ident[:], last=(h == 3))
                o_ = obT[ti % 2]
                c.copy("act", o_[:], pT[:, 0:4, :])
                c.sp.dma(V(omT.h[512:1024, tsl].rearrange("(c p) t -> p c t", p=128), omT.c(ti).buf), o_[:])
            c.barrier()

        if "b" not in EVEN_PARTS:
            return
        with ExitStack() as es:
            QT = c.sb(es, [128, 4, SEQ], BF16, "QT")
            KT = c.sb(es, [128, 4, SEQ], BF16, "KT")
            for cq_ in range(4):
                c.sp.dma(QT[:, cq_, :], V(qkd.h[:, cq_, :], qkd.buf))
                c.sp.dma(KT[:, cq_, :], V(qkd.h[:, 4 + cq_, :], qkd.buf))
            accs = c.sb(es, [65, 4, 2048], F32, "accs")
            Vp = [c.sb(es, [128, 32, 4, 65], BF16, "Vp") for _ in range(2)]
            eb = [c.sb(es, [128, 2, 128], BF16, "eb") for _ in range(3)]
            rden = c.sb(es, [64, 512], F32, "rden")
            oT = [c.sb(es, [64, 2048], BF16, "oT") for _ in range(2)]
            pA = [c.ps(es, [128, 512], F32, "pA") for _ in range(8)]
            pai = [0]

            def bank():
                b = pA[pai[0] % 8]
                pai[0] += 1
                return b

            for v_ in Vp:
                c.memset("pool", v_[:, :, :, 64:65], 1.0)
            cnt = 0
            ecnt = 0
            ocnt = 0
            vall = T(vatt[0].h, Buf())
            for hg in range(2):
                for H in range(2):
                    c.memset("pool", accs[:], 0.0)
                    for d in (1, 4, 16):
                        nb = 16 // d
                        b0 = nb * H
                        vp = Vp[cnt % 2]
                        cnt += 1
                        for r in range(d):
                            for lb in range(nb + 1):
                                b = b0 - 1 + lb
                                if b < 0:
                                    continue
                                t0 = r + d * 128 * b
                                src = P["vatt_full"].h[t0:t0 + d * 127 + 1:d, hg * 256:(hg + 1) * 256]
                                q = c.sp
                                q.dma(vp[:, r * (nb + 1) + lb, :, 0:64],
                                      V(src.rearrange("p (h e) -> p h e", h=4), P["vatt_full"].buf))
                        for hh in range(4):
                            h = hg * 4 + hh
                            cq = h // 2
                            pb = 64 * (h % 2)
                            units = [(r, b) for r in range(d) for b in range(b0, b0 + nb)]
                            for u4 in range(4):
                                pnum = bank()
                                for ui in range(4):
                                    r, b = units[u4 * 4 + ui]
                                    q0 = r + d * 128 * b
                                    q_ap = QT.h[pb:pb + 64, cq, q0:q0 + d * 127 + 1:d]
                                    kbs = [b - 1, b] if b >= 1 else [b]
                                    ps_ = bank()
                                    for j, kb_ in enumerate(kbs):
                                        k0 = r + d * 128 * kb_
                                        k_ap = KT.h[pb:pb + 64, cq, k0:k0 + d * 127 + 1:d]
                                        c.pe.issue(lambda: nc.tensor.matmul(out=ps_.h[:, j * 128:(j + 1) * 128], lhsT=k_ap,
                                                                            rhs=q_ap, start=True, stop=False),
                                                   [QT.buf, KT.buf], [ps_.buf], inc=False)
                                        mk = maskp if kb_ == b - 1 else maskb
                                        c.mm(ps_[:, j * 128:(j + 1) * 128], ident[:], mk[:, 0:128], start=False, stop=True)
                                    e_ = eb[ecnt % 3]
                                    ecnt += 1
                                    nk = len(kbs)
                                    c.actf(e_[:, 0:nk, :], ps_[:, 0:nk * 128].rearrange("p (j q) -> p j q", j=nk), AF.Exp,
                                           scale=0.125)
                                    for j, kb_ in enumerate(kbs):
                                        slot = r * (nb + 1) + (kb_ - (b0 - 1))
                                        c.mm(pnum[0:65, ui * 128:(ui + 1) * 128], vp[:, slot, hh, :], e_[:, j, :],
                                             start=(j == 0), stop=(j == nk - 1))
                                if d == 1:
                                    av = accs[0:65, hh, u4 * 512:(u4 + 1) * 512]
                                    pvw = pnum[0:65, :]
                                elif d == 4:
                                    av = accs[0:65, hh, u4:2048:4]
                                    pvw = pnum[0:65, :]
                                else:
                                    av = V(accs.h[0:65, hh, :].rearrange("p (i r) -> p r i", r=16)[:, u4 * 4:(u4 + 1) * 4, :],
                                           accs.buf)
                                    pvw = pnum[0:65, :].rearrange("p (u q) -> p u q", u=4)
                                c.tt("dve", av, av, pvw, ALU.add)
                    for hh in range(4):
                        h = hg * 4 + hh
                        o_ = oT[ocnt % 2]
                        ocnt += 1
                        for q4 in range(4):
                            pden = bank()
                            c.mm(pden[0:64, :], sel[0:65, 0:64], accs[0:65, hh, q4 * 512:(q4 + 1) * 512], start=True, stop=True)
                            c.recip(rden[:], pden[0:64, :])
                            c.tt("pool", o_[:, q4 * 512:(q4 + 1) * 512], accs[0:64, hh, q4 * 512:(q4 + 1) * 512], rden[:],
                                 ALU.mult)
                        c.sp.dma(V(omT.h[h * 64:(h + 1) * 64, H * 2048:(H + 1) * 2048], omT.c("a%d_%d" % (h, H)).buf), o_[:])
            c.barrier()

INPUT_NAMES = ["norm_w", "ffn_w_gate", "ffn_w_up", "ffn_w_down", "even_w_in", "even_conv_w", "even_a_log",
               "even_dt_bias", "even_head_norm_w", "even_w_out", "odd_w_in", "odd_conv_w", "odd_conv_b",
               "odd_dt_bias", "odd_a_log", "odd_d_skip", "odd_out_norm_w", "odd_w_out"]


def consts_np():
    k = np.arange(128)
    triU = (k[:, None] <= k[None, :]).astype(np.float32)
    maskb = np.where(k[None, :] >= k[:, None], 0.0, -30000.0).astype(np.float32)
    return {"ident": np.eye(128, dtype=np.float32).astype(ml_dtypes.bfloat16),
            "identf": np.eye(128, dtype=np.float32),
            "triU": triU, "ones": np.ones((128, 128), np.float32),
            "maskb": np.tile(maskb, (1, 4)).astype(ml_dtypes.bfloat16),
            "negones": -np.ones((128, 128), np.float32),
            "maskus": np.tile(np.where(k[None, :] < k[:, None], 0.0, -30000.0), (1, 4)).astype(ml_dtypes.bfloat16),
            "onesb": np.ones((128, 128), np.float32).astype(ml_dtypes.bfloat16),
            "maskp": np.where(k[None, :] <= k[:, None], 0.0, -30000.0).astype(ml_dtypes.bfloat16),
            "sel": np.concatenate([np.zeros((64, 64), np.float32), np.ones((64, 64), np.float32)], 0)}


CONST_SPECS = [("ident", [128, 128], BF16), ("identf", [128, 128], F32), ("triU", [128, 128], F32),
               ("ones", [128, 128], F32), ("maskb", [128, 512], BF16), ("negones", [128, 128], F32),
               ("maskus", [128, 512], BF16), ("onesb", [128, 128], BF16), ("maskp", [128, 128], BF16),
               ("sel", [128, 64], F32)]


def build(shapes, stages=None, debug=False):
    if stages is None:
        stages = list(range(12))
    nc = bass.Bass("TRN2", target_bir_lowering=False)
    P = {}
    x = nc.dram_tensor("x", [SEQ, D], F32, kind="ExternalInput").ap()
    for n in INPUT_NAMES:
        P[n] = T(nc.dram_tensor(n, list(shapes[n]), F32, kind="ExternalInput").ap())
    out = nc.dram_tensor("out", [SEQ, D], F32, kind="ExternalOutput").ap()
    ymix = nc.dram_tensor("ymix", [SEQ, 2048], BF16, kind="Internal").ap()
    omix = nc.dram_tensor("omix", [D, SEQ], BF16, kind="ExternalOutput" if debug else "Internal").ap()
    vattd = nc.dram_tensor("vattd", [SEQ, 512], BF16, kind="Internal").ap()
    qkdd = nc.dram_tensor("qkdd", [128, 8, SEQ], BF16, kind="Internal").ap()
    with ExitStack() as es:
        c = Ctx(nc, es)
        for (n, shp, dt) in CONST_SPECS:
            d = nc.dram_tensor(n, shp, dt, kind="ExternalInput").ap()
            t = c.sb(es, shp, dt, n + "_sb")
            c.sp.dma(t[:], V(d[:, :], Buf()))
            P[n] = t
        xt = [T(x[i * 128:(i + 1) * 128, :]) for i in range(NT)]
        ht = [T(out[i * 128:(i + 1) * 128, :]) for i in range(NT)]
        yt2048 = [T(ymix[i * 128:(i + 1) * 128, :]) for i in range(NT)]
        omT = T(omix)
        P["vatt_full"] = T(vattd)
        P["qkd"] = T(qkdd)
        vatt = [T(vattd[i * 128:(i + 1) * 128, :], P["vatt_full"].buf) for i in range(NT)]
        src = xt
        for sid in stages:
            L, kind = sid // 3, sid % 3
            if kind == 0:
                ffn_stage(c, P, L, 0, src, ht)
            elif kind == 2:
                ffn_stage(c, P, L, 1, src, ht)
            else:
                if L % 2 == 1:
                    ssd_stage(c, P, L, src, yt2048)
                    outproj_stage(c, P, L, yt2048, "odd_w_out", L // 2, 2048, src, ht)
                else:
                    even_stage(c, P, L, src, omT, vatt)
                    if "o" in os.environ.get("EVEN_PARTS", "abo"):
                        outproj_stage(c, P, L, None, "even_w_out", L // 2, 1024, src, ht, fm=omT)
            src = ht
        c.barrier()
    return nc


def kernel(**inputs):
    x = np.ascontiguousarray(inputs["x"], dtype=np.float32)
    nb = x.shape[0]
    shapes = {n: inputs[n].shape for n in INPUT_NAMES}
    nc = build(shapes)
    base = {n: np.ascontiguousarray(inputs[n], dtype=np.float32) for n in INPUT_NAMES}
    base.update(consts_np())
    in_maps = []
    for b in range(nb):
        m = dict(base)
        m["x"] = x[b]
        in_maps.append(m)
    res = run_bass_kernel_spmd(nc, in_maps, core_ids=list(range(nb)))
    return np.stack([np.asarray(r["out"]) for r in res.results], axis=0).astype(np.float32)
```

```python
from contextlib import ExitStack
import numpy as np
import ml_dtypes
import concourse.bass as bass
import concourse.mybir as mybir
from concourse.bass_utils import run_bass_kernel_spmd

F32 = mybir.dt.float32
BF16 = mybir.dt.bfloat16
ALU = mybir.AluOpType
AF = mybir.ActivationFunctionType

SEQ = 4096
D = 1024
DFF = 2816
NT = SEQ // 128
EPS = 1e-6
import os
EVEN_PARTS = os.environ.get('EVEN_PARTS', 'ab')
NT_LIM = int(os.environ.get('NT_LIM', '32'))
STRICT = os.environ.get('STRICT', '0') == '1'
CUT = int(os.environ.get('CUT', '99'))
SEM_LIMIT = int(os.environ.get('SEM_LIMIT', '24000'))
NQ_SEMS = 20


class Buf:
    __slots__ = ("w", "r")

    def __init__(self):
        self.w = {}
        self.r = {}


class V:
    __slots__ = ("ap", "buf")

    def __init__(self, ap, buf):
        self.ap = ap
        self.buf = buf

    def rearrange(self, pat, **kw):
        return V(self.ap.rearrange(pat, **kw), self.buf)


class T:
    def __init__(self, h, buf=None):
        self.h = h
        self.buf = buf if buf is not None else Buf()
        self.chunks = {}

    def __getitem__(self, idx):
        return V(self.h[idx], self.buf)

    def c(self, key):
        t = self.chunks.get(key)
        if t is None:
            t = T(self.h, Buf())
            self.chunks[key] = t
        return t


class Eng:
    def __init__(self, ctx, name, e, compute=True, dma=False):
        self.ctx = ctx
        self.name = name
        self.e = e
        self.waited = {}
        self.last_tok = None
        self.own = set()
        self.cnt = 0
        self.sem = None
        if compute:
            self._new_sem()
        self.qsems = []
        self.qvals = []
        self.qi = 0
        if dma:
            for _ in range(NQ_SEMS):
                self.qsems.append(ctx.new_sem())
                self.qvals.append(0)

    def _new_sem(self):
        self.sem = self.ctx.new_sem()
        self.own.add(self.sem)
        self.cnt = 0

    def wait(self, toks):
        for s, v in toks.items():
            if self.waited.get(s, 0) < v:
                self.e.wait_ge(self.ctx.sems[s], v)
                self.waited[s] = v

    def deps(self, reads, writes):
        need = {}
        for b in reads:
            for s, v in b.w.items():
                if need.get(s, 0) < v:
                    need[s] = v
        skip_own = (not STRICT) or self.name == "pe"
        for b in writes:
            for s, v in b.w.items():
                if s in self.own and skip_own:
                    continue
                if need.get(s, 0) < v:
                    need[s] = v
            for s, v in b.r.items():
                if s in self.own and skip_own:
                    continue
                if need.get(s, 0) < v:
                    need[s] = v
        self.wait(need)

    def token(self, inc):
        if inc and self.cnt >= SEM_LIMIT:
            pass
        return (self.sem, self.cnt + 1)

    def mark(self, reads, writes, tok):
        s, v = tok
        for b in reads:
            if b.r.get(s, 0) < v:
                b.r[s] = v
        for b in writes:
            if b.w.get(s, 0) < v:
                b.w[s] = v

    def issue(self, fn, reads, writes, inc=True):
        self.deps(reads, writes)
        tok = (self.sem, self.cnt + 1)
        ins = fn()
        self.mark(reads, writes, tok)
        if inc:
            ins.then_inc(self.ctx.sems[self.sem], 1)
            self.cnt += 1
            self.last_tok = (self.sem, self.cnt)
            if self.cnt >= SEM_LIMIT:
                self._new_sem()
        return ins

    def dma(self, out, in_, share=False, **kw):
        need = {}
        if not share:
            self.qi = (self.qi + 1) % NQ_SEMS
        qi = self.qi
        s = self.qsems[qi]
        for ss, v in in_.buf.w.items():
            if need.get(ss, 0) < v:
                need[ss] = v
        for dct in (out.buf.w, out.buf.r):
            for ss, v in dct.items():
                if ss == s and share:
                    continue
                if need.get(ss, 0) < v:
                    need[ss] = v
        if not share and self.qvals[qi] > 0:
            if need.get(s, 0) < self.qvals[qi]:
                need[s] = self.qvals[qi]
        self.wait(need)
        self.qvals[qi] += 16
        v = self.qvals[qi]
        self.e.dma_start(out=out.ap, in_=in_.ap, **kw).then_inc(self.ctx.sems[s], 16)
        if in_.buf.r.get(s, 0) < v:
            in_.buf.r[s] = v
        if out.buf.w.get(s, 0) < v:
            out.buf.w[s] = v
        return (s, v)


class Ctx:
    def __init__(self, nc, es):
        self.nc = nc
        self.es = es
        self.sems = []
        self.pe = Eng(self, "pe", nc.tensor)
        self.act = Eng(self, "act", nc.scalar, dma=True)
        self.dve = Eng(self, "dve", nc.vector)
        self.pool = Eng(self, "pool", nc.gpsimd, dma=True)
        self.sp = Eng(self, "sp", nc.sync, compute=False, dma=True)
        self.engs = [self.pe, self.act, self.dve, self.pool, self.sp]
        self.nalloc = 0

    def new_sem(self):
        h = self.es.enter_context(self.nc.semaphore("s%d" % len(self.sems)))
        self.sems.append(h)
        return len(self.sems) - 1

    def sb(self, es, shape, dt, name=None):
        self.nalloc += 1
        return T(es.enter_context(self.nc.sbuf_tensor("%s_%d" % (name or "t", self.nalloc), shape, dt)))

    def ps(self, es, shape, dt, name=None):
        self.nalloc += 1
        return T(es.enter_context(self.nc.psum_tensor("%s_%d" % (name or "p", self.nalloc), shape, dt)))

    def barrier(self):
        toks = {}
        for e in self.engs:
            if e.last_tok is not None:
                toks[e.last_tok[0]] = e.last_tok[1]
            for s, v in zip(e.qsems, e.qvals):
                if v > 0:
                    toks[s] = v
        for e in self.engs:
            e.wait(toks)

    def mm(self, out, lhsT, rhs, start, stop, last=None, **kw):
        if last is None:
            last = stop
        return self.pe.issue(
            lambda: self.nc.tensor.matmul(out=out.ap, lhsT=lhsT.ap, rhs=rhs.ap, start=start, stop=stop, **kw),
            [lhsT.buf, rhs.buf], [out.buf], inc=last)

    def tr(self, out, in_, ident, last=True):
        return self.pe.issue(
            lambda: self.nc.tensor.transpose(out=out.ap, in_=in_.ap, identity=ident.ap),
            [in_.buf, ident.buf], [out.buf], inc=last)

    def actf(self, out, in_, func, bias=None, scale=None, accum=None):
        reads = [in_.buf]
        writes = [out.buf]
        kw = {}
        if bias is not None:
            if isinstance(bias, V):
                reads.append(bias.buf)
                kw["bias"] = bias.ap
            else:
                kw["bias"] = bias
        if scale is not None:
            if isinstance(scale, V):
                reads.append(scale.buf)
                kw["scale"] = scale.ap
            else:
                kw["scale"] = scale
        if accum is not None:
            writes.append(accum.buf)
            kw["accum_out"] = accum.ap
        return self.act.issue(
            lambda: self.nc.scalar.activation(out=out.ap, in_=in_.ap, func=func, **kw), reads, writes)

    def _veng(self, eng):
        return (self.dve, self.nc.vector) if eng == "dve" else (self.pool, self.nc.gpsimd)

    def ts(self, eng, out, in0, s1, s2, op0, op1=None, accum=None):
        E, e = self._veng(eng)
        reads = [in0.buf]
        writes = [out.buf]
        a1 = s1
        a2 = s2
        if isinstance(s1, V):
            reads.append(s1.buf)
            a1 = s1.ap
        if isinstance(s2, V):
            reads.append(s2.buf)
            a2 = s2.ap
        kw = {}
        if op1 is not None:
            kw["op1"] = op1
        if accum is not None:
            writes.append(accum.buf)
            kw["accum_out"] = accum.ap
        return E.issue(lambda: e.tensor_scalar(out=out.ap, in0=in0.ap, scalar1=a1, scalar2=a2, op0=op0, **kw),
                       reads, writes)

    def tt(self, eng, out, in0, in1, op):
        E, e = self._veng(eng)
        return E.issue(lambda: e.tensor_tensor(out=out.ap, in0=in0.ap, in1=in1.ap, op=op),
                       [in0.buf, in1.buf], [out.buf])

    def stt(self, out, in0, scalar, in1, op0, op1):
        reads = [in0.buf, in1.buf]
        a = scalar
        if isinstance(scalar, V):
            reads.append(scalar.buf)
            a = scalar.ap
        return self.dve.issue(
            lambda: self.nc.vector.scalar_tensor_tensor(out=out.ap, in0=in0.ap, scalar=a, in1=in1.ap,
                                                        op0=op0, op1=op1), reads, [out.buf])

    def copy(self, eng, out, in_):
        if eng == "act":
            return self.actf(out, in_, AF.Copy)
        E, e = self._veng(eng)
        return E.issue(lambda: e.tensor_copy(out=out.ap, in_=in_.ap), [in_.buf], [out.buf])

    def recip(self, out, in_):
        return self.dve.issue(lambda: self.nc.vector.reciprocal(out=out.ap, in_=in_.ap), [in_.buf], [out.buf])

    def memset(self, eng, out, val):
        E, e = self._veng(eng)
        return E.issue(lambda: e.memset(out.ap, val), [], [out.buf])


class Stats:
    def __init__(self, c, es, n=64):
        self.t = c.sb(es, [128, n], F32)
        self.n = n
        self.i = 0

    def get(self):
        k = self.i % self.n
        self.i += 1
        return self.t.c(k)[:, k:k + 1]


def rstd_chain(c, st, ss, n):
    a = st.get()
    c.ts("dve", a, ss, 1.0 / n, EPS, ALU.mult, ALU.add)
    b = st.get()
    c.actf(b, a, AF.Sqrt)
    r = st.get()
    c.recip(r, b)
    return r


def ffn_stage(c, P, L, j, hsrc, hdst):
    nc = c.nc
    pre_i, post_i = (0, 1) if j == 0 else (4, 5)
    with ExitStack() as es:
        Wg = c.sb(es, [128, 8, DFF], BF16, "Wg")
        Wu = c.sb(es, [128, 8, DFF], BF16, "Wu")
        Wd = c.sb(es, [128, 22, D], BF16, "Wd")
        gpre = c.sb(es, [128, D], F32, "gpre")
        gpost = c.sb(es, [128, D], F32, "gpost")
        hl = [c.sb(es, [128, D], F32, "hl%d" % i) for i in range(2)]
        hr = [c.sb(es, [128, D], F32, "hr%d" % i) for i in range(2)]
        xn = [c.sb(es, [128, D], BF16, "xn%d" % i) for i in range(4)]
        xT = [c.sb(es, [128, 8, 512], BF16, "xT%d" % i) for i in range(1)]
        actb = c.sb(es, [128, 22, 512], BF16, "actb")
        sg = [c.sb(es, [128, 512], F32, "sg%d" % i) for i in range(1)]
        tmp = [c.sb(es, [128, D], F32, "tmp%d" % i) for i in range(1)]
        junk = c.sb(es, [128, D], BF16, "junk")
        st = Stats(c, es, 64)
        pT = c.ps(es, [128, 8, 128], BF16, "pT")
        pG = [c.ps(es, [128, 512], F32, "pG%d" % i) for i in range(2)]
        pU = [c.ps(es, [128, 512], F32, "pU%d" % i) for i in range(2)]
        pD = [c.ps(es, [128, 512], F32, "pD%d" % i) for i in range(3)]

        gsrc = P["ffn_w_gate"].h[L, j].rearrange("(k p) f -> p k f", p=128)
        usrc = P["ffn_w_up"].h[L, j].rearrange("(k p) f -> p k f", p=128)
        dsrc = P["ffn_w_down"].h[L, j].rearrange("(k p) f -> p k f", p=128)
        for k in range(8):
            c.pool.dma(Wg[:, k, :], V(gsrc[:, k, :], P["ffn_w_gate"].buf), share=(k > 0))
        for k in range(8):
            c.pool.dma(Wu[:, k, :], V(usrc[:, k, :], P["ffn_w_up"].buf), share=(k > 0))
        for k in range(22):
            c.pool.dma(Wd[:, k, :], V(dsrc[:, k, :], P["ffn_w_down"].buf), share=(k > 0))
        c.sp.dma(gpre[:], V(P["norm_w"].h[L, pre_i, :].partition_broadcast(128), P["norm_w"].buf))
        c.sp.dma(gpost[:], V(P["norm_w"].h[L, post_i, :].partition_broadcast(128), P["norm_w"].buf))
        c.ts("dve", gpost[:], gpost[:], 0.5, None, ALU.mult)

        ident = P["ident"]
        hts = {}

        def norm(g):
            for s in range(4):
                ti = g * 4 + s
                ht = hl[ti % 2]
                c.sp.dma(ht[:], hsrc[ti][:, :])
                ss = st.get()
                c.actf(junk[:], ht[:], AF.Square, accum=ss)
                r = rstd_chain(c, st, ss, D)
                x = xn[ti % 4]
                c.stt(x[:], ht[:], r, gpre[:], ALU.mult, ALU.mult)

        def transp(g):
            for s in range(4):
                ti = g * 4 + s
                x = xn[ti % 4]
                for k in range(8):
                    c.tr(pT[:, k, :], x[:, k * 128:(k + 1) * 128], ident[:], last=(k == 7))
                c.copy("act", xT[0].c(s)[:, :, s * 128:(s + 1) * 128], pT[:])

        def xTv(g, k):
            return [xT[0].c(s).buf for s in range(4)]

        def gateup(g, hook):
            xt = xT[0]
            for f in range(22):
                pg = pG[f % 2]
                pu = pU[f % 2]
                for (W, pp) in ((Wg, pg), (Wu, pu)):
                    for k in range(8):
                        rhs = V(xt.h[:, k, :], xt.c(0).buf)
                        ins = c.pe.issue(
                            lambda W=W, pp=pp, k=k, rhs=rhs: nc.tensor.matmul(
                                out=pp.h[:], lhsT=W.h[:, k, f * 128:(f + 1) * 128], rhs=rhs.ap,
                                start=(k == 0), stop=(k == 7)),
                            [W.buf] + xTv(g, k), [pp.buf], inc=(k == 7))
                c.actf(sg[0][:], pg[:], AF.Silu)
                c.tt("dve", actb.c(f)[:, f, :], sg[0][:], pu[:], ALU.mult)
                if f == 8 and hook is not None:
                    hook()

        dcount = [0]

        def down(g):
            for s in range(4):
                ti = g * 4 + s
                ht = hr[ti % 2]
                c.sp.dma(ht[:], hsrc[ti][:, :])
                banks = []
                sss = []
                for half in range(2):
                    pd = pD[dcount[0] % 3]
                    dcount[0] += 1
                    banks.append(pd)
                    for f in range(22):
                        c.mm(pd[:], actb.c(f)[:, f, s * 128:(s + 1) * 128], Wd[:, f, half * 512:(half + 1) * 512],
                             start=(f == 0), stop=(f == 21))
                    ssh = st.get()
                    c.actf(junk[:, 0:512], pd[:], AF.Square, accum=ssh)
                    sss.append(ssh)
                ss = st.get()
                c.tt("dve", ss, sss[0], sss[1], ALU.add)
                r = rstd_chain(c, st, ss, D)
                tm = tmp[0]
                for half in range(2):
                    c.stt(tm[:, half * 512:(half + 1) * 512], banks[half][:], r,
                          gpost[:, half * 512:(half + 1) * 512], ALU.mult, ALU.mult)
                c.tt("pool", ht[:], ht[:], tm[:], ALU.add)
                c.sp.dma(hdst[ti][:, :], ht[:])

        NG = NT // 4
        norm(0)
        transp(0)
        for g in range(NG):
            nxt = (lambda g=g: norm(g + 1)) if g + 1 < NG else None
            gateup(g, nxt)
            if g + 1 < NG:
                transp(g + 1)
            down(g)
        c.barrier()


def load_bcast(c, es, src_ap, srcbuf, n, name):
    t = c.sb(es, [128, n], F32, name)
    c.sp.dma(t[:], V(src_ap.partition_broadcast(128), srcbuf))
    return t


def front_norm_T(c, st, ht, gain, xn, junk, pT, uT, ident):
    ss = st.get()
    c.actf(junk[:], ht[:], AF.Square, accum=ss)
    r = rstd_chain(c, st, ss, D)
    c.stt(xn[:], ht[:], r, gain[:], ALU.mult, ALU.mult)
    for k in range(8):
        c.tr(pT[:, k, :], xn[:, k * 128:(k + 1) * 128], ident[:], last=(k == 7))
    c.copy("act", uT[:], pT[:])


def outproj_stage(c, P, L, ysrc, wname, i, K, hsrc, hdst, fm=None):
    KC = K // 128
    with ExitStack() as es:
        W = c.sb(es, [128, KC, D], BF16, "Wo")
        wsrc = P[wname].h[i].rearrange("(k p) f -> p k f", p=128)
        for k in range(KC):
            c.pool.dma(W[:, k, :], V(wsrc[:, k, :], P[wname].buf), share=(k > 0))
        g3 = load_bcast(c, es, P["norm_w"].h[L, 3, :], P["norm_w"].buf, D, "g3")
        yt = [c.sb(es, [128, K], BF16, "yt") for _ in range(2)]
        yT = [c.sb(es, [128, KC, 128], BF16, "yT") for _ in range(2)]
        hr = [c.sb(es, [128, D], F32, "hr") for _ in range(2)]
        tmp = c.sb(es, [128, D], F32, "tmp")
        junk = c.sb(es, [128, 512], BF16, "junk")
        st = Stats(c, es, 32)
        pT = [c.ps(es, [128, 8, 128], BF16, "pT") for _ in range(2)]
        pD = [c.ps(es, [128, 512], F32, "pD") for _ in range(4)]
        ident = P["ident"]
        dc = 0

        def loads(tj):
            c.sp.dma(hr[tj % 2][:], hsrc[tj][:, :])
            if fm is not None:
                c.sp.dma(yT[tj % 2][:], V(fm.h[:, tj * 128:(tj + 1) * 128].rearrange("(k p) t -> p k t", p=128), fm.buf))
            else:
                c.sp.dma(yt[tj % 2][:], ysrc[tj][:, :])

        loads(0)
        for ti in range(NT):
            y = yt[ti % 2]
            h = hr[ti % 2]
            yt_T = yT[ti % 2]
            if ti + 1 < NT:
                loads(ti + 1)
            for kb in range(KC // 8 if fm is None else 0):
                p = pT[kb % 2]
                for k in range(8):
                    kk = kb * 8 + k
                    c.tr(p[:, k, :], y[:, kk * 128:(kk + 1) * 128], ident[:], last=(k == 7))
                c.copy("act", yt_T[:, kb * 8:(kb + 1) * 8, :], p[:])
            banks = []
            sss = []
            for half in range(2):
                pd = pD[dc % 4]
                dc += 1
                banks.append(pd)
                for k in range(KC):
                    c.mm(pd[:], yt_T[:, k, :], W[:, k, half * 512:(half + 1) * 512], start=(k == 0), stop=(k == KC - 1))
                ssh = st.get()
                c.actf(junk[:], pd[:], AF.Square, accum=ssh)
                sss.append(ssh)
            ss = st.get()
            c.tt("dve", ss, sss[0], sss[1], ALU.add)
            r = rstd_chain(c, st, ss, D)
            for half in range(2):
                c.stt(tmp[:, half * 512:(half + 1) * 512], banks[half][:], r, g3[:, half * 512:(half + 1) * 512],
                      ALU.mult, ALU.mult)
            c.tt("pool", h[:], h[:], tmp[:], ALU.add)
            c.sp.dma(hdst[ti][:, :], h[:])
        c.barrier()


def small_T(c, es, rows_src, nrow, ncol, P, name):
    nch = ncol // 128
    rowt = c.sb(es, [nrow, ncol], F32, name + "r")
    for j, (ap, buf) in enumerate(rows_src):
        c.sp.dma(rowt[j:j + 1, :], V(ap.partition_broadcast(1), buf))
    pt = c.ps(es, [128, nch, nrow], F32, name + "p")
    for ch in range(nch):
        c.tr(pt[:, ch, :], rowt[0:nrow, ch * 128:(ch + 1) * 128], P["identf"][0:nrow, 0:nrow], last=(ch == nch - 1))
    out = c.sb(es, [128, nch, nrow], F32, name)
    c.copy("dve", out[:], pt[:])
    return out


def ssd_stage(c, P, L, hsrc, ydst):
    nc = c.nc
    i = L // 2
    NX = 5152
    with ExitStack() as es0:
        cw = c.sb(es0, [128, 24, 5], F32, "cwk")
        with ExitStack() as es1:
            rows = [(P["odd_conv_w"].h[i, k, :], P["odd_conv_w"].buf) for k in range(4)]
            rows.append((P["odd_conv_b"].h[i, :], P["odd_conv_b"].buf))
            cw_tmp = small_T(c, es1, rows, 5, 3072, P, "cw")
            c.copy("dve", cw[:], cw_tmp[:])
            c.barrier()
        es = es0
        W = c.sb(es, [128, 8, NX], BF16, "Win")
        wsrc = P["odd_w_in"].h[i].rearrange("(k p) f -> p k f", p=128)
        for k in range(8):
            c.pool.dma(W[:, k, :], V(wsrc[:, k, :], P["odd_w_in"].buf), share=(k > 0))
        g2 = load_bcast(c, es, P["norm_w"].h[L, 2, :], P["norm_w"].buf, D, "g2")
        dtb = load_bcast(c, es, P["odd_dt_bias"].h[i, :], P["odd_dt_bias"].buf, 32, "dtb")
        negA = load_bcast(c, es, P["odd_a_log"].h[i, :], P["odd_a_log"].buf, 32, "negA")
        dsk = load_bcast(c, es, P["odd_d_skip"].h[i, :], P["odd_d_skip"].buf, 32, "dsk")
        onw = load_bcast(c, es, P["odd_out_norm_w"].h[i, :], P["odd_out_norm_w"].buf, 2048, "onw")
        c.actf(negA[:], negA[:], AF.Exp)
        c.ts("dve", negA[:], negA[:], -1.0, None, ALU.mult)

        ident, identf, triU, ones, maskb = P["ident"], P["identf"], P["triU"], P["ones"], P["maskb"]
        hl = [c.sb(es, [128, D], F32, "hl") for _ in range(1)]
        xn = c.sb(es, [128, D], BF16, "xn")
        junk = xn
        uT = [c.sb(es, [128, 8, 128], BF16, "uT") for _ in range(1)]
        st = Stats(c, es, 64)
        pcb = c.sb(es, [128, 24, 131], BF16, "pcb")
        dgw = c.sb(es, [128, 24, 4, 128], BF16, "dgw")
        for k in range(4):
            c.tt("pool", dgw[:, :, k, :], V(ident.h[:, :].unsqueeze(1).broadcast_to([128, 24, 128]), ident.buf),
                 V(cw.h[:, :, k:k + 1].broadcast_to([128, 24, 128]), cw.buf), ALU.mult)
        xc = c.sb(es, [128, 24, 128], BF16, "xc")
        xtm = c.sb(es, [128, 32, 64], BF16, "xtm")
        Btm = c.sb(es, [128, 4, 128], BF16, "Btm")
        xdt = c.sb(es, [128, 32, 64], BF16, "xdt")
        xdec = c.sb(es, [128, 32, 64], BF16, "xdec")
        xD = c.sb(es, [128, 32, 64], BF16, "xD")
        sm = c.sb(es, [128, 12, 32], F32, "sm")
        R1 = [c.sb(es, [128, 8, 128], F32, "R1") for _ in range(1)]
        segT = [c.sb(es, [128, 8, 128], BF16, "segT") for _ in range(4)]
        cbs4 = c.sb(es, [128, 4, 128], BF16, "cbs4")
        MT = [c.sb(es, [128, 8, 128], BF16, "MT") for _ in range(4)]
        tb = [c.sb(es, [128, 8, 64], F32, "tb") for _ in range(1)]
        yb = [c.sb(es, [128, 512], F32, "yb") for _ in range(4)]

        yn = [c.sb(es, [128, 512], BF16, "yn") for _ in range(2)]
        S = c.sb(es, [128, 32, 64], F32, "S")
        Sb = c.sb(es, [128, 32, 64], BF16, "Sb")
        for g in range(4):
            c.memset("dve", S.c(g)[:, g * 8:(g + 1) * 8, :], 0.0)
            c.memset("dve", Sb.c(g)[:, g * 8:(g + 1) * 8, :], 0.0)
        for ch in range(24):
            c.memset("pool", pcb.c(ch)[:, ch, :], 0.0)

        pT = c.ps(es, [128, 8, 128], BF16, "pT")
        pA = [c.ps(es, [128, 512], F32, "pA") for _ in range(7)]
        pai = [0]

        def bank():
            b = pA[pai[0] % 7]
            pai[0] += 1
            return b

        def smv(j):
            return sm.c(j)[:, j, :]

        for ti in range(NT):
            ht = hl[0]
            c.sp.dma(ht[:], hsrc[ti][:, :])
            u = uT[0]
            front_norm_T(c, st, ht, g2, xn, junk, pT, u, ident)

            pdt = bank()
            for k in range(8):
                c.mm(pdt[:, 0:32], u[:, k, :], W[:, k, 5120:5152], start=(k == 0), stop=(k == 7))
            dtr, ex, dt, a, nacum, eacum, tot, edarg, edec, etot = [smv(j) for j in range(10)]
            c.tt("dve", dtr, pdt[:, 0:32], dtb[:], ALU.add)
            c.actf(ex, dtr, AF.Exp)
            c.actf(dt, ex, AF.Ln, bias=1.0)
            c.tt("dve", a, dt, negA[:], ALU.mult)
            pac = bank()
            c.mm(pac[:, 0:32], triU[:], a, start=True, stop=True)
            c.mm(pac[:, 32:64], ones[:], a, start=True, stop=True)
            c.actf(nacum, pac[:, 0:32], AF.Copy, scale=-1.0)
            c.actf(eacum, pac[:, 0:32], AF.Exp)
            c.actf(tot, pac[:, 32:64], AF.Copy)
            c.tt("dve", edarg, tot, nacum, ALU.add)
            c.actf(edec, edarg, AF.Exp)
            c.actf(etot, tot, AF.Exp)

            pps = {}

            def cv_in(ch):
                pp = bank()
                pps[ch] = pp
                col = 2048 + ch * 128
                for k in range(8):
                    c.mm(pp[:, 0:128], W[:, k, col:col + 128], u[:, k, :], start=(k == 0), stop=(k == 7))
                c.copy("act" if ch % 2 == 0 else "dve", pcb.c(ch)[:, ch, 3:131], pp[:, 0:128])

            def cv_out(ch):
                pp = pps.pop(ch)
                pc = pcb.c(ch)
                for kk in range(4):
                    c.mm(pp[:, 128:256], dgw[:, ch, kk, :], pc[:, ch, kk:kk + 128], start=(kk == 0), stop=(kk == 3))
                c.copy("pool", pc[:, ch, 0:3], pc[:, ch, 128:131])
                c.actf(xc.c(ch)[:, ch, :], pp[:, 128:256], AF.Silu, bias=cw[:, ch, 4:5])

            LAG = 3
            for ch in range(24 + LAG):
                if ch < 24:
                    cv_in(ch)
                if ch >= LAG:
                    cv_out(ch - LAG)

            for kb in range(2):
                for k in range(8):
                    ch = kb * 8 + k
                    c.tr(pT[:, k, :], xc.c(ch)[:, ch, :], ident[:], last=(k == 7))
                c.copy("act", xtm[:, kb * 16:(kb + 1) * 16, :], pT[:].rearrange("p k (a b) -> p (k a) b", a=2))
            for g in range(4):
                c.tr(pT[:, g, :], xc.c(16 + g)[:, 16 + g, :], ident[:], last=(g == 3))
            c.copy("act", Btm[:], pT[:, 0:4, :])
            bc = lambda v: V(v.ap.unsqueeze(2).broadcast_to([128, 32, 64]), v.buf)
            c.tt("dve", xdt[:], xtm[:], bc(dt), ALU.mult)
            c.tt("pool", xdec[:], xdt[:], bc(edec), ALU.mult)
            c.tt("pool", xD[:], xtm[:], V(dsk.h[:, :].unsqueeze(2).broadcast_to([128, 32, 64]), dsk.buf), ALU.mult)

            zs4 = T(xtm.h, xtm.buf)
            zs4v = lambda g: V(xtm.h[:, g * 8:(g + 1) * 8, :].rearrange("p r d -> p (r d)"), xtm.buf)
            HS = [slice(g * 8, (g + 1) * 8) for g in range(4)]
            bcr = lambda v, hs: V(v.ap[:, hs].unsqueeze(2).broadcast_to([128, 8, 64]), v.buf)
            for g in range(4):
                pz = bank()
                for k in range(8):
                    c.mm(pz[:], u[:, k, :], W[:, k, g * 512:(g + 1) * 512], start=(k == 0), stop=(k == 7))
                c.actf(zs4v(g), pz[:], AF.Silu)
            pcbk = bank()
            for g in range(4):
                c.mm(pcbk[:, g * 128:(g + 1) * 128], xc.c(16 + g)[:, 16 + g, :], xc.c(20 + g)[:, 20 + g, :], start=True,
                     stop=True, last=(g == 3))
            c.copy("dve", cbs4[:], pcbk[:].rearrange("p (g l) -> p g l", g=4))
            if CUT <= 2:
                continue
            for g in range(4):
                r1 = R1[0]
                c.tt("pool", r1[:], V(a.ap[:, HS[g]].unsqueeze(2).broadcast_to([128, 8, 128]), a.buf),
                     V(triU.h[:, :].unsqueeze(1).broadcast_to([128, 8, 128]), triU.buf), ALU.mult)
                for half in range(2):
                    ps_ = bank()
                    c.mm(ps_[:], ones[:], r1[:, half * 4:(half + 1) * 4, :], start=True, stop=False, last=False)
                    c.mm(ps_[:], ident[:], maskb[:], start=False, stop=True)
                    for rr in range(4):
                        r = half * 4 + rr
                        hcol = g * 8 + r
                        c.actf(segT[g][:, r, :], ps_[:, rr * 128:(rr + 1) * 128], AF.Exp,
                               bias=V(nacum.ap[:, hcol:hcol + 1], nacum.buf))
            for g in range(4):
                c.tt("dve", MT[g][:], segT[g][:], V(cbs4.h[:, g:g + 1, :].broadcast_to([128, 8, 128]), cbs4.buf), ALU.mult)
            if CUT <= 3:
                continue
            for g in range(4):
                hs = HS[g]
                pY1 = bank()
                c.mm(pY1[:], ident[:], xD[:, hs, :], start=True, stop=False, last=False)
                for r in range(8):
                    c.mm(pY1[:, r * 64:(r + 1) * 64], MT[g][:, r, :], xdt[:, g * 8 + r, :], start=False, stop=(r == 7),
                         last=(r == 7))
                pY2 = bank()
                c.mm(pY2[:], xc.c(20 + g)[:, 20 + g, :], Sb.c(g)[:, hs, :], start=True, stop=True)
                t_ = tb[0]
                c.tt("dve", t_[:], pY2[:].rearrange("p (r d) -> p r d", r=8), bcr(eacum, hs), ALU.mult)
                c.tt("dve", yb[g][:], t_[:].rearrange("p r d -> p (r d)"), pY1[:], ALU.add)
            if CUT <= 4:
                continue
            pSs = []
            for g in range(4):
                pS = bank()
                pSs.append(pS)
                c.mm(pS[:], Btm[:, g, :], xdec[:, HS[g], :], start=True, stop=True)
                Sg = S.c(g)
                c.tt("pool", Sg[:, HS[g], :], Sg[:, HS[g], :], bcr(etot, HS[g]), ALU.mult)
            for g in range(4):
                Sg = S.c(g)
                c.tt("dve", Sg[:, HS[g], :], Sg[:, HS[g], :], pSs[g][:].rearrange("p (r d) -> p r d", r=8), ALU.add)
                c.copy("act", Sb.c(g)[:, HS[g], :], Sg[:, HS[g], :])
            if CUT <= 5:
                continue
            sss = []
            for g in range(4):
                c.tt("pool", yb[g][:], yb[g][:], zs4v(g), ALU.mult)
                ss = st.get()
                sss.append(ss)
                c.actf(junk[:, 0:512], yb[g][:], AF.Square, accum=ss)
            aa = []
            for g in range(4):
                a_ = st.get()
                c.ts("dve", a_, sss[g], 1.0 / 512, EPS, ALU.mult, ALU.add)
                aa.append(a_)
            bb = []
            for g in range(4):
                b_ = st.get()
                c.actf(b_, aa[g], AF.Sqrt)
                bb.append(b_)
            for g in range(4):
                r_ = st.get()
                c.recip(r_, bb[g])
                ynt = yn[g % 2]
                c.stt(ynt[:], yb[g][:], r_, onw[:, g * 512:(g + 1) * 512], ALU.mult, ALU.mult)
                c.sp.dma(V(ydst[ti].h[:, g * 512:(g + 1) * 512], ydst[ti].buf), ynt[:])
        c.barrier()


def even_stage(c, P, L, hsrc, omT, vatt):
    nc = c.nc
    i = L // 2
    NX = 3592
    ident, identf, triU, ones, maskb = P["ident"], P["identf"], P["triU"], P["ones"], P["maskb"]
    negones, maskus, onesb, maskp, sel = P["negones"], P["maskus"], P["onesb"], P["maskp"], P["sel"]
    SCALE_B = 128.0 ** -0.5
    with ExitStack() as esq:
        qkd = P["qkd"]
        with ExitStack() as es:
            cw = c.sb(es, [128, 12, 4], F32, "cwk")
            with ExitStack() as es1:
                rows = [(P["even_conv_w"].h[i, k, :], P["even_conv_w"].buf) for k in range(4)]
                cw_tmp = small_T(c, es1, rows, 4, 1536, P, "cw")
                c.copy("dve", cw[:], cw_tmp[:])
                c.barrier()
            W = c.sb(es, [128, 8, NX], BF16, "Win")
            wsrc = P["even_w_in"].h[i].rearrange("(k p) f -> p k f", p=128)
            for k in range(8):
                c.pool.dma(W[:, k, :], V(wsrc[:, k, :], P["even_w_in"].buf), share=(k > 0))
            g2 = load_bcast(c, es, P["norm_w"].h[L, 2, :], P["norm_w"].buf, D, "g2")
            dtb = load_bcast(c, es, P["even_dt_bias"].h[i, :], P["even_dt_bias"].buf, 4, "dtb")
            negA = load_bcast(c, es, P["even_a_log"].h[i, :], P["even_a_log"].buf, 4, "negA")
            hnw = load_bcast(c, es, P["even_head_norm_w"].h[i, :], P["even_head_norm_w"].buf, 128, "hnw")
            c.actf(negA[:], negA[:], AF.Exp)
            c.ts("dve", negA[:], negA[:], -1.0, None, ALU.mult)

            hl = [c.sb(es, [128, D], F32, "hl") for _ in range(2)]
            xn = c.sb(es, [128, D], BF16, "xn")
            junk = c.sb(es, [128, D], BF16, "junk")
            uT = [c.sb(es, [128, 8, 128], BF16, "uT") for _ in range(2)]
            st = Stats(c, es, 64)
            vt = [c.sb(es, [128, 512], BF16, "vt") for _ in range(2)]
            qkt = [c.sb(es, [128, 8, 128], BF16, "qkt") for _ in range(2)]
            zsb = [c.sb(es, [128, 512], F32, "zs") for _ in range(2)]
            sm = c.sb(es, [128, 48, 4], F32, "sm")
            smi = [0]

            def smv():
                j = smi[0] % 32
                smi[0] += 1
                return sm.c(j)[:, j, :]

            pcb = c.sb(es, [128, 12, 131], BF16, "pcb")
            dgw = c.sb(es, [128, 12, 4, 128], BF16, "dgw")
            for k in range(4):
                c.tt("pool", dgw[:, :, k, :], V(ident.h[:, :].unsqueeze(1).broadcast_to([128, 12, 128]), ident.buf),
                     V(cw.h[:, :, k:k + 1].broadcast_to([128, 12, 128]), cw.buf), ALU.mult)
            xg = c.sb(es, [128, 12, 128], F32, "xg")
            xtm = c.sb(es, [128, 12, 128], F32, "xtm")
            dg = [c.sb(es, [128, 4, 128], F32, "dg") for _ in range(2)]
            knT = c.sb(es, [128, 4, 128], F32, "knT")
            kbT = c.sb(es, [128, 4, 128], F32, "kbT")
            qnT = c.sb(es, [128, 4, 128], F32, "qnT")
            qdT = c.sb(es, [128, 4, 128], F32, "qdT")
            kbg = c.sb(es, [128, 4, 128], F32, "kbg")
            kdec = c.sb(es, [128, 4, 128], F32, "kdec")
            vb = c.sb(es, [128, 4, 128], F32, "vb")
            R1 = c.sb(es, [128, 4, 128], F32, "R1")
            segT = c.sb(es, [128, 4, 128], F32, "segT")
            segU = c.sb(es, [128, 4, 128], F32, "segU")
            qkT = c.sb(es, [128, 4, 128], F32, "qkT")
            Am = [c.sb(es, [128, 4, 128], F32, "Am") for _ in range(2)]
            Bm = [c.sb(es, [128, 4, 128], F32, "Bm") for _ in range(2)]
            Xm = [c.sb(es, [128, 4, 128], F32, "Xm") for _ in range(2)]
            nwT = c.sb(es, [128, 4, 128], F32, "nwT")
            vn = c.sb(es, [128, 4, 128], F32, "vn")
            S = c.sb(es, [128, 4, 128], F32, "S")
            on = c.sb(es, [128, 4, 128], F32, "on")
            ob = c.sb(es, [128, 4, 128], BF16, "ob")
            obT = [c.sb(es, [128, 4, 128], BF16, "obT") for _ in range(2)]
            c.memset("dve", S[:], 0.0)
            for ch in range(12):
                c.memset("pool", pcb.c(ch)[:, ch, :], 0.0)

            pT = c.ps(es, [128, 8, 128], BF16, "pT")
            pA = [c.ps(es, [128, 512], F32, "pA") for _ in range(7)]
            pai = [0]

            def bank():
                b = pA[pai[0] % 7]
                pai[0] += 1
                return b

            def trf(dst3, src_fn, n):
                pb_ = bank()
                for j in range(n):
                    c.tr(pb_[:, j * 128:(j + 1) * 128], src_fn(j), identf[:], last=(j == n - 1))
                return pb_

            def b4(v):
                return v.rearrange("p (h d) -> p h d", h=4)

            def bcl(v, n=128):
                return V(v.ap.unsqueeze(2).broadcast_to([128, 4, n]), v.buf)

            def bcm(t):
                return V(t.h[:, :].unsqueeze(1).broadcast_to([128, 4, 128]), t.buf)

            NTE = min(NT, NT_LIM)

            def front(tj):
                ht = hl[tj % 2]
                c.sp.dma(ht[:], hsrc[tj][:, :])
                front_norm_T(c, st, ht, g2, xn, junk, pT, uT[tj % 2], ident)

            def attn_gen(tj):
                u_ = uT[tj % 2]
                tsl_ = slice(tj * 128, (tj + 1) * 128)
                for cch in range(8):
                    pp = bank()
                    for k in range(8):
                        c.mm(pp[:, 0:128], W[:, k, cch * 128:(cch + 1) * 128], u_[:, k, :], start=(k == 0), stop=(k == 7))
                    c.copy("act" if cch % 2 == 0 else "dve", qkt[tj % 2][:, cch, :], pp[:, 0:128])
                    if cch == 7:
                        c.sp.dma(V(qkd.h[:, :, tsl_], qkd.buf), qkt[tj % 2][:])
                    yield
                pv = bank()
                for k in range(8):
                    c.mm(pv[:], u_[:, k, :], W[:, k, 1024:1536], start=(k == 0), stop=(k == 7))
                v_ = vt[tj % 2]
                c.copy("dve", v_[:], pv[:])
                c.sp.dma(vatt[tj][:, :], v_[:])
                yield
                pz = bank()
                for k in range(8):
                    c.mm(pz[:], u_[:, k, :], W[:, k, 3072:3584], start=(k == 0), stop=(k == 7))
                c.actf(zsb[tj % 2][:], pz[:], AF.Silu)
                yield

            def step(gen, n=1):
                if gen is None:
                    return
                for _ in range(n):
                    try:
                        next(gen)
                    except StopIteration:
                        return

            front(0)
            step(attn_gen(0), 100)
            for ti in range(NTE):
                tsl = slice(ti * 128, (ti + 1) * 128)
                u = uT[ti % 2]
                zs = zsb[ti % 2]
                ag = None
                if CUT <= 2:
                    continue
                pba = bank()
                for k in range(8):
                    c.mm(pba[:, 0:8], u[:, k, :], W[:, k, 3584:3592], start=(k == 0), stop=(k == 7))
                beta, spi, ex, spv, g, nacum, acum, egc, tot, edarg, edec, etot = [smv() for _ in range(12)]
                c.actf(beta, pba[:, 0:4], AF.Sigmoid)
                c.tt("dve", spi, pba[:, 4:8], dtb[:], ALU.add)
                c.actf(ex, spi, AF.Exp)
                c.actf(spv, ex, AF.Ln, bias=1.0)
                c.tt("dve", g, spv, negA[:], ALU.mult)
                pac = bank()
                c.mm(pac[:, 0:4], triU[:], g, start=True, stop=True)
                c.mm(pac[:, 4:8], ones[:], g, start=True, stop=True)
                c.actf(nacum, pac[:, 0:4], AF.Copy, scale=-1.0)
                c.actf(acum, pac[:, 0:4], AF.Copy)
                c.actf(egc, pac[:, 0:4], AF.Exp)
                c.actf(tot, pac[:, 4:8], AF.Copy)
                c.tt("dve", edarg, tot, nacum, ALU.add)
                c.actf(edec, edarg, AF.Exp)
                c.actf(etot, tot, AF.Exp)

                if CUT <= 3:
                    continue
                pps = {}

                def cv_in(ch):
                    pp = bank()
                    pps[ch] = pp
                    col = 1536 + ch * 128
                    for k in range(8):
                        c.mm(pp[:, 0:128], W[:, k, col:col + 128], u[:, k, :], start=(k == 0), stop=(k == 7))
                    c.copy("act" if ch % 2 == 0 else "dve", pcb.c(ch)[:, ch, 3:131], pp[:, 0:128])

                def cv_out(ch):
                    pp = pps.pop(ch)
                    pc = pcb.c(ch)
                    for kk in range(4):
                        c.mm(pp[:, 128:256], dgw[:, ch, kk, :], pc[:, ch, kk:kk + 128], start=(kk == 0), stop=(kk == 3))
                    c.copy("pool", pc[:, ch, 0:3], pc[:, ch, 128:131])
                    c.actf(xg.c(ch)[:, ch, :], pp[:, 128:256], AF.Silu)

                LAG = 3
                for ch in range(12 + LAG):
                    if ch < 12:
                        cv_in(ch)
                    if ch >= LAG:
                        cv_out(ch - LAG)
                if ti + 1 < NTE:
                    front(ti + 1)
                    ag = attn_gen(ti + 1)
                if CUT <= 4:
                    step(ag, 100)
                    continue
                for q3 in range(3):
                    pq = trf(None, lambda j: xg.c(q3 * 4 + j)[:, q3 * 4 + j, :], 4)
                    c.copy("act" if q3 != 1 else "dve", xtm[:, q3 * 4:(q3 + 1) * 4, :], b4(pq[:]))
                if CUT <= 5:
                    continue
                ssq = sm.c("ssq")
                ssqk = [V(sm.h[:, 40 + (j // 4), (j % 4):(j % 4) + 1], ssq.buf) for j in range(8)]
                for j in range(8):
                    c.actf(junk[:, 0:128], xtm[:, j, :], AF.Square, accum=ssqk[j])
                ssv = V(sm.h[:, 40:42, :], ssq.buf)
                rn0 = V(sm.h[:, 42:44, :], sm.c("rn0").buf)
                rn1 = V(sm.h[:, 44:46, :], sm.c("rn1").buf)
                rn = V(sm.h[:, 46:48, :], sm.c("rn").buf)
                c.ts("dve", rn0, ssv, EPS, None, ALU.add)
                c.actf(rn1, rn0, AF.Sqrt)
                c.recip(rn, rn1)
                rq = V(sm.h[:, 46, :], rn.buf)
                rk = V(sm.h[:, 47, :], rn.buf)
                s_kb, s_qn, s_qd, s_kbg, s_kdec = [smv() for _ in range(5)]
                c.tt("dve", s_kb, rk, beta, ALU.mult)
                c.ts("dve", s_qn, rq, SCALE_B, None, ALU.mult)
                c.tt("dve", s_qd, s_qn, egc, ALU.mult)
                c.tt("dve", s_kbg, s_kb, egc, ALU.mult)
                c.tt("dve", s_kdec, rk, edec, ALU.mult)
                if CUT <= 6:
                    continue
                for qi, (sc, src0, dstT) in enumerate(((rk, 4, knT), (s_kb, 4, kbT), (s_qn, 0, qnT), (s_qd, 0, qdT))):
                    d_ = dg[qi % 2]
                    c.tt("pool", d_[:], bcm(identf), bcl(sc), ALU.mult)
                    pb_ = bank()
                    c.mm(pb_[:], ones[:], d_[:], start=True, stop=True)
                    srcv = V(xg.h[:, src0:src0 + 4, :], xg.c(src0).buf)
                    E = c.dve
                    E.issue(lambda: nc.vector.tensor_tensor(out=dstT.h[:], in0=srcv.ap, in1=b4(pb_[:]).ap, op=ALU.mult),
                            [xg.c(src0 + j).buf for j in range(4)] + [pb_.buf], [dstT.buf])
                if CUT <= 7:
                    continue
                c.tt("pool", kbg[:], xtm[:, 4:8, :], bcl(s_kbg), ALU.mult)
                c.tt("pool", kdec[:], xtm[:, 4:8, :], bcl(s_kdec), ALU.mult)
                c.tt("pool", vb[:], xtm[:, 8:12, :], bcl(beta), ALU.mult)
                if CUT <= 8:
                    continue
                c.tt("pool", R1[:], bcl(g), bcm(triU), ALU.mult)
                pL = bank()
                c.mm(pL[:], ones[:], R1[:], start=True, stop=False, last=False)
                c.mm(pL[:], ident[:], maskb[:], start=False, stop=True)
                pU = bank()
                c.mm(pU[:], negones[:], R1[:], start=True, stop=False, last=False)
                c.mm(pU[:], ident[:], maskus[:], start=False, stop=True)
                for h in range(4):
                    c.actf(segT[:, h, :], pL[:, h * 128:(h + 1) * 128], AF.Exp, bias=V(nacum.ap[:, h:h + 1], nacum.buf))
                    c.actf(segU[:, h, :], pU[:, h * 128:(h + 1) * 128], AF.Exp, bias=V(acum.ap[:, h:h + 1], acum.buf))
                if CUT <= 9:
                    continue
                pG = bank()
                for h in range(4):
                    c.mm(pG[:, h * 128:(h + 1) * 128], kbT[:, h, :], knT[:, h, :], start=True, stop=True, last=(h == 3))
                pQK = bank()
                for h in range(4):
                    c.mm(pQK[:, h * 128:(h + 1) * 128], knT[:, h, :], qnT[:, h, :], start=True, stop=True, last=(h == 3))
                A_, B_, X_ = Am[0], Bm[0], Xm[0]
                c.stt(A_[:], b4(pG[:]), negones[:, 0:1], segU[:], ALU.mult, ALU.mult)
                c.tt("dve", qkT[:], b4(pQK[:]), segT[:], ALU.mult)
                pq = trf(None, lambda j: A_[:, j, :], 4)
                c.copy("act", B_[:], b4(pq[:]))
                c.tt("dve", X_[:], B_[:], bcm(identf), ALU.add)
                if CUT <= 10:
                    continue
                for lvl in range(1, 7):
                    A2, B2, X2 = Am[lvl % 2], Bm[lvl % 2], Xm[lvl % 2]
                    pAq = bank()
                    for h in range(4):
                        c.mm(pAq[:, h * 128:(h + 1) * 128], B_[:, h, :], A_[:, h, :], start=True, stop=True, last=(h == 3))
                    c.copy("act", A2[:], b4(pAq[:]))
                    if lvl < 6:
                        pBq = bank()
                        for h in range(4):
                            c.mm(pBq[:, h * 128:(h + 1) * 128], A_[:, h, :], B_[:, h, :], start=True, stop=True,
                                 last=(h == 3))
                        c.copy("dve", B2[:], b4(pBq[:]))
                    pX = bank()
                    for h in range(4):
                        c.mm(pX[:, h * 128:(h + 1) * 128], identf[:], X_[:, h, :], start=True, stop=False, last=False)
                        c.mm(pX[:, h * 128:(h + 1) * 128], A2[:, h, :], X_[:, h, :], start=False, stop=True, last=(h == 3))
                    c.copy("dve" if lvl % 2 else "act", X2[:], b4(pX[:]))
                    A_, B_, X_ = A2, B2, X2
                    step(ag, 2)
                PT_ = X_
                step(ag, 100)
                if CUT <= 11:
                    continue
                pW = bank()
                for h in range(4):
                    c.mm(pW[:, h * 128:(h + 1) * 128], kbg[:, h, :], PT_[:, h, :], start=True, stop=True, last=(h == 3))
                c.actf(nwT[:], b4(pW[:]), AF.Copy, scale=-1.0)
                pV = bank()
                for h in range(4):
                    c.mm(pV[:, h * 128:(h + 1) * 128], PT_[:, h, :], vb[:, h, :], start=True, stop=False, last=False)
                    c.mm(pV[:, h * 128:(h + 1) * 128], nwT[:, h, :], S[:, h, :], start=False, stop=True, last=(h == 3))
                c.copy("dve", vn[:], b4(pV[:]))
                pO = bank()
                for h in range(4):
                    c.mm(pO[:, h * 128:(h + 1) * 128], qdT[:, h, :], S[:, h, :], start=True, stop=False, last=False)
                    c.mm(pO[:, h * 128:(h + 1) * 128], qkT[:, h, :], vn[:, h, :], start=False, stop=True, last=(h == 3))
                pS = bank()
                for h in range(4):
                    c.mm(pS[:, h * 128:(h + 1) * 128], kdec[:, h, :], vn[:, h, :], start=True, stop=True, last=(h == 3))
                for h in range(4):
                    c.stt(S[:, h, :], S[:, h, :], V(etot.ap[:, h:h + 1], etot.buf), pS[:, h * 128:(h + 1) * 128],
                          ALU.mult, ALU.add)
                if CUT <= 12:
                    continue
                sso = smv()
                for h in range(4):
                    c.actf(junk[:, 0:128], pO[:, h * 128:(h + 1) * 128], AF.Square, accum=V(sso.ap[:, h:h + 1], sso.buf))
                r0, r1_, r2 = smv(), smv(), smv()
                c.ts("dve", r0, sso, 1.0 / 128, EPS, ALU.mult, ALU.add)
                c.actf(r1_, r0, AF.Sqrt)
                c.recip(r2, r1_)
                for h in range(4):
                    c.stt(on[:, h, :], pO[:, h * 128:(h + 1) * 128], V(r2.ap[:, h:h + 1], r2.buf), hnw[:], ALU.mult, ALU.mult)
                c.tt("pool", ob[:], on[:], zs[:].rearrange("p (h d) -> p h d", h=4), ALU.mult)
                for h in range(4):
                    c.tr(pT[:, h, :], ob[:, h, :], ident[:], last=(h == 3))
                o_ = obT[ti % 2]
                c.copy("act", o_[:], pT[:, 0:4, :])
                c.sp.dma(V(omT.h[512:1024, tsl].rearrange("(c p) t -> p c t", p=128), omT.c(ti).buf), o_[:])
            c.barrier()

        if "b" not in EVEN_PARTS:
            return
        with ExitStack() as es:
            QT = c.sb(es, [128, 4, SEQ], BF16, "QT")
            KT = c.sb(es, [128, 4, SEQ], BF16, "KT")
            for cq_ in range(4):
                c.sp.dma(QT[:, cq_, :], V(qkd.h[:, cq_, :], qkd.buf))
                c.sp.dma(KT[:, cq_, :], V(qkd.h[:, 4 + cq_, :], qkd.buf))
            accs = c.sb(es, [65, 4, 2048], F32, "accs")
            Vp = [c.sb(es, [128, 32, 4, 65], BF16, "Vp") for _ in range(2)]
            eb = [c.sb(es, [128, 2, 128], BF16, "eb") for _ in range(3)]
            rden = c.sb(es, [64, 512], F32, "rden")
            oT = [c.sb(es, [64, 2048], BF16, "oT") for _ in range(2)]
            pA = [c.ps(es, [128, 512], F32, "pA") for _ in range(8)]
            pai = [0]

            def bank():
                b = pA[pai[0] % 8]
                pai[0] += 1
                return b

            for v_ in Vp:
                c.memset("pool", v_[:, :, :, 64:65], 1.0)
            cnt = 0
            ecnt = 0
            ocnt = 0
            vall = T(vatt[0].h, Buf())
            for hg in range(2):
                for H in range(2):
                    c.memset("pool", accs[:], 0.0)
                    for d in (1, 4, 16):
                        nb = 16 // d
                        b0 = nb * H
                        vp = Vp[cnt % 2]
                        cnt += 1
                        for r in range(d):
                            for lb in range(nb + 1):
                                b = b0 - 1 + lb
                                if b < 0:
                                    continue
                                t0 = r + d * 128 * b
                                src = P["vatt_full"].h[t0:t0 + d * 127 + 1:d, hg * 256:(hg + 1) * 256]
                                q = c.sp
                                q.dma(vp[:, r * (nb + 1) + lb, :, 0:64],
                                      V(src.rearrange("p (h e) -> p h e", h=4), P["vatt_full"].buf))
                        for hh in range(4):
                            h = hg * 4 + hh
                            cq = h // 2
                            pb = 64 * (h % 2)
                            units = [(r, b) for r in range(d) for b in range(b0, b0 + nb)]
                            for u4 in range(4):
                                pnum = bank()
                                for ui in range(4):
                                    r, b = units[u4 * 4 + ui]
                                    q0 = r + d * 128 * b
                                    q_ap = QT.h[pb:pb + 64, cq, q0:q0 + d * 127 + 1:d]
                                    kbs = [b - 1, b] if b >= 1 else [b]
                                    ps_ = bank()
                                    for j, kb_ in enumerate(kbs):
                                        k0 = r + d * 128 * kb_
                                        k_ap = KT.h[pb:pb + 64, cq, k0:k0 + d * 127 + 1:d]
                                        c.pe.issue(lambda: nc.tensor.matmul(out=ps_.h[:, j * 128:(j + 1) * 128], lhsT=k_ap,
                                                                            rhs=q_ap, start=True, stop=False),
                                                   [QT.buf, KT.buf], [ps_.buf], inc=False)
                                        mk = maskp if kb_ == b - 1 else maskb
                                        c.mm(ps_[:, j * 128:(j + 1) * 128], ident[:], mk[:, 0:128], start=False, stop=True)
                                    e_ = eb[ecnt % 3]
                                    ecnt += 1
                                    nk = len(kbs)
                                    c.actf(e_[:, 0:nk, :], ps_[:, 0:nk * 128].rearrange("p (j q) -> p j q", j=nk), AF.Exp,
                                           scale=0.125)
                                    for j, kb_ in enumerate(kbs):
                                        slot = r * (nb + 1) + (kb_ - (b0 - 1))
                                        c.mm(pnum[0:65, ui * 128:(ui + 1) * 128], vp[:, slot, hh, :], e_[:, j, :],
                                             start=(j == 0), stop=(j == nk - 1))
                                if d == 1:
                                    av = accs[0:65, hh, u4 * 512:(u4 + 1) * 512]
                                    pvw = pnum[0:65, :]
                                elif d == 4:
                                    av = accs[0:65, hh, u4:2048:4]
                                    pvw = pnum[0:65, :]
                                else:
                                    av = V(accs.h[0:65, hh, :].rearrange("p (i r) -> p r i", r=16)[:, u4 * 4:(u4 + 1) * 4, :],
                                           accs.buf)
                                    pvw = pnum[0:65, :].rearrange("p (u q) -> p u q", u=4)
                                c.tt("dve", av, av, pvw, ALU.add)
                    for hh in range(4):
                        h = hg * 4 + hh
                        o_ = oT[ocnt % 2]
                        ocnt += 1
                        for q4 in range(4):
                            pden = bank()
                            c.mm(pden[0:64, :], sel[0:65, 0:64], accs[0:65, hh, q4 * 512:(q4 + 1) * 512], start=True, stop=True)
                            c.recip(rden[:], pden[0:64, :])
                            c.tt("pool", o_[:, q4 * 512:(q4 + 1) * 512], accs[0:64, hh, q4 * 512:(q4 + 1) * 512], rden[:],
                                 ALU.mult)
                        c.sp.dma(V(omT.h[h * 64:(h + 1) * 64, H * 2048:(H + 1) * 2048], omT.c("a%d_%d" % (h, H)).buf), o_[:])
            c.barrier()

INPUT_NAMES = ["norm_w", "ffn_w_gate", "ffn_w_up", "ffn_w_down", "even_w_in", "even_conv_w", "even_a_log",
               "even_dt_bias", "even_head_norm_w", "even_w_out", "odd_w_in", "odd_conv_w", "odd_conv_b",
               "odd_dt_bias", "odd_a_log", "odd_d_skip", "odd_out_norm_w", "odd_w_out"]


def consts_np():
    k = np.arange(128)
    triU = (k[:, None] <= k[None, :]).astype(np.float32)
    maskb = np.where(k[None, :] >= k[:, None], 0.0, -30000.0).astype(np.float32)
    return {"ident": np.eye(128, dtype=np.float32).astype(ml_dtypes.bfloat16),
            "identf": np.eye(128, dtype=np.float32),
            "triU": triU, "ones": np.ones((128, 128), np.float32),
            "maskb": np.tile(maskb, (1, 4)).astype(ml_dtypes.bfloat16),
            "negones": -np.ones((128, 128), np.float32),
            "maskus": np.tile(np.where(k[None, :] < k[:, None], 0.0, -30000.0), (1, 4)).astype(ml_dtypes.bfloat16),
            "onesb": np.ones((128, 128), np.float32).astype(ml_dtypes.bfloat16),
            "maskp": np.where(k[None, :] <= k[:, None], 0.0, -30000.0).astype(ml_dtypes.bfloat16),
            "sel": np.concatenate([np.zeros((64, 64), np.float32), np.ones((64, 64), np.float32)], 0)}


CONST_SPECS = [("ident", [128, 128], BF16), ("identf", [128, 128], F32), ("triU", [128, 128], F32),
               ("ones", [128, 128], F32), ("maskb", [128, 512], BF16), ("negones", [128, 128], F32),
               ("maskus", [128, 512], BF16), ("onesb", [128, 128], BF16), ("maskp", [128, 128], BF16),
               ("sel", [128, 64], F32)]


def build(shapes, stages=None, debug=False):
    if stages is None:
        stages = list(range(12))
    nc = bass.Bass("TRN2", target_bir_lowering=False)
    P = {}
    x = nc.dram_tensor("x", [SEQ, D], F32, kind="ExternalInput").ap()
    for n in INPUT_NAMES:
        P[n] = T(nc.dram_tensor(n, list(shapes[n]), F32, kind="ExternalInput").ap())
    out = nc.dram_tensor("out", [SEQ, D], F32, kind="ExternalOutput").ap()
    ymix = nc.dram_tensor("ymix", [SEQ, 2048], BF16, kind="Internal").ap()
    omix = nc.dram_tensor("omix", [D, SEQ], BF16, kind="ExternalOutput" if debug else "Internal").ap()
    vattd = nc.dram_tensor("vattd", [SEQ, 512], BF16, kind="Internal").ap()
    qkdd = nc.dram_tensor("qkdd", [128, 8, SEQ], BF16, kind="Internal").ap()
    with ExitStack() as es:
        c = Ctx(nc, es)
        for (n, shp, dt) in CONST_SPECS:
            d = nc.dram_tensor(n, shp, dt, kind="ExternalInput").ap()
            t = c.sb(es, shp, dt, n + "_sb")
            c.sp.dma(t[:], V(d[:, :], Buf()))
            P[n] = t
        xt = [T(x[i * 128:(i + 1) * 128, :]) for i in range(NT)]
        ht = [T(out[i * 128:(i + 1) * 128, :]) for i in range(NT)]
        yt2048 = [T(ymix[i * 128:(i + 1) * 128, :]) for i in range(NT)]
        omT = T(omix)
        P["vatt_full"] = T(vattd)
        P["qkd"] = T(qkdd)
        vatt = [T(vattd[i * 128:(i + 1) * 128, :], P["vatt_full"].buf) for i in range(NT)]
        src = xt
        for sid in stages:
            L, kind = sid // 3, sid % 3
            if kind == 0:
                ffn_stage(c, P, L, 0, src, ht)
            elif kind == 2:
                ffn_stage(c, P, L, 1, src, ht)
            else:
                if L % 2 == 1:
                    ssd_stage(c, P, L, src, yt2048)
                    outproj_stage(c, P, L, yt2048, "odd_w_out", L // 2, 2048, src, ht)
                else:
                    even_stage(c, P, L, src, omT, vatt)
                    if "o" in os.environ.get("EVEN_PARTS", "abo"):
                        outproj_stage(c, P, L, None, "even_w_out", L // 2, 1024, src, ht, fm=omT)
            src = ht
        c.barrier()
    return nc


def kernel(**inputs):
    x = np.ascontiguousarray(inputs["x"], dtype=np.float32)
    nb = x.shape[0]
    shapes = {n: inputs[n].shape for n in INPUT_NAMES}
    nc = build(shapes)
    base = {n: np.ascontiguousarray(inputs[n], dtype=np.float32) for n in INPUT_NAMES}
    base.update(consts_np())
    in_maps = []
    for b in range(nb):
        m = dict(base)
        m["x"] = x[b]
        in_maps.append(m)
    res = run_bass_kernel_spmd(nc, in_maps, core_ids=list(range(nb)))
    return np.stack([np.asarray(r["out"]) for r in res.results], axis=0).astype(np.float32)
```

```python
from contextlib import ExitStack
import numpy as np
import ml_dtypes
import concourse.bass as bass
import concourse.mybir as mybir
from concourse.bass_utils import run_bass_kernel_spmd

F32 = mybir.dt.float32
BF16 = mybir.dt.bfloat16
ALU = mybir.AluOpType
AF = mybir.ActivationFunctionType

SEQ = 4096
D = 1024
DFF = 2816
NT = SEQ // 128
EPS = 1e-6
import os
EVEN_PARTS = os.environ.get('EVEN_PARTS', 'ab')
NT_LIM = int(os.environ.get('NT_LIM', '32'))
STRICT = os.environ.get('STRICT', '0') == '1'
CUT = int(os.environ.get('CUT', '99'))
SEM_LIMIT = int(os.environ.get('SEM_LIMIT', '24000'))
NQ_SEMS = 20


class Buf:
    __slots__ = ("w", "r")

    def __init__(self):
        self.w = {}
        self.r = {}


class V:
    __slots__ = ("ap", "buf")

    def __init__(self, ap, buf):
        self.ap = ap
        self.buf = buf

    def rearrange(self, pat, **kw):
        return V(self.ap.rearrange(pat, **kw), self.buf)


class T:
    def __init__(self, h, buf=None):
        self.h = h
        self.buf = buf if buf is not None else Buf()
        self.chunks = {}

    def __getitem__(self, idx):
        return V(self.h[idx], self.buf)

    def c(self, key):
        t = self.chunks.get(key)
        if t is None:
            t = T(self.h, Buf())
            self.chunks[key] = t
        return t


class Eng:
    def __init__(self, ctx, name, e, compute=True, dma=False):
        self.ctx = ctx
        self.name = name
        self.e = e
        self.waited = {}
        self.last_tok = None
        self.own = set()
        self.cnt = 0
        self.sem = None
        if compute:
            self._new_sem()
        self.qsems = []
        self.qvals = []
        self.qi = 0
        if dma:
            for _ in range(NQ_SEMS):
                self.qsems.append(ctx.new_sem())
                self.qvals.append(0)

    def _new_sem(self):
        self.sem = self.ctx.new_sem()
        self.own.add(self.sem)
        self.cnt = 0

    def wait(self, toks):
        for s, v in toks.items():
            if self.waited.get(s, 0) < v:
                self.e.wait_ge(self.ctx.sems[s], v)
                self.waited[s] = v

    def deps(self, reads, writes):
        need = {}
        for b in reads:
            for s, v in b.w.items():
                if need.get(s, 0) < v:
                    need[s] = v
        skip_own = (not STRICT) or self.name == "pe"
        for b in writes:
            for s, v in b.w.items():
                if s in self.own and skip_own:
                    continue
                if need.get(s, 0) < v:
                    need[s] = v
            for s, v in b.r.items():
                if s in self.own and skip_own:
                    continue
                if need.get(s, 0) < v:
                    need[s] = v
        self.wait(need)

    def token(self, inc):
        if inc and self.cnt >= SEM_LIMIT:
            pass
        return (self.sem, self.cnt + 1)

    def mark(self, reads, writes, tok):
        s, v = tok
        for b in reads:
            if b.r.get(s, 0) < v:
                b.r[s] = v
        for b in writes:
            if b.w.get(s, 0) < v:
                b.w[s] = v

    def issue(self, fn, reads, writes, inc=True):
        self.deps(reads, writes)
        tok = (self.sem, self.cnt + 1)
        ins = fn()
        self.mark(reads, writes, tok)
        if inc:
            ins.then_inc(self.ctx.sems[self.sem], 1)
            self.cnt += 1
            self.last_tok = (self.sem, self.cnt)
            if self.cnt >= SEM_LIMIT:
                self._new_sem()
        return ins

    def dma(self, out, in_, share=False, **kw):
        need = {}
        if not share:
            self.qi = (self.qi + 1) % NQ_SEMS
        qi = self.qi
        s = self.qsems[qi]
        for ss, v in in_.buf.w.items():
            if need.get(ss, 0) < v:
                need[ss] = v
        for dct in (out.buf.w, out.buf.r):
            for ss, v in dct.items():
                if ss == s and share:
                    continue
                if need.get(ss, 0) < v:
                    need[ss] = v
        if not share and self.qvals[qi] > 0:
            if need.get(s, 0) < self.qvals[qi]:
                need[s] = self.qvals[qi]
        self.wait(need)
        self.qvals[qi] += 16
        v = self.qvals[qi]
        self.e.dma_start(out=out.ap, in_=in_.ap, **kw).then_inc(self.ctx.sems[s], 16)
        if in_.buf.r.get(s, 0) < v:
            in_.buf.r[s] = v
        if out.buf.w.get(s, 0) < v:
            out.buf.w[s] = v
        return (s, v)


class Ctx:
    def __init__(self, nc, es):
        self.nc = nc
        self.es = es
        self.sems = []
        self.pe = Eng(self, "pe", nc.tensor)
        self.act = Eng(self, "act", nc.scalar, dma=True)
        self.dve = Eng(self, "dve", nc.vector)
        self.pool = Eng(self, "pool", nc.gpsimd, dma=True)
        self.sp = Eng(self, "sp", nc.sync, compute=False, dma=True)
        self.engs = [self.pe, self.act, self.dve, self.pool, self.sp]
        self.nalloc = 0

    def new_sem(self):
        h = self.es.enter_context(self.nc.semaphore("s%d" % len(self.sems)))
        self.sems.append(h)
        return len(self.sems) - 1

    def sb(self, es, shape, dt, name=None):
        self.nalloc += 1
        return T(es.enter_context(self.nc.sbuf_tensor("%s_%d" % (name or "t", self.nalloc), shape, dt)))

    def ps(self, es, shape, dt, name=None):
        self.nalloc += 1
        return T(es.enter_context(self.nc.psum_tensor("%s_%d" % (name or "p", self.nalloc), shape, dt)))

    def barrier(self):
        toks = {}
        for e in self.engs:
            if e.last_tok is not None:
                toks[e.last_tok[0]] = e.last_tok[1]
            for s, v in zip(e.qsems, e.qvals):
                if v > 0:
                    toks[s] = v
        for e in self.engs:
            e.wait(toks)

    def mm(self, out, lhsT, rhs, start, stop, last=None, **kw):
        if last is None:
            last = stop
        return self.pe.issue(
            lambda: self.nc.tensor.matmul(out=out.ap, lhsT=lhsT.ap, rhs=rhs.ap, start=start, stop=stop, **kw),
            [lhsT.buf, rhs.buf], [out.buf], inc=last)

    def tr(self, out, in_, ident, last=True):
        return self.pe.issue(
            lambda: self.nc.tensor.transpose(out=out.ap, in_=in_.ap, identity=ident.ap),
            [in_.buf, ident.buf], [out.buf], inc=last)

    def actf(self, out, in_, func, bias=None, scale=None, accum=None):
        reads = [in_.buf]
        writes = [out.buf]
        kw = {}
        if bias is not None:
            if isinstance(bias, V):
                reads.append(bias.buf)
                kw["bias"] = bias.ap
            else:
                kw["bias"] = bias
        if scale is not None:
            if isinstance(scale, V):
                reads.append(scale.buf)
                kw["scale"] = scale.ap
            else:
                kw["scale"] = scale
        if accum is not None:
            writes.append(accum.buf)
            kw["accum_out"] = accum.ap
        return self.act.issue(
            lambda: self.nc.scalar.activation(out=out.ap, in_=in_.ap, func=func, **kw), reads, writes)

    def _veng(self, eng):
        return (self.dve, self.nc.vector) if eng == "dve" else (self.pool, self.nc.gpsimd)

    def ts(self, eng, out, in0, s1, s2, op0, op1=None, accum=None):
        E, e = self._veng(eng)
        reads = [in0.buf]
        writes = [out.buf]
        a1 = s1
        a2 = s2
        if isinstance(s1, V):
            reads.append(s1.buf)
            a1 = s1.ap
        if isinstance(s2, V):
            reads.append(s2.buf)
            a2 = s2.ap
        kw = {}
        if op1 is not None:
            kw["op1"] = op1
        if accum is not None:
            writes.append(accum.buf)
            kw["accum_out"] = accum.ap
        return E.issue(lambda: e.tensor_scalar(out=out.ap, in0=in0.ap, scalar1=a1, scalar2=a2, op0=op0, **kw),
                       reads, writes)

    def tt(self, eng, out, in0, in1, op):
        E, e = self._veng(eng)
        return E.issue(lambda: e.tensor_tensor(out=out.ap, in0=in0.ap, in1=in1.ap, op=op),
                       [in0.buf, in1.buf], [out.buf])

    def stt(self, out, in0, scalar, in1, op0, op1):
        reads = [in0.buf, in1.buf]
        a = scalar
        if isinstance(scalar, V):
            reads.append(scalar.buf)
            a = scalar.ap
        return self.dve.issue(
            lambda: self.nc.vector.scalar_tensor_tensor(out=out.ap, in0=in0.ap, scalar=a, in1=in1.ap,
                                                        op0=op0, op1=op1), reads, [out.buf])

    def copy(self, eng, out, in_):
        if eng == "act":
            return self.actf(out, in_, AF.Copy)
        E, e = self._veng(eng)
        return E.issue(lambda: e.tensor_copy(out=out.ap, in_=in_.ap), [in_.buf], [out.buf])

    def recip(self, out, in_):
        return self.dve.issue(lambda: self.nc.vector.reciprocal(out=out.ap, in_=in_.ap), [in_.buf], [out.buf])

    def memset(self, eng, out, val):
        E, e = self._veng(eng)
        return E.issue(lambda: e.memset(out.ap, val), [], [out.buf])


class Stats:
    def __init__(self, c, es, n=64):
        self.t = c.sb(es, [128, n], F32)
        self.n = n
        self.i = 0

    def get(self):
        k = self.i % self.n
        self.i += 1
        return self.t.c(k)[:, k:k + 1]


def rstd_chain(c, st, ss, n):
    a = st.get()
    c.ts("dve", a, ss, 1.0 / n, EPS, ALU.mult, ALU.add)
    b = st.get()
    c.actf(b, a, AF.Sqrt)
    r = st.get()
    c.recip(r, b)
    return r


def ffn_stage(c, P, L, j, hsrc, hdst):
    nc = c.nc
    pre_i, post_i = (0, 1) if j == 0 else (4, 5)
    with ExitStack() as es:
        Wg = c.sb(es, [128, 8, DFF], BF16, "Wg")
        Wu = c.sb(es, [128, 8, DFF], BF16, "Wu")
        Wd = c.sb(es, [128, 22, D], BF16, "Wd")
        gpre = c.sb(es, [128, D], F32, "gpre")
        gpost = c.sb(es, [128, D], F32, "gpost")
        hl = [c.sb(es, [128, D], F32, "hl%d" % i) for i in range(2)]
        hr = [c.sb(es, [128, D], F32, "hr%d" % i) for i in range(2)]
        xn = [c.sb(es, [128, D], BF16, "xn%d" % i) for i in range(4)]
        xT = [c.sb(es, [128, 8, 512], BF16, "xT%d" % i) for i in range(1)]
        actb = c.sb(es, [128, 22, 512], BF16, "actb")
        sg = [c.sb(es, [128, 512], F32, "sg%d" % i) for i in range(1)]
        tmp = [c.sb(es, [128, D], F32, "tmp%d" % i) for i in range(1)]
        junk = c.sb(es, [128, D], BF16, "junk")
        st = Stats(c, es, 64)
        pT = c.ps(es, [128, 8, 128], BF16, "pT")
        pG = [c.ps(es, [128, 512], F32, "pG%d" % i) for i in range(2)]
        pU = [c.ps(es, [128, 512], F32, "pU%d" % i) for i in range(2)]
        pD = [c.ps(es, [128, 512], F32, "pD%d" % i) for i in range(3)]

        gsrc = P["ffn_w_gate"].h[L, j].rearrange("(k p) f -> p k f", p=128)
        usrc = P["ffn_w_up"].h[L, j].rearrange("(k p) f -> p k f", p=128)
        dsrc = P["ffn_w_down"].h[L, j].rearrange("(k p) f -> p k f", p=128)
        for j in range(11):
            cs = slice(j * 256, (j + 1) * 256)
            c.pool.dma(Wg.c(j)[:, :, cs], V(gsrc[:, :, cs], P["ffn_w_gate"].buf))
            c.pool.dma(Wu.c(j)[:, :, cs], V(usrc[:, :, cs], P["ffn_w_up"].buf))
        for k in range(22):
            c.pool.dma(Wd[:, k, :], V(dsrc[:, k, :], P["ffn_w_down"].buf), share=(k > 0))
        c.sp.dma(gpre[:], V(P["norm_w"].h[L, pre_i, :].partition_broadcast(128), P["norm_w"].buf))
        c.sp.dma(gpost[:], V(P["norm_w"].h[L, post_i, :].partition_broadcast(128), P["norm_w"].buf))
        c.ts("dve", gpost[:], gpost[:], 0.5, None, ALU.mult)

        ident = P["ident"]
        hts = {}

        def norm(g):
            for s in range(4):
                ti = g * 4 + s
                ht = hl[ti % 2]
                c.sp.dma(ht[:], hsrc[ti][:, :])
                ss = st.get()
                c.actf(junk[:], ht[:], AF.Square, accum=ss)
                r = rstd_chain(c, st, ss, D)
                x = xn[ti % 4]
                c.stt(x[:], ht[:], r, gpre[:], ALU.mult, ALU.mult)

        def transp(g):
            for s in range(4):
                ti = g * 4 + s
                x = xn[ti % 4]
                for k in range(8):
                    c.tr(pT[:, k, :], x[:, k * 128:(k + 1) * 128], ident[:], last=(k == 7))
                c.copy("act", xT[0].c(s)[:, :, s * 128:(s + 1) * 128], pT[:])

        def xTv(g, k):
            return [xT[0].c(s).buf for s in range(4)]

        def gateup(g, hook):
            xt = xT[0]
            for f in range(22):
                pg = pG[f % 2]
                pu = pU[f % 2]
                for (W, pp) in ((Wg, pg), (Wu, pu)):
                    for k in range(8):
                        rhs = V(xt.h[:, k, :], xt.c(0).buf)
                        ins = c.pe.issue(
                            lambda W=W, pp=pp, k=k, rhs=rhs: nc.tensor.matmul(
                                out=pp.h[:], lhsT=W.h[:, k, f * 128:(f + 1) * 128], rhs=rhs.ap,
                                start=(k == 0), stop=(k == 7)),
                            [W.c(f // 2).buf] + xTv(g, k), [pp.buf], inc=(k == 7))
                c.actf(sg[0][:], pg[:], AF.Silu)
                c.tt("dve", actb.c(f)[:, f, :], sg[0][:], pu[:], ALU.mult)
                if f == 8 and hook is not None:
                    hook()

        dcount = [0]

        def down(g):
            for s in range(4):
                ti = g * 4 + s
                ht = hr[ti % 2]
                c.sp.dma(ht[:], hsrc[ti][:, :])
                banks = []
                sss = []
                for half in range(2):
                    pd = pD[dcount[0] % 3]
                    dcount[0] += 1
                    banks.append(pd)
                    for f in range(22):
                        c.mm(pd[:], actb.c(f)[:, f, s * 128:(s + 1) * 128], Wd[:, f, half * 512:(half + 1) * 512],
                             start=(f == 0), stop=(f == 21))
                    ssh = st.get()
                    c.actf(junk[:, 0:512], pd[:], AF.Square, accum=ssh)
                    sss.append(ssh)
                ss = st.get()
                c.tt("dve", ss, sss[0], sss[1], ALU.add)
                r = rstd_chain(c, st, ss, D)
                tm = tmp[0]
                for half in range(2):
                    c.stt(tm[:, half * 512:(half + 1) * 512], banks[half][:], r,
                          gpost[:, half * 512:(half + 1) * 512], ALU.mult, ALU.mult)
                c.tt("pool", ht[:], ht[:], tm[:], ALU.add)
                c.sp.dma(hdst[ti][:, :], ht[:])

        NG = NT // 4
        norm(0)
        transp(0)
        for g in range(NG):
            nxt = (lambda g=g: norm(g + 1)) if g + 1 < NG else None
            gateup(g, nxt)
            if g + 1 < NG:
                transp(g + 1)
            down(g)
        c.barrier()


def load_bcast(c, es, src_ap, srcbuf, n, name):
    t = c.sb(es, [128, n], F32, name)
    c.sp.dma(t[:], V(src_ap.partition_broadcast(128), srcbuf))
    return t


def front_norm_T(c, st, ht, gain, xn, junk, pT, uT, ident):
    ss = st.get()
    c.actf(junk[:], ht[:], AF.Square, accum=ss)
    r = rstd_chain(c, st, ss, D)
    c.stt(xn[:], ht[:], r, gain[:], ALU.mult, ALU.mult)
    for k in range(8):
        c.tr(pT[:, k, :], xn[:, k * 128:(k + 1) * 128], ident[:], last=(k == 7))
    c.copy("act", uT[:], pT[:])


def outproj_stage(c, P, L, ysrc, wname, i, K, hsrc, hdst, fm=None):
    KC = K // 128
    with ExitStack() as es:
        W = c.sb(es, [128, KC, D], BF16, "Wo")
        wsrc = P[wname].h[i].rearrange("(k p) f -> p k f", p=128)
        for k in range(KC):
            c.pool.dma(W[:, k, :], V(wsrc[:, k, :], P[wname].buf), share=(k > 0))
        g3 = load_bcast(c, es, P["norm_w"].h[L, 3, :], P["norm_w"].buf, D, "g3")
        yt = [c.sb(es, [128, K], BF16, "yt") for _ in range(2)]
        yT = [c.sb(es, [128, KC, 128], BF16, "yT") for _ in range(2)]
        hr = [c.sb(es, [128, D], F32, "hr") for _ in range(2)]
        tmp = c.sb(es, [128, D], F32, "tmp")
        junk = c.sb(es, [128, 512], BF16, "junk")
        st = Stats(c, es, 32)
        pT = [c.ps(es, [128, 8, 128], BF16, "pT") for _ in range(2)]
        pD = [c.ps(es, [128, 512], F32, "pD") for _ in range(4)]
        ident = P["ident"]
        dc = 0

        def loads(tj):
            c.sp.dma(hr[tj % 2][:], hsrc[tj][:, :])
            if fm is not None:
                c.sp.dma(yT[tj % 2][:], V(fm.h[:, tj * 128:(tj + 1) * 128].rearrange("(k p) t -> p k t", p=128), fm.buf))
            else:
                c.sp.dma(yt[tj % 2][:], ysrc[tj][:, :])

        loads(0)
        for ti in range(NT):
            y = yt[ti % 2]
            h = hr[ti % 2]
            yt_T = yT[ti % 2]
            if ti + 1 < NT:
                loads(ti + 1)
            for kb in range(KC // 8 if fm is None else 0):
                p = pT[kb % 2]
                for k in range(8):
                    kk = kb * 8 + k
                    c.tr(p[:, k, :], y[:, kk * 128:(kk + 1) * 128], ident[:], last=(k == 7))
                c.copy("act", yt_T[:, kb * 8:(kb + 1) * 8, :], p[:])
            banks = []
            sss = []
            for half in range(2):
                pd = pD[dc % 4]
                dc += 1
                banks.append(pd)
                for k in range(KC):
                    c.mm(pd[:], yt_T[:, k, :], W[:, k, half * 512:(half + 1) * 512], start=(k == 0), stop=(k == KC - 1))
                ssh = st.get()
                c.actf(junk[:], pd[:], AF.Square, accum=ssh)
                sss.append(ssh)
            ss = st.get()
            c.tt("dve", ss, sss[0], sss[1], ALU.add)
            r = rstd_chain(c, st, ss, D)
            for half in range(2):
                c.stt(tmp[:, half * 512:(half + 1) * 512], banks[half][:], r, g3[:, half * 512:(half + 1) * 512],
                      ALU.mult, ALU.mult)
            c.tt("pool", h[:], h[:], tmp[:], ALU.add)
            c.sp.dma(hdst[ti][:, :], h[:])
        c.barrier()


def small_T(c, es, rows_src, nrow, ncol, P, name):
    nch = ncol // 128
    rowt = c.sb(es, [nrow, ncol], F32, name + "r")
    for j, (ap, buf) in enumerate(rows_src):
        c.sp.dma(rowt[j:j + 1, :], V(ap.partition_broadcast(1), buf))
    pt = c.ps(es, [128, nch, nrow], F32, name + "p")
    for ch in range(nch):
        c.tr(pt[:, ch, :], rowt[0:nrow, ch * 128:(ch + 1) * 128], P["identf"][0:nrow, 0:nrow], last=(ch == nch - 1))
    out = c.sb(es, [128, nch, nrow], F32, name)
    c.copy("dve", out[:], pt[:])
    return out


def ssd_stage(c, P, L, hsrc, ydst):
    nc = c.nc
    i = L // 2
    NX = 5152
    with ExitStack() as es0:
        cw = c.sb(es0, [128, 24, 5], F32, "cwk")
        with ExitStack() as es1:
            rows = [(P["odd_conv_w"].h[i, k, :], P["odd_conv_w"].buf) for k in range(4)]
            rows.append((P["odd_conv_b"].h[i, :], P["odd_conv_b"].buf))
            cw_tmp = small_T(c, es1, rows, 5, 3072, P, "cw")
            c.copy("dve", cw[:], cw_tmp[:])
            c.barrier()
        es = es0
        W = c.sb(es, [128, 8, NX], BF16, "Win")
        wsrc = P["odd_w_in"].h[i].rearrange("(k p) f -> p k f", p=128)
        for k in range(8):
            c.pool.dma(W[:, k, :], V(wsrc[:, k, :], P["odd_w_in"].buf), share=(k > 0))
        g2 = load_bcast(c, es, P["norm_w"].h[L, 2, :], P["norm_w"].buf, D, "g2")
        dtb = load_bcast(c, es, P["odd_dt_bias"].h[i, :], P["odd_dt_bias"].buf, 32, "dtb")
        negA = load_bcast(c, es, P["odd_a_log"].h[i, :], P["odd_a_log"].buf, 32, "negA")
        dsk = load_bcast(c, es, P["odd_d_skip"].h[i, :], P["odd_d_skip"].buf, 32, "dsk")
        onw = load_bcast(c, es, P["odd_out_norm_w"].h[i, :], P["odd_out_norm_w"].buf, 2048, "onw")
        c.actf(negA[:], negA[:], AF.Exp)
        c.ts("dve", negA[:], negA[:], -1.0, None, ALU.mult)

        ident, identf, triU, ones, maskb = P["ident"], P["identf"], P["triU"], P["ones"], P["maskb"]
        hl = [c.sb(es, [128, D], F32, "hl") for _ in range(1)]
        xn = c.sb(es, [128, D], BF16, "xn")
        junk = xn
        uT = [c.sb(es, [128, 8, 128], BF16, "uT") for _ in range(1)]
        st = Stats(c, es, 64)
        pcb = c.sb(es, [128, 24, 131], BF16, "pcb")
        dgw = c.sb(es, [128, 24, 4, 128], BF16, "dgw")
        for k in range(4):
            c.tt("pool", dgw[:, :, k, :], V(ident.h[:, :].unsqueeze(1).broadcast_to([128, 24, 128]), ident.buf),
                 V(cw.h[:, :, k:k + 1].broadcast_to([128, 24, 128]), cw.buf), ALU.mult)
        xc = c.sb(es, [128, 24, 128], BF16, "xc")
        xtm = c.sb(es, [128, 32, 64], BF16, "xtm")
        Btm = c.sb(es, [128, 4, 128], BF16, "Btm")
        xdt = c.sb(es, [128, 32, 64], BF16, "xdt")
        xdec = c.sb(es, [128, 32, 64], BF16, "xdec")
        xD = c.sb(es, [128, 32, 64], BF16, "xD")
        sm = c.sb(es, [128, 12, 32], F32, "sm")
        R1 = [c.sb(es, [128, 8, 128], F32, "R1") for _ in range(1)]
        segT = [c.sb(es, [128, 8, 128], BF16, "segT") for _ in range(4)]
        cbs4 = c.sb(es, [128, 4, 128], BF16, "cbs4")
        MT = [c.sb(es, [128, 8, 128], BF16, "MT") for _ in range(4)]
        tb = [c.sb(es, [128, 8, 64], F32, "tb") for _ in range(1)]
        yb = [c.sb(es, [128, 512], F32, "yb") for _ in range(4)]

        yn = [c.sb(es, [128, 512], BF16, "yn") for _ in range(2)]
        S = c.sb(es, [128, 32, 64], F32, "S")
        Sb = c.sb(es, [128, 32, 64], BF16, "Sb")
        for g in range(4):
            c.memset("dve", S.c(g)[:, g * 8:(g + 1) * 8, :], 0.0)
            c.memset("dve", Sb.c(g)[:, g * 8:(g + 1) * 8, :], 0.0)
        for ch in range(24):
            c.memset("pool", pcb.c(ch)[:, ch, :], 0.0)

        pT = c.ps(es, [128, 8, 128], BF16, "pT")
        pA = [c.ps(es, [128, 512], F32, "pA") for _ in range(7)]
        pai = [0]

        def bank():
            b = pA[pai[0] % 7]
            pai[0] += 1
            return b

        def smv(j):
            return sm.c(j)[:, j, :]

        for ti in range(NT):
            ht = hl[0]
            c.sp.dma(ht[:], hsrc[ti][:, :])
            u = uT[0]
            front_norm_T(c, st, ht, g2, xn, junk, pT, u, ident)

            pdt = bank()
            for k in range(8):
                c.mm(pdt[:, 0:32], u[:, k, :], W[:, k, 5120:5152], start=(k == 0), stop=(k == 7))
            dtr, ex, dt, a, nacum, eacum, tot, edarg, edec, etot = [smv(j) for j in range(10)]
            c.tt("dve", dtr, pdt[:, 0:32], dtb[:], ALU.add)
            c.actf(ex, dtr, AF.Exp)
            c.actf(dt, ex, AF.Ln, bias=1.0)
            c.tt("dve", a, dt, negA[:], ALU.mult)
            pac = bank()
            c.mm(pac[:, 0:32], triU[:], a, start=True, stop=True)
            c.mm(pac[:, 32:64], ones[:], a, start=True, stop=True)
            c.actf(nacum, pac[:, 0:32], AF.Copy, scale=-1.0)
            c.actf(eacum, pac[:, 0:32], AF.Exp)
            c.actf(tot, pac[:, 32:64], AF.Copy)
            c.tt("dve", edarg, tot, nacum, ALU.add)
            c.actf(edec, edarg, AF.Exp)
            c.actf(etot, tot, AF.Exp)

            pps = {}

            def cv_in(ch):
                pp = bank()
                pps[ch] = pp
                col = 2048 + ch * 128
                for k in range(8):
                    c.mm(pp[:, 0:128], W[:, k, col:col + 128], u[:, k, :], start=(k == 0), stop=(k == 7))
                c.copy("act" if ch % 2 == 0 else "dve", pcb.c(ch)[:, ch, 3:131], pp[:, 0:128])

            def cv_out(ch):
                pp = pps.pop(ch)
                pc = pcb.c(ch)
                for kk in range(4):
                    c.mm(pp[:, 128:256], dgw[:, ch, kk, :], pc[:, ch, kk:kk + 128], start=(kk == 0), stop=(kk == 3))
                c.copy("pool", pc[:, ch, 0:3], pc[:, ch, 128:131])
                c.actf(xc.c(ch)[:, ch, :], pp[:, 128:256], AF.Silu, bias=cw[:, ch, 4:5])

            LAG = 3
            for ch in range(24 + LAG):
                if ch < 24:
                    cv_in(ch)
                if ch >= LAG:
                    cv_out(ch - LAG)

            for kb in range(2):
                for k in range(8):
                    ch = kb * 8 + k
                    c.tr(pT[:, k, :], xc.c(ch)[:, ch, :], ident[:], last=(k == 7))
                c.copy("act", xtm[:, kb * 16:(kb + 1) * 16, :], pT[:].rearrange("p k (a b) -> p (k a) b", a=2))
            for g in range(4):
                c.tr(pT[:, g, :], xc.c(16 + g)[:, 16 + g, :], ident[:], last=(g == 3))
            c.copy("act", Btm[:], pT[:, 0:4, :])
            bc = lambda v: V(v.ap.unsqueeze(2).broadcast_to([128, 32, 64]), v.buf)
            c.tt("dve", xdt[:], xtm[:], bc(dt), ALU.mult)
            c.tt("pool", xdec[:], xdt[:], bc(edec), ALU.mult)
            c.tt("pool", xD[:], xtm[:], V(dsk.h[:, :].unsqueeze(2).broadcast_to([128, 32, 64]), dsk.buf), ALU.mult)

            zs4 = T(xtm.h, xtm.buf)
            zs4v = lambda g: V(xtm.h[:, g * 8:(g + 1) * 8, :].rearrange("p r d -> p (r d)"), xtm.buf)
            HS = [slice(g * 8, (g + 1) * 8) for g in range(4)]
            bcr = lambda v, hs: V(v.ap[:, hs].unsqueeze(2).broadcast_to([128, 8, 64]), v.buf)
            for g in range(4):
                pz = bank()
                for k in range(8):
                    c.mm(pz[:], u[:, k, :], W[:, k, g * 512:(g + 1) * 512], start=(k == 0), stop=(k == 7))
                c.actf(zs4v(g), pz[:], AF.Silu)
            pcbk = bank()
            for g in range(4):
                c.mm(pcbk[:, g * 128:(g + 1) * 128], xc.c(16 + g)[:, 16 + g, :], xc.c(20 + g)[:, 20 + g, :], start=True,
                     stop=True, last=(g == 3))
            c.copy("dve", cbs4[:], pcbk[:].rearrange("p (g l) -> p g l", g=4))
            if CUT <= 2:
                continue
            for g in range(4):
                r1 = R1[0]
                c.tt("pool", r1[:], V(a.ap[:, HS[g]].unsqueeze(2).broadcast_to([128, 8, 128]), a.buf),
                     V(triU.h[:, :].unsqueeze(1).broadcast_to([128, 8, 128]), triU.buf), ALU.mult)
                for half in range(2):
                    ps_ = bank()
                    c.mm(ps_[:], ones[:], r1[:, half * 4:(half + 1) * 4, :], start=True, stop=False, last=False)
                    c.mm(ps_[:], ident[:], maskb[:], start=False, stop=True)
                    for rr in range(4):
                        r = half * 4 + rr
                        hcol = g * 8 + r
                        c.actf(segT[g][:, r, :], ps_[:, rr * 128:(rr + 1) * 128], AF.Exp,
                               bias=V(nacum.ap[:, hcol:hcol + 1], nacum.buf))
            for g in range(4):
                c.tt("dve", MT[g][:], segT[g][:], V(cbs4.h[:, g:g + 1, :].broadcast_to([128, 8, 128]), cbs4.buf), ALU.mult)
            if CUT <= 3:
                continue
            for g in range(4):
                hs = HS[g]
                pY1 = bank()
                c.mm(pY1[:], ident[:], xD[:, hs, :], start=True, stop=False, last=False)
                for r in range(8):
                    c.mm(pY1[:, r * 64:(r + 1) * 64], MT[g][:, r, :], xdt[:, g * 8 + r, :], start=False, stop=(r == 7),
                         last=(r == 7))
                pY2 = bank()
                c.mm(pY2[:], xc.c(20 + g)[:, 20 + g, :], Sb.c(g)[:, hs, :], start=True, stop=True)
                t_ = tb[0]
                c.tt("dve", t_[:], pY2[:].rearrange("p (r d) -> p r d", r=8), bcr(eacum, hs), ALU.mult)
                c.tt("dve", yb[g][:], t_[:].rearrange("p r d -> p (r d)"), pY1[:], ALU.add)
            if CUT <= 4:
                continue
            pSs = []
            for g in range(4):
                pS = bank()
                pSs.append(pS)
                c.mm(pS[:], Btm[:, g, :], xdec[:, HS[g], :], start=True, stop=True)
                Sg = S.c(g)
                c.tt("pool", Sg[:, HS[g], :], Sg[:, HS[g], :], bcr(etot, HS[g]), ALU.mult)
            for g in range(4):
                Sg = S.c(g)
                c.tt("dve", Sg[:, HS[g], :], Sg[:, HS[g], :], pSs[g][:].rearrange("p (r d) -> p r d", r=8), ALU.add)
                c.copy("act", Sb.c(g)[:, HS[g], :], Sg[:, HS[g], :])
            if CUT <= 5:
                continue
            sss = []
            for g in range(4):
                c.tt("pool", yb[g][:], yb[g][:], zs4v(g), ALU.mult)
                ss = st.get()
                sss.append(ss)
                c.actf(junk[:, 0:512], yb[g][:], AF.Square, accum=ss)
            aa = []
            for g in range(4):
                a_ = st.get()
                c.ts("dve", a_, sss[g], 1.0 / 512, EPS, ALU.mult, ALU.add)
                aa.append(a_)
            bb = []
            for g in range(4):
                b_ = st.get()
                c.actf(b_, aa[g], AF.Sqrt)
                bb.append(b_)
            for g in range(4):
                r_ = st.get()
                c.recip(r_, bb[g])
                ynt = yn[g % 2]
                c.stt(ynt[:], yb[g][:], r_, onw[:, g * 512:(g + 1) * 512], ALU.mult, ALU.mult)
                c.sp.dma(V(ydst[ti].h[:, g * 512:(g + 1) * 512], ydst[ti].buf), ynt[:])
        c.barrier()


def even_stage(c, P, L, hsrc, omT, vatt):
    nc = c.nc
    i = L // 2
    NX = 3592
    ident, identf, triU, ones, maskb = P["ident"], P["identf"], P["triU"], P["ones"], P["maskb"]
    negones, maskus, onesb, maskp, sel = P["negones"], P["maskus"], P["onesb"], P["maskp"], P["sel"]
    SCALE_B = 128.0 ** -0.5
    with ExitStack() as esq:
        qkd = P["qkd"]
        with ExitStack() as es:
            cw = c.sb(es, [128, 12, 4], F32, "cwk")
            with ExitStack() as es1:
                rows = [(P["even_conv_w"].h[i, k, :], P["even_conv_w"].buf) for k in range(4)]
                cw_tmp = small_T(c, es1, rows, 4, 1536, P, "cw")
                c.copy("dve", cw[:], cw_tmp[:])
                c.barrier()
            W = c.sb(es, [128, 8, NX], BF16, "Win")
            wsrc = P["even_w_in"].h[i].rearrange("(k p) f -> p k f", p=128)
            for k in range(8):
                c.pool.dma(W[:, k, :], V(wsrc[:, k, :], P["even_w_in"].buf), share=(k > 0))
            g2 = load_bcast(c, es, P["norm_w"].h[L, 2, :], P["norm_w"].buf, D, "g2")
            dtb = load_bcast(c, es, P["even_dt_bias"].h[i, :], P["even_dt_bias"].buf, 4, "dtb")
            negA = load_bcast(c, es, P["even_a_log"].h[i, :], P["even_a_log"].buf, 4, "negA")
            hnw = load_bcast(c, es, P["even_head_norm_w"].h[i, :], P["even_head_norm_w"].buf, 128, "hnw")
            c.actf(negA[:], negA[:], AF.Exp)
            c.ts("dve", negA[:], negA[:], -1.0, None, ALU.mult)

            hl = [c.sb(es, [128, D], F32, "hl") for _ in range(2)]
            xn = c.sb(es, [128, D], BF16, "xn")
            junk = c.sb(es, [128, D], BF16, "junk")
            uT = [c.sb(es, [128, 8, 128], BF16, "uT") for _ in range(2)]
            st = Stats(c, es, 64)
            vt = [c.sb(es, [128, 512], BF16, "vt") for _ in range(2)]
            qkt = [c.sb(es, [128, 8, 128], BF16, "qkt") for _ in range(2)]
            zsb = [c.sb(es, [128, 512], F32, "zs") for _ in range(2)]
            sm = c.sb(es, [128, 48, 4], F32, "sm")
            smi = [0]

            def smv():
                j = smi[0] % 32
                smi[0] += 1
                return sm.c(j)[:, j, :]

            pcb = c.sb(es, [128, 12, 131], BF16, "pcb")
            dgw = c.sb(es, [128, 12, 4, 128], BF16, "dgw")
            for k in range(4):
                c.tt("pool", dgw[:, :, k, :], V(ident.h[:, :].unsqueeze(1).broadcast_to([128, 12, 128]), ident.buf),
                     V(cw.h[:, :, k:k + 1].broadcast_to([128, 12, 128]), cw.buf), ALU.mult)
            xg = c.sb(es, [128, 12, 128], F32, "xg")
            xtm = c.sb(es, [128, 12, 128], F32, "xtm")
            dg = [c.sb(es, [128, 4, 128], F32, "dg") for _ in range(2)]
            knT = c.sb(es, [128, 4, 128], F32, "knT")
            kbT = c.sb(es, [128, 4, 128], F32, "kbT")
            qnT = c.sb(es, [128, 4, 128], F32, "qnT")
            qdT = c.sb(es, [128, 4, 128], F32, "qdT")
            kbg = c.sb(es, [128, 4, 128], F32, "kbg")
            kdec = c.sb(es, [128, 4, 128], F32, "kdec")
            vb = c.sb(es, [128, 4, 128], F32, "vb")
            R1 = c.sb(es, [128, 4, 128], F32, "R1")
            segT = c.sb(es, [128, 4, 128], F32, "segT")
            segU = c.sb(es, [128, 4, 128], F32, "segU")
            qkT = c.sb(es, [128, 4, 128], F32, "qkT")
            Am = [c.sb(es, [128, 4, 128], F32, "Am") for _ in range(2)]
            Bm = [c.sb(es, [128, 4, 128], F32, "Bm") for _ in range(2)]
            Xm = [c.sb(es, [128, 4, 128], F32, "Xm") for _ in range(2)]
            nwT = c.sb(es, [128, 4, 128], F32, "nwT")
            vn = c.sb(es, [128, 4, 128], F32, "vn")
            S = c.sb(es, [128, 4, 128], F32, "S")
            on = c.sb(es, [128, 4, 128], F32, "on")
            ob = c.sb(es, [128, 4, 128], BF16, "ob")
            obT = [c.sb(es, [128, 4, 128], BF16, "obT") for _ in range(2)]
            c.memset("dve", S[:], 0.0)
            for ch in range(12):
                c.memset("pool", pcb.c(ch)[:, ch, :], 0.0)

            pT = c.ps(es, [128, 8, 128], BF16, "pT")
            pA = [c.ps(es, [128, 512], F32, "pA") for _ in range(7)]
            pai = [0]

            def bank():
                b = pA[pai[0] % 7]
                pai[0] += 1
                return b

            def trf(dst3, src_fn, n):
                pb_ = bank()
                for j in range(n):
                    c.tr(pb_[:, j * 128:(j + 1) * 128], src_fn(j), identf[:], last=(j == n - 1))
                return pb_

            def b4(v):
                return v.rearrange("p (h d) -> p h d", h=4)

            def bcl(v, n=128):
                return V(v.ap.unsqueeze(2).broadcast_to([128, 4, n]), v.buf)

            def bcm(t):
                return V(t.h[:, :].unsqueeze(1).broadcast_to([128, 4, 128]), t.buf)

            NTE = min(NT, NT_LIM)

            def front(tj):
                ht = hl[tj % 2]
                c.sp.dma(ht[:], hsrc[tj][:, :])
                front_norm_T(c, st, ht, g2, xn, junk, pT, uT[tj % 2], ident)

            def attn_gen(tj):
                u_ = uT[tj % 2]
                tsl_ = slice(tj * 128, (tj + 1) * 128)
                for cch in range(8):
                    pp = bank()
                    for k in range(8):
                        c.mm(pp[:, 0:128], W[:, k, cch * 128:(cch + 1) * 128], u_[:, k, :], start=(k == 0), stop=(k == 7))
                    c.copy("act" if cch % 2 == 0 else "dve", qkt[tj % 2][:, cch, :], pp[:, 0:128])
                    if cch == 7:
                        c.sp.dma(V(qkd.h[:, :, tsl_], qkd.buf), qkt[tj % 2][:])
                    yield
                pv = bank()
                for k in range(8):
                    c.mm(pv[:], u_[:, k, :], W[:, k, 1024:1536], start=(k == 0), stop=(k == 7))
                v_ = vt[tj % 2]
                c.copy("dve", v_[:], pv[:])
                c.sp.dma(vatt[tj][:, :], v_[:])
                yield
                pz = bank()
                for k in range(8):
                    c.mm(pz[:], u_[:, k, :], W[:, k, 3072:3584], start=(k == 0), stop=(k == 7))
                c.actf(zsb[tj % 2][:], pz[:], AF.Silu)
                yield

            def step(gen, n=1):
                if gen is None:
                    return
                for _ in range(n):
                    try:
                        next(gen)
                    except StopIteration:
                        return

            front(0)
            step(attn_gen(0), 100)
            for ti in range(NTE):
                tsl = slice(ti * 128, (ti + 1) * 128)
                u = uT[ti % 2]
                zs = zsb[ti % 2]
                ag = None
                if CUT <= 2:
                    continue
                pba = bank()
                for k in range(8):
                    c.mm(pba[:, 0:8], u[:, k, :], W[:, k, 3584:3592], start=(k == 0), stop=(k == 7))
                beta, spi, ex, spv, g, nacum, acum, egc, tot, edarg, edec, etot = [smv() for _ in range(12)]
                c.actf(beta, pba[:, 0:4], AF.Sigmoid)
                c.tt("dve", spi, pba[:, 4:8], dtb[:], ALU.add)
                c.actf(ex, spi, AF.Exp)
                c.actf(spv, ex, AF.Ln, bias=1.0)
                c.tt("dve", g, spv, negA[:], ALU.mult)
                pac = bank()
                c.mm(pac[:, 0:4], triU[:], g, start=True, stop=True)
                c.mm(pac[:, 4:8], ones[:], g, start=True, stop=True)
                c.actf(nacum, pac[:, 0:4], AF.Copy, scale=-1.0)
                c.actf(acum, pac[:, 0:4], AF.Copy)
                c.actf(egc, pac[:, 0:4], AF.Exp)
                c.actf(tot, pac[:, 4:8], AF.Copy)
                c.tt("dve", edarg, tot, nacum, ALU.add)
                c.actf(edec, edarg, AF.Exp)
                c.actf(etot, tot, AF.Exp)

                if CUT <= 3:
                    continue
                pps = {}

                def cv_in(ch):
                    pp = bank()
                    pps[ch] = pp
                    col = 1536 + ch * 128
                    for k in range(8):
                        c.mm(pp[:, 0:128], W[:, k, col:col + 128], u[:, k, :], start=(k == 0), stop=(k == 7))
                    c.copy("act" if ch % 2 == 0 else "dve", pcb.c(ch)[:, ch, 3:131], pp[:, 0:128])

                def cv_out(ch):
                    pp = pps.pop(ch)
                    pc = pcb.c(ch)
                    for kk in range(4):
                        c.mm(pp[:, 128:256], dgw[:, ch, kk, :], pc[:, ch, kk:kk + 128], start=(kk == 0), stop=(kk == 3))
                    c.copy("pool", pc[:, ch, 0:3], pc[:, ch, 128:131])
                    c.actf(xg.c(ch)[:, ch, :], pp[:, 128:256], AF.Silu)

                LAG = 3
                for ch in range(12 + LAG):
                    if ch < 12:
                        cv_in(ch)
                    if ch >= LAG:
                        cv_out(ch - LAG)
                if ti + 1 < NTE:
                    front(ti + 1)
                    ag = attn_gen(ti + 1)
                if CUT <= 4:
                    step(ag, 100)
                    continue
                for q3 in range(3):
                    pq = trf(None, lambda j: xg.c(q3 * 4 + j)[:, q3 * 4 + j, :], 4)
                    c.copy("act" if q3 != 1 else "dve", xtm[:, q3 * 4:(q3 + 1) * 4, :], b4(pq[:]))
                if CUT <= 5:
                    continue
                ssq = sm.c("ssq")
                ssqk = [V(sm.h[:, 40 + (j // 4), (j % 4):(j % 4) + 1], ssq.buf) for j in range(8)]
                for j in range(8):
                    c.actf(junk[:, 0:128], xtm[:, j, :], AF.Square, accum=ssqk[j])
                ssv = V(sm.h[:, 40:42, :], ssq.buf)
                rn0 = V(sm.h[:, 42:44, :], sm.c("rn0").buf)
                rn1 = V(sm.h[:, 44:46, :], sm.c("rn1").buf)
                rn = V(sm.h[:, 46:48, :], sm.c("rn").buf)
                c.ts("dve", rn0, ssv, EPS, None, ALU.add)
                c.actf(rn1, rn0, AF.Sqrt)
                c.recip(rn, rn1)
                rq = V(sm.h[:, 46, :], rn.buf)
                rk = V(sm.h[:, 47, :], rn.buf)
                s_kb, s_qn, s_qd, s_kbg, s_kdec = [smv() for _ in range(5)]
                c.tt("dve", s_kb, rk, beta, ALU.mult)
                c.ts("dve", s_qn, rq, SCALE_B, None, ALU.mult)
                c.tt("dve", s_qd, s_qn, egc, ALU.mult)
                c.tt("dve", s_kbg, s_kb, egc, ALU.mult)
                c.tt("dve", s_kdec, rk, edec, ALU.mult)
                if CUT <= 6:
                    continue
                for qi, (sc, src0, dstT) in enumerate(((rk, 4, knT), (s_kb, 4, kbT), (s_qn, 0, qnT), (s_qd, 0, qdT))):
                    d_ = dg[qi % 2]
                    c.tt("pool", d_[:], bcm(identf), bcl(sc), ALU.mult)
                    pb_ = bank()
                    c.mm(pb_[:], ones[:], d_[:], start=True, stop=True)
                    srcv = V(xg.h[:, src0:src0 + 4, :], xg.c(src0).buf)
                    E = c.dve
                    E.issue(lambda: nc.vector.tensor_tensor(out=dstT.h[:], in0=srcv.ap, in1=b4(pb_[:]).ap, op=ALU.mult),
                            [xg.c(src0 + j).buf for j in range(4)] + [pb_.buf], [dstT.buf])
                if CUT <= 7:
                    continue
                c.tt("pool", kbg[:], xtm[:, 4:8, :], bcl(s_kbg), ALU.mult)
                c.tt("pool", kdec[:], xtm[:, 4:8, :], bcl(s_kdec), ALU.mult)
                c.tt("pool", vb[:], xtm[:, 8:12, :], bcl(beta), ALU.mult)
                if CUT <= 8:
                    continue
                c.tt("pool", R1[:], bcl(g), bcm(triU), ALU.mult)
                pL = bank()
                c.mm(pL[:], ones[:], R1[:], start=True, stop=False, last=False)
                c.mm(pL[:], ident[:], maskb[:], start=False, stop=True)
                pU = bank()
                c.mm(pU[:], negones[:], R1[:], start=True, stop=False, last=False)
                c.mm(pU[:], ident[:], maskus[:], start=False, stop=True)
                for h in range(4):
                    c.actf(segT[:, h, :], pL[:, h * 128:(h + 1) * 128], AF.Exp, bias=V(nacum.ap[:, h:h + 1], nacum.buf))
                    c.actf(segU[:, h, :], pU[:, h * 128:(h + 1) * 128], AF.Exp, bias=V(acum.ap[:, h:h + 1], acum.buf))
                if CUT <= 9:
                    continue
                pG = bank()
                for h in range(4):
                    c.mm(pG[:, h * 128:(h + 1) * 128], kbT[:, h, :], knT[:, h, :], start=True, stop=True, last=(h == 3))
                pQK = bank()
                for h in range(4):
                    c.mm(pQK[:, h * 128:(h + 1) * 128], knT[:, h, :], qnT[:, h, :], start=True, stop=True, last=(h == 3))
                A_, B_, X_ = Am[0], Bm[0], Xm[0]
                c.stt(A_[:], b4(pG[:]), negones[:, 0:1], segU[:], ALU.mult, ALU.mult)
                c.tt("dve", qkT[:], b4(pQK[:]), segT[:], ALU.mult)
                pq = trf(None, lambda j: A_[:, j, :], 4)
                c.copy("act", B_[:], b4(pq[:]))
                c.tt("dve", X_[:], B_[:], bcm(identf), ALU.add)
                if CUT <= 10:
                    continue
                for lvl in range(1, 7):
                    A2, B2, X2 = Am[lvl % 2], Bm[lvl % 2], Xm[lvl % 2]
                    pAq = bank()
                    for h in range(4):
                        c.mm(pAq[:, h * 128:(h + 1) * 128], B_[:, h, :], A_[:, h, :], start=True, stop=True, last=(h == 3))
                    c.copy("act", A2[:], b4(pAq[:]))
                    if lvl < 6:
                        pBq = bank()
                        for h in range(4):
                            c.mm(pBq[:, h * 128:(h + 1) * 128], A_[:, h, :], B_[:, h, :], start=True, stop=True,
                                 last=(h == 3))
                        c.copy("dve", B2[:], b4(pBq[:]))
                    pX = bank()
                    for h in range(4):
                        c.mm(pX[:, h * 128:(h + 1) * 128], identf[:], X_[:, h, :], start=True, stop=False, last=False)
                        c.mm(pX[:, h * 128:(h + 1) * 128], A2[:, h, :], X_[:, h, :], start=False, stop=True, last=(h == 3))
                    c.copy("dve" if lvl % 2 else "act", X2[:], b4(pX[:]))
                    A_, B_, X_ = A2, B2, X2
                    step(ag, 2)
                PT_ = X_
                step(ag, 100)
                if CUT <= 11:
                    continue
                pW = bank()
                for h in range(4):
                    c.mm(pW[:, h * 128:(h + 1) * 128], kbg[:, h, :], PT_[:, h, :], start=True, stop=True, last=(h == 3))
                c.actf(nwT[:], b4(pW[:]), AF.Copy, scale=-1.0)
                pV = bank()
                for h in range(4):
                    c.mm(pV[:, h * 128:(h + 1) * 128], PT_[:, h, :], vb[:, h, :], start=True, stop=False, last=False)
                    c.mm(pV[:, h * 128:(h + 1) * 128], nwT[:, h, :], S[:, h, :], start=False, stop=True, last=(h == 3))
                c.copy("dve", vn[:], b4(pV[:]))
                pO = bank()
                for h in range(4):
                    c.mm(pO[:, h * 128:(h + 1) * 128], qdT[:, h, :], S[:, h, :], start=True, stop=False, last=False)
                    c.mm(pO[:, h * 128:(h + 1) * 128], qkT[:, h, :], vn[:, h, :], start=False, stop=True, last=(h == 3))
                pS = bank()
                for h in range(4):
                    c.mm(pS[:, h * 128:(h + 1) * 128], kdec[:, h, :], vn[:, h, :], start=True, stop=True, last=(h == 3))
                for h in range(4):
                    c.stt(S[:, h, :], S[:, h, :], V(etot.ap[:, h:h + 1], etot.buf), pS[:, h * 128:(h + 1) * 128],
                          ALU.mult, ALU.add)
                if CUT <= 12:
                    continue
                sso = smv()
                for h in range(4):
                    c.actf(junk[:, 0:128], pO[:, h * 128:(h + 1) * 128], AF.Square, accum=V(sso.ap[:, h:h + 1], sso.buf))
                r0, r1_, r2 = smv(), smv(), smv()
                c.ts("dve", r0, sso, 1.0 / 128, EPS, ALU.mult, ALU.add)
                c.actf(r1_, r0, AF.Sqrt)
                c.recip(r2, r1_)
                for h in range(4):
                    c.stt(on[:, h, :], pO[:, h * 128:(h + 1) * 128], V(r2.ap[:, h:h + 1], r2.buf), hnw[:], ALU.mult, ALU.mult)
                c.tt("pool", ob[:], on[:], zs[:].rearrange("p (h d) -> p h d", h=4), ALU.mult)
                for h in range(4):
                    c.tr(pT[:, h, :], ob[:, h, :], ident[:], last=(h == 3))
                o_ = obT[ti % 2]
                c.copy("act", o_[:], pT[:, 0:4, :])
                c.sp.dma(V(omT.h[512:1024, tsl].rearrange("(c p) t -> p c t", p=128), omT.c(ti).buf), o_[:])
            c.barrier()

        if "b" not in EVEN_PARTS:
            return
        with ExitStack() as es:
            QT = c.sb(es, [128, 4, SEQ], BF16, "QT")
            KT = c.sb(es, [128, 4, SEQ], BF16, "KT")
            for cq_ in range(4):
                c.sp.dma(QT[:, cq_, :], V(qkd.h[:, cq_, :], qkd.buf))
                c.sp.dma(KT[:, cq_, :], V(qkd.h[:, 4 + cq_, :], qkd.buf))
            accs = c.sb(es, [65, 4, 2048], F32, "accs")
            Vp = [c.sb(es, [128, 32, 4, 65], BF16, "Vp") for _ in range(2)]
            eb = [c.sb(es, [128, 2, 128], BF16, "eb") for _ in range(3)]
            rden = c.sb(es, [64, 512], F32, "rden")
            oT = [c.sb(es, [64, 2048], BF16, "oT") for _ in range(2)]
            pA = [c.ps(es, [128, 512], F32, "pA") for _ in range(8)]
            pai = [0]

            def bank():
                b = pA[pai[0] % 8]
                pai[0] += 1
                return b

            for v_ in Vp:
                c.memset("pool", v_[:, :, :, 64:65], 1.0)
            cnt = 0
            ecnt = 0
            ocnt = 0
            vall = T(vatt[0].h, Buf())
            for hg in range(2):
                for H in range(2):
                    c.memset("pool", accs[:], 0.0)
                    for d in (1, 4, 16):
                        nb = 16 // d
                        b0 = nb * H
                        vp = Vp[cnt % 2]
                        cnt += 1
                        for r in range(d):
                            for lb in range(nb + 1):
                                b = b0 - 1 + lb
                                if b < 0:
                                    continue
                                t0 = r + d * 128 * b
                                src = P["vatt_full"].h[t0:t0 + d * 127 + 1:d, hg * 256:(hg + 1) * 256]
                                q = c.sp
                                q.dma(vp[:, r * (nb + 1) + lb, :, 0:64],
                                      V(src.rearrange("p (h e) -> p h e", h=4), P["vatt_full"].buf))
                        for hh in range(4):
                            h = hg * 4 + hh
                            cq = h // 2
                            pb = 64 * (h % 2)
                            units = [(r, b) for r in range(d) for b in range(b0, b0 + nb)]
                            for u4 in range(4):
                                pnum = bank()
                                for ui in range(4):
                                    r, b = units[u4 * 4 + ui]
                                    q0 = r + d * 128 * b
                                    q_ap = QT.h[pb:pb + 64, cq, q0:q0 + d * 127 + 1:d]
                                    kbs = [b - 1, b] if b >= 1 else [b]
                                    ps_ = bank()
                                    for j, kb_ in enumerate(kbs):
                                        k0 = r + d * 128 * kb_
                                        k_ap = KT.h[pb:pb + 64, cq, k0:k0 + d * 127 + 1:d]
                                        c.pe.issue(lambda: nc.tensor.matmul(out=ps_.h[:, j * 128:(j + 1) * 128], lhsT=k_ap,
                                                                            rhs=q_ap, start=True, stop=False),
                                                   [QT.buf, KT.buf], [ps_.buf], inc=False)
                                        mk = maskp if kb_ == b - 1 else maskb
                                        c.mm(ps_[:, j * 128:(j + 1) * 128], ident[:], mk[:, 0:128], start=False, stop=True)
                                    e_ = eb[ecnt % 3]
                                    ecnt += 1
                                    nk = len(kbs)
                                    c.actf(e_[:, 0:nk, :], ps_[:, 0:nk * 128].rearrange("p (j q) -> p j q", j=nk), AF.Exp,
                                           scale=0.125)
                                    for j, kb_ in enumerate(kbs):
                                        slot = r * (nb + 1) + (kb_ - (b0 - 1))
                                        c.mm(pnum[0:65, ui * 128:(ui + 1) * 128], vp[:, slot, hh, :], e_[:, j, :],
                                             start=(j == 0), stop=(j == nk - 1))
                                if d == 1:
                                    av = accs[0:65, hh, u4 * 512:(u4 + 1) * 512]
                                    pvw = pnum[0:65, :]
                                elif d == 4:
                                    av = accs[0:65, hh, u4:2048:4]
                                    pvw = pnum[0:65, :]
                                else:
                                    av = V(accs.h[0:65, hh, :].rearrange("p (i r) -> p r i", r=16)[:, u4 * 4:(u4 + 1) * 4, :],
                                           accs.buf)
                                    pvw = pnum[0:65, :].rearrange("p (u q) -> p u q", u=4)
                                c.tt("dve", av, av, pvw, ALU.add)
                    for hh in range(4):
                        h = hg * 4 + hh
                        o_ = oT[ocnt % 2]
                        ocnt += 1
                        for q4 in range(4):
                            pden = bank()
                            c.mm(pden[0:64, :], sel[0:65, 0:64], accs[0:65, hh, q4 * 512:(q4 + 1) * 512], start=True, stop=True)
                            c.recip(rden[:], pden[0:64, :])
                            c.tt("pool", o_[:, q4 * 512:(q4 + 1) * 512], accs[0:64, hh, q4 * 512:(q4 + 1) * 512], rden[:],
                                 ALU.mult)
                        c.sp.dma(V(omT.h[h * 64:(h + 1) * 64, H * 2048:(H + 1) * 2048], omT.c("a%d_%d" % (h, H)).buf), o_[:])
            c.barrier()

INPUT_NAMES = ["norm_w", "ffn_w_gate", "ffn_w_up", "ffn_w_down", "even_w_in", "even_conv_w", "even_a_log",
               "even_dt_bias", "even_head_norm_w", "even_w_out", "odd_w_in", "odd_conv_w", "odd_conv_b",
               "odd_dt_bias", "odd_a_log", "odd_d_skip", "odd_out_norm_w", "odd_w_out"]


def consts_np():
    k = np.arange(128)
    triU = (k[:, None] <= k[None, :]).astype(np.float32)
    maskb = np.where(k[None, :] >= k[:, None], 0.0, -30000.0).astype(np.float32)
    return {"ident": np.eye(128, dtype=np.float32).astype(ml_dtypes.bfloat16),
            "identf": np.eye(128, dtype=np.float32),
            "triU": triU, "ones": np.ones((128, 128), np.float32),
            "maskb": np.tile(maskb, (1, 4)).astype(ml_dtypes.bfloat16),
            "negones": -np.ones((128, 128), np.float32),
            "maskus": np.tile(np.where(k[None, :] < k[:, None], 0.0, -30000.0), (1, 4)).astype(ml_dtypes.bfloat16),
            "onesb": np.ones((128, 128), np.float32).astype(ml_dtypes.bfloat16),
            "maskp": np.where(k[None, :] <= k[:, None], 0.0, -30000.0).astype(ml_dtypes.bfloat16),
            "sel": np.concatenate([np.zeros((64, 64), np.float32), np.ones((64, 64), np.float32)], 0)}


CONST_SPECS = [("ident", [128, 128], BF16), ("identf", [128, 128], F32), ("triU", [128, 128], F32),
               ("ones", [128, 128], F32), ("maskb", [128, 512], BF16), ("negones", [128, 128], F32),
               ("maskus", [128, 512], BF16), ("onesb", [128, 128], BF16), ("maskp", [128, 128], BF16),
               ("sel", [128, 64], F32)]


def build(shapes, stages=None, debug=False):
    if stages is None:
        stages = list(range(12))
    nc = bass.Bass("TRN2", target_bir_lowering=False)
    P = {}
    x = nc.dram_tensor("x", [SEQ, D], F32, kind="ExternalInput").ap()
    for n in INPUT_NAMES:
        P[n] = T(nc.dram_tensor(n, list(shapes[n]), F32, kind="ExternalInput").ap())
    out = nc.dram_tensor("out", [SEQ, D], F32, kind="ExternalOutput").ap()
    ymix = nc.dram_tensor("ymix", [SEQ, 2048], BF16, kind="Internal").ap()
    omix = nc.dram_tensor("omix", [D, SEQ], BF16, kind="ExternalOutput" if debug else "Internal").ap()
    vattd = nc.dram_tensor("vattd", [SEQ, 512], BF16, kind="Internal").ap()
    qkdd = nc.dram_tensor("qkdd", [128, 8, SEQ], BF16, kind="Internal").ap()
    with ExitStack() as es:
        c = Ctx(nc, es)
        for (n, shp, dt) in CONST_SPECS:
            d = nc.dram_tensor(n, shp, dt, kind="ExternalInput").ap()
            t = c.sb(es, shp, dt, n + "_sb")
            c.sp.dma(t[:], V(d[:, :], Buf()))
            P[n] = t
        xt = [T(x[i * 128:(i + 1) * 128, :]) for i in range(NT)]
        ht = [T(out[i * 128:(i + 1) * 128, :]) for i in range(NT)]
        yt2048 = [T(ymix[i * 128:(i + 1) * 128, :]) for i in range(NT)]
        omT = T(omix)
        P["vatt_full"] = T(vattd)
        P["qkd"] = T(qkdd)
        vatt = [T(vattd[i * 128:(i + 1) * 128, :], P["vatt_full"].buf) for i in range(NT)]
        src = xt
        for sid in stages:
            L, kind = sid // 3, sid % 3
            if kind == 0:
                ffn_stage(c, P, L, 0, src, ht)
            elif kind == 2:
                ffn_stage(c, P, L, 1, src, ht)
            else:
                if L % 2 == 1:
                    ssd_stage(c, P, L, src, yt2048)
                    outproj_stage(c, P, L, yt2048, "odd_w_out", L // 2, 2048, src, ht)
                else:
                    even_stage(c, P, L, src, omT, vatt)
                    if "o" in os.environ.get("EVEN_PARTS", "abo"):
                        outproj_stage(c, P, L, None, "even_w_out", L // 2, 1024, src, ht, fm=omT)
            src = ht
        c.barrier()
    return nc


def kernel(**inputs):
    x = np.ascontiguousarray(inputs["x"], dtype=np.float32)
    nb = x.shape[0]
    shapes = {n: inputs[n].shape for n in INPUT_NAMES}
    nc = build(shapes)
    base = {n: np.ascontiguousarray(inputs[n], dtype=np.float32) for n in INPUT_NAMES}
    base.update(consts_np())
    in_maps = []
    for b in range(nb):
        m = dict(base)
        m["x"] = x[b]
        in_maps.append(m)
    res = run_bass_kernel_spmd(nc, in_maps, core_ids=list(range(nb)))
    return np.stack([np.asarray(r["out"]) for r in res.results], axis=0).astype(np.float32)
```
